# Optimizing a Trainium2 kernel written in Bass

```python
import jax, jax.numpy as jnp
from jax import lax
import numpy as np

D_MODEL = 2048
BATCH = 8
SEQ = 2048
DEPTH = 1

HEAD_DIM = 64
N_Q_HEADS = 32
N_KV_HEADS = 4
GROUP = N_Q_HEADS // N_KV_HEADS
WINDOW = 128
ATT_BLOCK = 128
ATT_Q_W = N_Q_HEADS * HEAD_DIM
ATT_KV_W = N_KV_HEADS * HEAD_DIM
ATT_W = ATT_Q_W

DN_K_HEADS = 16
DN_V_HEADS = 32
DN_K_DIM = 128
DN_V_DIM = 128
V_PER_K = DN_V_HEADS // DN_K_HEADS
DN_QK_W = DN_K_HEADS * DN_K_DIM
DN_V_W = DN_V_HEADS * DN_V_DIM
DN_CONV_CH = 2 * DN_QK_W + DN_V_W
CONV_WIDTH = 4
CHUNK = 64

N_BRANCH = 2
EPS = 1e-6

OFF_AQ = 0
OFF_AK = OFF_AQ + ATT_Q_W
OFF_AV = OFF_AK + ATT_KV_W
OFF_AZ = OFF_AV + ATT_KV_W
OFF_DQKV = OFF_AZ + ATT_W
OFF_DZ = OFF_DQKV + DN_CONV_CH
OFF_DB = OFF_DZ + DN_V_W
OFF_DA = OFF_DB + DN_V_HEADS
OFF_G = OFF_DA + DN_V_HEADS
IN_W = OFF_G + N_BRANCH * D_MODEL
ATT_QKV_W = ATT_Q_W + 2 * ATT_KV_W

kernel_name = "hybrid_swa_sink_gated_deltanet_parallel"


def rms_norm(x, w):
    xf = x.astype(jnp.float32)
    y = xf * lax.rsqrt(jnp.mean(xf * xf, axis=-1, keepdims=True) + EPS)
    return (y * w.astype(jnp.float32)).astype(x.dtype)


def l2_norm(x):
    xf = x.astype(jnp.float32)
    return xf * lax.rsqrt(jnp.sum(xf * xf, axis=-1, keepdims=True) + EPS)


def sliding_window_attention(q, k, v, sinks):
    B, S = q.shape[0], q.shape[1]
    L = ATT_BLOCK
    nb = S // L
    scale = HEAD_DIM ** -0.5
    qb = q.reshape(B, nb, L, N_KV_HEADS, GROUP, HEAD_DIM).transpose(1, 0, 2, 3, 4, 5)
    pad = jnp.zeros((B, L, N_KV_HEADS, HEAD_DIM), k.dtype)
    kp = jnp.concatenate([pad, k], axis=1).reshape(B, nb + 1, L, N_KV_HEADS, HEAD_DIM)
    vp = jnp.concatenate([pad, v], axis=1).reshape(B, nb + 1, L, N_KV_HEADS, HEAD_DIM)
    kw = jnp.concatenate([kp[:, :-1], kp[:, 1:]], axis=2).transpose(1, 0, 2, 3, 4)
    vw = jnp.concatenate([vp[:, :-1], vp[:, 1:]], axis=2).transpose(1, 0, 2, 3, 4)
    qi = jnp.arange(L)[:, None]
    ki = jnp.arange(2 * L)[None, :]
    rel = qi + L - ki
    band = (rel >= 0) & (rel < WINDOW)
    key_valid = (jnp.arange(nb)[:, None] * L - L + jnp.arange(2 * L)[None, :]) >= 0
    sink = sinks.astype(jnp.float32).reshape(N_KV_HEADS, GROUP)[None, :, :, None, None]

    def one_block(args):
        q_blk, k_blk, v_blk, kv_ok = args
        s = jnp.einsum('bqhgd,bkhd->bhgqk', q_blk, k_blk).astype(jnp.float32) * scale
        s = jnp.where((band & kv_ok[None, :])[None, None, None], s, -jnp.inf)
        m = jnp.maximum(jnp.max(s, axis=-1, keepdims=True), sink)
        p = jnp.exp(s - m)
        p = p / (jnp.sum(p, axis=-1, keepdims=True) + jnp.exp(sink - m))
        return jnp.einsum('bhgqk,bkhd->bqhgd', p.astype(v_blk.dtype), v_blk)

    o = lax.map(one_block, (qb, kw, vw, key_valid))
    return o.transpose(1, 0, 2, 3, 4, 5).reshape(B, S, N_Q_HEADS * HEAD_DIM)


def causal_depthwise_conv(u, w):
    S = u.shape[1]
    up = jnp.pad(u, ((0, 0), (CONV_WIDTH - 1, 0), (0, 0)))
    y = up[:, 0:S] * w[0]
    for i in range(1, CONV_WIDTH):
        y = y + up[:, i:i + S] * w[i]
    return y


def gated_delta_rule(q, k, v, g, beta):
    B, S, H, DK = q.shape
    DV = v.shape[-1]
    N = S // CHUNK
    f32 = jnp.float32

    def chunks(t):
        return t.astype(f32).reshape(B, N, CHUNK, H, -1).transpose(0, 3, 1, 2, 4)

    q = chunks(q) * (DK ** -0.5)
    k = chunks(k)
    v = chunks(v)
    beta = beta.astype(f32).reshape(B, N, CHUNK, H).transpose(0, 3, 1, 2)
    g = jnp.cumsum(g.astype(f32).reshape(B, N, CHUNK, H).transpose(0, 3, 1, 2), axis=-1)
    k_beta = k * beta[..., None]
    v_beta = v * beta[..., None]
    tril = jnp.tril(jnp.ones((CHUNK, CHUNK), bool))
    strict = jnp.tril(jnp.ones((CHUNK, CHUNK), bool), -1)
    decay = jnp.exp(jnp.where(tril, g[..., :, None] - g[..., None, :], -jnp.inf))
    a_low = jnp.where(strict, jnp.einsum('bhncd,bhnkd->bhnck', k_beta, k) * decay, 0.0)
    lhs = a_low + jnp.eye(CHUNK, dtype=f32)
    rhs = jnp.concatenate([v_beta, k_beta * jnp.exp(g)[..., None]], axis=-1)
    sol = lax.linalg.triangular_solve(lhs, rhs, left_side=True, lower=True, unit_diagonal=True)
    u = sol[..., :DV]
    w = sol[..., DV:]
    qk = jnp.einsum('bhncd,bhnkd->bhnck', q, k) * decay

    def to_scan(t):
        return jnp.moveaxis(t, 2, 0)

    def step(state, inp):
        q_i, k_i, u_i, w_i, g_i, qk_i = inp
        v_new = u_i - jnp.einsum('bhcd,bhde->bhce', w_i, state)
        o = (jnp.einsum('bhcd,bhde->bhce', q_i * jnp.exp(g_i)[..., None], state)
             + jnp.einsum('bhck,bhke->bhce', qk_i, v_new))
        g_last = g_i[..., -1]
        k_dec = k_i * jnp.exp(g_last[..., None] - g_i)[..., None]
        state = state * jnp.exp(g_last)[..., None, None] + jnp.einsum('bhcd,bhce->bhde', k_dec, v_new)
        return state, o

    state0 = jnp.zeros((B, H, DK, DV), f32)
    _, o = lax.scan(step, state0, (to_scan(q), to_scan(k), to_scan(u), to_scan(w), to_scan(g), to_scan(qk)))
    return o.transpose(1, 0, 3, 2, 4).reshape(B, S, H, DV)


def setup_inputs(seed: int = 0) -> dict:
    key = jax.random.key(seed)
    ks = jax.random.split(key, 16)
    f32 = jnp.float32
    nrm = lambda k, shape, s: jax.random.normal(k, shape, f32) * s
    return {
        "x": jax.random.normal(ks[0], (BATCH, SEQ, D_MODEL), f32),
        "norm_w": 1.0 + nrm(ks[1], (DEPTH, D_MODEL), 0.02),
        "w_in": nrm(ks[2], (DEPTH, D_MODEL, IN_W), D_MODEL ** -0.5),
        "b_qkv": nrm(ks[3], (DEPTH, ATT_QKV_W), 0.02),
        "sinks": nrm(ks[4], (DEPTH, N_Q_HEADS), 0.5),
        "conv_w": nrm(ks[5], (DEPTH, CONV_WIDTH, DN_CONV_CH), CONV_WIDTH ** -0.5),
        "a_log": jnp.log(jax.random.uniform(ks[6], (DEPTH, DN_V_HEADS), f32, 1.0, 16.0)),
        "dt_bias": 1.0 + nrm(ks[7], (DEPTH, DN_V_HEADS), 0.1),
        "dn_norm_w": 1.0 + nrm(ks[8], (DEPTH, DN_V_DIM), 0.02),
        "w_att_branch": nrm(ks[9], (DEPTH, ATT_W, D_MODEL), ATT_W ** -0.5),
        "w_dn_branch": nrm(ks[10], (DEPTH, DN_V_W, D_MODEL), DN_V_W ** -0.5),
        "w_out": nrm(ks[11], (DEPTH, D_MODEL, D_MODEL), D_MODEL ** -0.5),
        "final_norm_w": 1.0 + nrm(ks[12], (D_MODEL,), 0.02),
    }


def reference(x, norm_w, w_in, b_qkv, sinks, conv_w, a_log, dt_bias, dn_norm_w,
              w_att_branch, w_dn_branch, w_out, final_norm_w):
    B, S, _ = x.shape
    for l in range(DEPTH):
        h = rms_norm(x, norm_w[l])
        proj = jnp.einsum('bsd,de->bse', h, w_in[l])

        qkv = proj[..., OFF_AQ:OFF_AZ] + b_qkv[l]
        q_a = qkv[..., :ATT_Q_W].reshape(B, S, N_Q_HEADS, HEAD_DIM)
        k_a = qkv[..., ATT_Q_W:ATT_Q_W + ATT_KV_W].reshape(B, S, N_KV_HEADS, HEAD_DIM)
        v_a = qkv[..., ATT_Q_W + ATT_KV_W:].reshape(B, S, N_KV_HEADS, HEAD_DIM)
        o_a = sliding_window_attention(q_a, k_a, v_a, sinks[l])
        o_a = o_a * jax.nn.silu(proj[..., OFF_AZ:OFF_DQKV])
        y_a = jnp.einsum('bse,ed->bsd', o_a, w_att_branch[l])

        mixed = jax.nn.silu(causal_depthwise_conv(proj[..., OFF_DQKV:OFF_DZ], conv_w[l]))
        q_d = l2_norm(mixed[..., :DN_QK_W].reshape(B, S, DN_K_HEADS, DN_K_DIM))
        k_d = l2_norm(mixed[..., DN_QK_W:2 * DN_QK_W].reshape(B, S, DN_K_HEADS, DN_K_DIM))
        v_d = mixed[..., 2 * DN_QK_W:].reshape(B, S, DN_V_HEADS, DN_V_DIM)
        q_d = jnp.repeat(q_d, V_PER_K, axis=2)
        k_d = jnp.repeat(k_d, V_PER_K, axis=2)
        beta = jax.nn.sigmoid(proj[..., OFF_DB:OFF_DA].astype(jnp.float32))
        g = -jnp.exp(a_log[l].astype(jnp.float32)) * jax.nn.softplus(
            proj[..., OFF_DA:OFF_G].astype(jnp.float32) + dt_bias[l].astype(jnp.float32))
        o_d = gated_delta_rule(q_d, k_d, v_d, g, beta).astype(x.dtype)
        z_d = proj[..., OFF_DZ:OFF_DB].reshape(B, S, DN_V_HEADS, DN_V_DIM)
        o_d = rms_norm(o_d, dn_norm_w[l]) * jax.nn.silu(z_d)
        y_d = jnp.einsum('bse,ed->bsd', o_d.reshape(B, S, DN_V_W), w_dn_branch[l])

        gates = jax.nn.sigmoid(proj[..., OFF_G:].astype(jnp.float32)).astype(x.dtype)
        merged = gates[..., :D_MODEL] * y_a + gates[..., D_MODEL:] * y_d
        x = x + jnp.einsum('bsd,de->bse', merged, w_out[l])
    return rms_norm(x, final_norm_w)
```

```python
import contextlib
import numpy as np
import concourse.bass as bass
import concourse.mybir as mybir
from concourse.bass_utils import run_bass_kernel_spmd

F32 = mybir.dt.float32
BF16 = mybir.dt.bfloat16
AF = mybir.ActivationFunctionType
ALU = mybir.AluOpType
AX = mybir.AxisListType

S = 2048
D = 2048
TT = 512
NT = S // TT
P = 128
NSLOT = 8
EPS = 1e-6
OFF_AQ = 0
OFF_AK = 2048
OFF_AV = 2304
OFF_AZ = 2560
OFF_DQKV = 4608
OFF_DZ = 12800
OFF_DB = 16896
OFF_DA = 16928
OFF_G = 16960
NEG = -30000.0

ENABLE_DN = True
NT_RUN = NT
SKIP_P3 = False
DN_LEVEL = 9
DN_SUB = 99
import os as _os
BISV = int(_os.environ.get('BISV', '0'))


class Builder:
    ENG = ["pe", "act", "dve", "pool", "sp"]

    def __init__(self):
        self.ops = {e: [] for e in self.ENG}
        self.lastw = {}
        self.readers = {}
        self.dma_cnt = {}
        self.mute = False

    def op(self, eng, fn, r=(), w=(), dma=None):
        if self.mute and dma is None:
            return None
        deps = set()
        for k in r:
            t = self.lastw.get(k)
            if t is not None:
                deps.add(t)
            if k.startswith("bank"):
                for t in self.readers.get(k, ()):
                    if not (t[0] == "e" and t[1] == eng):
                        deps.add(t)
        for k in w:
            t = self.lastw.get(k)
            if t is not None:
                deps.add(t)
            for t in self.readers.get(k, ()):
                deps.add(t)
        idx = len(self.ops[eng])
        if dma is not None:
            self.dma_cnt[dma] = self.dma_cnt.get(dma, 0) + 1
            tok = ("d", dma, 16 * self.dma_cnt[dma])
        else:
            tok = ("e", eng, idx)
        self.ops[eng].append({"fn": fn, "deps": deps, "dma": dma, "sig": False})
        for k in w:
            self.lastw[k] = tok
            self.readers[k] = []
        for k in r:
            if k in w:
                continue
            lst = self.readers.setdefault(k, [])
            if tok[0] == "e":
                lst[:] = [t for t in lst if not (t[0] == "e" and t[1] == eng)]
            lst.append(tok)
        return tok

    def wait_all(self, eng, toks):
        self.ops[eng].append({"fn": None, "deps": set(toks), "dma": None, "sig": False})

    def emit(self, nc, block, stack):
        for e in self.ENG:
            for rec in self.ops[e]:
                for t in rec["deps"]:
                    if t[0] == "e" and not (t[1] == "pe" and e == "pe"):
                        self.ops[t[1]][t[2]]["sig"] = True
        sem_e = {e: stack.enter_context(nc.semaphore("se_" + e)) for e in self.ENG}
        sem_d = {k: stack.enter_context(nc.semaphore("sd_" + k)) for k in self.dma_cnt}
        for e in self.ENG:
            c = 0
            for rec in self.ops[e]:
                if rec["sig"]:
                    c += 1
                rec["sv"] = c
        ops = self.ops

        def gen(e):
            def run(eh):
                waited = {}
                for rec in ops[e]:
                    need = {}
                    for t in rec["deps"]:
                        if t[0] == "e":
                            if t[1] == "pe" and e == "pe":
                                continue
                            key = ("e", t[1])
                            val = ops[t[1]][t[2]]["sv"]
                        else:
                            key = ("d", t[1])
                            val = t[2]
                        if val > need.get(key, 0):
                            need[key] = val
                    for key, val in need.items():
                        if waited.get(key, 0) >= val:
                            continue
                        waited[key] = val
                        sem = sem_e[key[1]] if key[0] == "e" else sem_d[key[1]]
                        eh.wait_ge(sem, val)
                    if rec["fn"] is None:
                        continue
                    ins = rec["fn"](eh)
                    if rec["dma"] is not None:
                        ins.then_inc(sem_d[rec["dma"]], 16)
                    elif rec["sig"]:
                        ins.then_inc(sem_e[e], 1)
            return run

        block.tensor(gen("pe"))
        block.scalar(gen("act"))
        block.vector(gen("dve"))
        block.gpsimd(gen("pool"))
        block.sync(gen("sp"))


def AP(t, off, dims):
    return bass.AP(t, off, [list(d) for d in dims])


def dup2(ap):
    a = [list(d) for d in ap.ap]
    assert len(a) == 2
    return bass.AP(ap.tensor, ap.offset, [a[0], [0, 2], a[1]])


def wblock(w, r0, cols):
    sub = w[r0:r0 + 2048][:, cols]
    return np.ascontiguousarray(sub.reshape(16, 128, 128).transpose(1, 0, 2))


def build_wstream(w_in, w_att, w_dn, w_out):
    blocks = []
    ar = np.arange

    def cin(c0, n=128):
        return ar(c0, c0 + n)

    blocks.append(wblock(w_in, 0, cin(OFF_AV)))
    blocks.append(wblock(w_in, 0, cin(OFF_AV + 128)))
    for g in range(4):
        kc = cin(OFF_AK + 64 * g, 64)
        blocks.append(wblock(w_in, 0, np.concatenate([kc, kc])))
        for j in range(4):
            blocks.append(wblock(w_in, 0, cin(OFF_AQ + 128 * (4 * g + j))))
        for j in range(4):
            blocks.append(wblock(w_in, 0, cin(OFF_AZ + 128 * (4 * g + j))))
    for d in range(16):
        blocks.append(wblock(w_in, 0, cin(OFF_G + 128 * d)))
        blocks.append(wblock(w_att, 0, cin(128 * d)))
    ba = cin(OFF_DB, 64)
    blocks.append(wblock(w_in, 0, np.concatenate([ba, ba])))
    for kh in range(16):
        blocks.append(wblock(w_in, 0, cin(OFF_DQKV + 128 * kh)))
        blocks.append(wblock(w_in, 0, cin(OFF_DQKV + 2048 + 128 * kh)))
        for h in range(2):
            blocks.append(wblock(w_in, 0, cin(OFF_DQKV + 4096 + 128 * (2 * kh + h))))
        for h in range(2):
            blocks.append(wblock(w_in, 0, cin(OFF_DZ + 128 * (2 * kh + h))))
    for d in range(16):
        blocks.append(wblock(w_in, 0, cin(OFF_G + 2048 + 128 * d)))
        blocks.append(wblock(w_dn, 0, cin(128 * d)))
        blocks.append(wblock(w_dn, 2048, cin(128 * d)))
    while len(blocks) % 4 != 0:
        blocks.append(blocks[-1])
    for cg in range(4):
        for j in range(4):
            blocks.append(wblock(w_out, 0, cin(512 * cg + 128 * j)))
    return np.ascontiguousarray(np.stack(blocks).reshape(len(blocks), 128, 2048))


N_ATT_BLK = 2 + 4 * 9 + 32
N_DN_BLK = 1 + 96 + 48


def consts_f32():
    p = np.arange(128)
    h = p // 64
    t = p % 64
    same = (h[:, None] == h[None, :])
    bdu = (same & (t[:, None] <= t[None, :])).astype(np.float32)
    bdsu = (same & (t[:, None] > t[None, :])).astype(np.float32)
    hsel0 = np.repeat((h == 0)[:, None], 128, 1).astype(np.float32)
    hsel1 = np.repeat((h == 1)[:, None], 128, 1).astype(np.float32)
    bdones = same.astype(np.float32)
    ident = np.eye(128, dtype=np.float32)
    i64 = np.arange(64)
    u2 = (t[:, None] <= i64[None, :]).astype(np.float32)
    ones64 = np.ones((128, 64), np.float32)
    masks = np.where(i64[None, :] < t[:, None], 0.0, NEG).astype(np.float32)
    maskt = np.where(i64[None, :] >= t[:, None], 0.0, NEG).astype(np.float32)
    parts = [ident, bdu, bdsu, hsel0, hsel1, bdones, -bdones, u2, ones64, -ones64, masks, maskt]
    return np.ascontiguousarray(np.concatenate(parts, axis=1))


CF_OFF = {}
_o = 0
for _n, _w in [("ident", 128), ("bdu", 128), ("bdsu", 128), ("hsel0", 128), ("hsel1", 128), ("bdones", 128),
               ("bdneg", 128), ("u2", 64), ("ones64", 64), ("neg64", 64), ("masks", 64), ("maskt", 64)]:
    CF_OFF[_n] = (_o, _w)
    _o += _w
NCF = _o

CB_OFF = {}
_o = 0
for _n, _w in [("ident", 128), ("mown", 128), ("mprev", 128), ("ones64", 64), ("onescol", 2), ("onesrow", 128),
               ("bv", 256)]:
    CB_OFF[_n] = (_o, _w)
    _o += _w
NCB = _o


def consts_b16(b_qkv):
    j = np.arange(128)[:, None]
    i = np.arange(128)[None, :]
    ident = np.eye(128, dtype=np.float32)
    mown = (j <= i).astype(np.float32)
    mprev = (j > i).astype(np.float32)
    ones64 = np.ones((128, 64), np.float32)
    onescol = np.ones((128, 2), np.float32)
    onesrow = np.zeros((128, 128), np.float32)
    onesrow[0, :] = 1.0
    bv = np.zeros((128, 256), np.float32)
    bv[0, :] = b_qkv[2304:2560]
    return np.ascontiguousarray(np.concatenate([ident, mown, mprev, ones64, onescol, onesrow, bv], axis=1))


PR_OFF = {}
_o = 0
for _n, _w in [("nw", 16), ("bq", 16), ("bk", 4), ("sk", 16), ("cw", 256), ("dtb", 16), ("alg", 16), ("dnw", 1)]:
    PR_OFF[_n] = (_o, _w)
    _o += _w
NPR = _o


def params_f32(norm_w, b_qkv, sinks, conv_w, a_log, dt_bias, dn_norm_w):
    p = np.arange(128)
    hi = (p >= 64).astype(np.int64)
    nw = norm_w.reshape(16, 128).T
    bq = b_qkv[0:2048].reshape(16, 128).T
    bk = b_qkv[2048:2304].reshape(4, 64)[:, p % 64].T
    sk = sinks[(2 * np.arange(16))[None, :] + hi[:, None]]
    cw = conv_w.reshape(4, 64, 128).transpose(2, 1, 0).reshape(128, 256)
    dtb = dt_bias[(2 * np.arange(16))[None, :] + hi[:, None]]
    alg = a_log[(2 * np.arange(16))[None, :] + hi[:, None]]
    dnw = dn_norm_w.reshape(128, 1)
    return np.ascontiguousarray(np.concatenate([nw, bq, bk, sk, cw, dtb, alg, dnw], axis=1).astype(np.float32))


def build_program(n_wblk):
    nc = bass.Bass("TRN2", target_bir_lowering=False)
    x_d = nc.dram_tensor("x", [S, D], F32, kind="ExternalInput").ap()
    w_d = nc.dram_tensor("wst", [n_wblk, 128, 2048], F32, kind="ExternalInput").ap()
    cf_d = nc.dram_tensor("cf", [128, NCF], F32, kind="ExternalInput").ap()
    cb_d = nc.dram_tensor("cb", [128, NCB], F32, kind="ExternalInput").ap()
    pr_d = nc.dram_tensor("pr", [128, NPR], F32, kind="ExternalInput").ap()
    fw_d = nc.dram_tensor("fw", [128, D], F32, kind="ExternalInput").ap()
    y_d = nc.dram_tensor("y", [S, D], F32, kind="ExternalOutput").ap()

    B = Builder()
    stack = contextlib.ExitStack()
    with stack:
        def sb(name, shape, dt):
            return stack.enter_context(nc.sbuf_tensor(name, shape, dt))

        def ps(name, shape=(128, 512), dt=F32):
            return stack.enter_context(nc.psum_tensor(name, list(shape), dt))

        ring = sb("ring", [128, NSLOT, 16, 128], BF16)
        hT = sb("hT", [128, 16, TT], BF16)
        m1T = sb("m1T", [128, 16, TT], BF16)
        XO = sb("XO", [128, 4, D], F32)
        XOb = XO[:].bitcast(BF16)
        FW = sb("FW", [128, D], F32)
        CF = sb("CF", [128, NCF], F32)
        CB = sb("CB", [128, NCB], BF16)
        PR = sb("PR", [128, NPR], F32)
        hb = sb("hb", [128, D], BF16)
        st1 = sb("st1", [128, 8], F32)
        qT = sb("qT", [128, 4, TT], BF16)
        zT = sb("zT", [128, 4, TT], BF16)
        kT2 = sb("kT2", [128, 4, 128 + TT], BF16)
        vtok = sb("vtok", [128, 5, 256], BF16)
        pT = sb("pT", [128, 4, 512], BF16)
        dnb = sb("dnb", [128, 512], F32)
        ogf = sb("ogf", [128, 512], F32)
        gsb = sb("gsb", [128, 1, 512], F32)
        ES = sb("ES", [128, 16], F32)
        NEGA = sb("NEGA", [128, 16], F32)

        S2 = sb("S2", [128, 16, 256], F32)
        S2b = sb("S2b", [128, 16, 256], BF16)
        HALO = sb("HALO", [128, 64, 3], F32)
        BET = sb("BET", [128, 8, 16], F32)
        GST = sb("GST", [128, 8, 16], F32)
        EX = sb("EX", [128, 4, 8, 16], F32)
        CS = [sb("CS%d" % i, [128, 516], F32) for i in range(2)]
        ACC = [sb("ACC%d" % i, [128, 512], F32) for i in range(2)]
        XQK = sb("XQK", [128, 2, TT], BF16)
        VT = [sb("VT%d" % i, [128, TT], BF16) for i in range(2)]
        ZS = [sb("ZS%d" % i, [128, TT], BF16) for i in range(2)]
        SQ = sb("SQ", [128, 2, TT], BF16)
        RS = sb("RS", [128, 8, 2], F32)
        SC = sb("SC", [128, 5, 8], F32)
        XKD = sb("XKD", [128, 8, 128], BF16)
        VB = sb("VB", [128, 8, 128], F32)
        GP = sb("GP", [128, 64], F32)
        GBD = sb("GBD", [128, 128], F32)
        DSS = sb("DSS", [128, 128], F32)
        P0B = [sb("P0B%d" % i, [128, 128], BF16) for i in range(2)]
        Q0B = sb("Q0B", [128, 128], BF16)
        PQ = [sb("PQ%d" % i, [128, 256], BF16) for i in range(2)]
        Y32 = sb("Y32", [128, 128], F32)
        Yb = sb("Yb", [128, 128], BF16)
        YT = sb("YT", [128, 8, 128], BF16)
        MQ = sb("MQ", [128, 8, 128], BF16)
        RB = sb("RB", [128, 128], BF16)
        VPB = sb("VPB", [128, 128], BF16)
        T1 = sb("T1", [128, 128], F32)
        OB = sb("OB", [128, 8, 128], F32)
        JK = sb("JK", [128, 128], BF16)
        SSO = sb("SSO", [128, 8, 4], F32)
        ON = sb("ON", [128, 8, 128], BF16)

        banks = [ps("bank%d" % i) for i in range(8)]

        def cf(name):
            o, w = CF_OFF[name]
            return CF[:, o:o + w]

        def cb(name):
            o, w = CB_OFF[name]
            return CB[:, o:o + w]

        def pr(name, j=None, n=1):
            o, w = PR_OFF[name]
            if j is None:
                return PR[:, o:o + w]
            return PR[:, o + j:o + j + n]

        xob_t = XOb.tensor
        xob_ps = XOb.ap[0][0]

        def XOb_view(b, t0, n):
            return AP(xob_t, XOb.offset + b * TT + t0, [[xob_ps, 128], [1, n]])

        def XOb_view3(b0, nb, t0, n):
            return AP(xob_t, XOb.offset + b0 * TT + t0, [[xob_ps, 128], [TT, nb], [1, n]])

        B.op("sp", lambda e: e.dma_start(out=CF[:], in_=cf_d), w=["CF"], dma="c0")
        B.op("sp", lambda e: e.dma_start(out=PR[:], in_=pr_d), w=["PR"], dma="c1")
        B.op("sp", lambda e: e.dma_start(out=FW[:], in_=fw_d), w=["FW"], dma="c2")
        B.op("pool", lambda e: e.dma_start(out=CB[:], in_=cb_d), w=["CB"], dma="c3")
        B.op("act", lambda e: e.activation(out=ES[:], in_=pr("sk"), func=AF.Exp), r=["PR"], w=["ES"])
        B.op("act", lambda e: e.activation(out=NEGA[:], in_=pr("alg"), func=AF.Exp), r=["PR"], w=["NEGA"])
        B.op("dve", lambda e: e.tensor_scalar(out=NEGA[:], in0=NEGA[:], scalar1=-1.0, scalar2=None, op0=ALU.mult),
             r=["NEGA"], w=["NEGA"])
        B.op("pool", lambda e: e.memset(kT2[:], 0.0), w=["kT2"])
        B.op("pool", lambda e: e.memset(vtok[:], 0.0), w=["vtok"])
        B.op("pool", lambda e: e.memset(S2[:], 0.0), w=["S2"])
        B.op("pool", lambda e: e.memset(S2b[:], 0.0), w=["S2b"])
        B.op("pool", lambda e: e.memset(HALO[:], 0.0), w=["HALO"])
        B.op("pool", lambda e: e.memset(MQ[:], 0.0), w=["MQ"])
        B.op("pool", lambda e: e.memset(P0B[0][:], 0.0), w=["P0B0"])
        B.op("pool", lambda e: e.memset(P0B[1][:], 0.0), w=["P0B1"])

        wstate = {"next_load": 0, "next_use": 0}
        n_total = n_wblk * 0 + 0

        def w_load_upto(k):
            while wstate["next_load"] < k:
                i = wstate["next_load"]
                sl = i % NSLOT
                src = w_d[i % n_wblk]
                B.op("pool", (lambda e, sl=sl, src=src: e.dma_start(
                    out=ring[:, sl, :, :].rearrange("p c n -> p (c n)"), in_=src)),
                    w=["ring%d" % sl], dma="w%d" % sl)
                wstate["next_load"] += 1

        def w_next():
            i = wstate["next_use"]
            wstate["next_use"] += 1
            w_load_upto(i + 1)
            return i % NSLOT

        def w_prefetch():
            lim = min(wstate["next_use"] + NSLOT, n_wblk * NT_RUN)
            w_load_upto(min(lim, wstate["next_use"] + NSLOT))

        pbank = {"i": 0}

        def proj_fm(slot, rhs_fn, nk=16, bankset=(0, 1)):
            bi = bankset[pbank["i"] % len(bankset)]
            pbank["i"] += 1
            bk = banks[bi]
            for c in range(nk):
                B.op("pe", (lambda e, c=c, bk=bk: e.matmul(bk[:], lhsT=ring[:, slot, c, :], rhs=rhs_fn(c),
                                                           start=(c == 0), stop=(c == nk - 1))),
                     r=["ring%d" % slot, "hT"], w=["bank%d" % bi])
            return bi

        out_toks = []

        for T in range(NT_RUN):
            t0 = T * TT
            B.op("sp", lambda e, t0=t0: e.dma_start(
                out=XO[:], in_=x_d[t0:t0 + TT, :].rearrange("(t p) d -> p t d", p=128)), w=["XO"], dma="x")
            for tb in range(4):
                B.op("act", lambda e, tb=tb: e.activation(out=hb[:], in_=XO[:, tb, :], func=AF.Square,
                                                          accum_out=st1[:, 0:1]), r=["XO"], w=["hb", "st1"])
                B.op("dve", lambda e: e.tensor_scalar(out=st1[:, 1:2], in0=st1[:, 0:1], scalar1=1.0 / D, scalar2=EPS,
                                                      op0=ALU.mult, op1=ALU.add), r=["st1"], w=["st1b"])
                B.op("act", lambda e: e.activation(out=st1[:, 2:3], in_=st1[:, 1:2], func=AF.Sqrt),
                     r=["st1b"], w=["st1c"])
                B.op("dve", lambda e: e.reciprocal(out=st1[:, 3:4], in_=st1[:, 2:3]), r=["st1c"], w=["st1d"])
                B.op("dve", lambda e, tb=tb: e.tensor_scalar(out=hb[:], in0=XO[:, tb, :], scalar1=st1[:, 3:4],
                                                             scalar2=None, op0=ALU.mult),
                     r=["XO", "st1d"], w=["hb"])
                for c4 in range(4):
                    bi = 2 + (c4 % 2)
                    bkb = banks[bi][:].bitcast(BF16)
                    for cc in range(4):
                        c = c4 * 4 + cc
                        B.op("pe", lambda e, c=c, cc=cc, bkb=bkb: e.transpose(
                            out=bkb[:, cc * 128:(cc + 1) * 128], in_=hb[:, c * 128:(c + 1) * 128], identity=cb("ident")),
                            r=["hb", "CB"], w=["bank%d" % bi])
                    for cc in range(4):
                        c = c4 * 4 + cc
                        B.op("dve", lambda e, c=c, cc=cc, bkb=bkb, tb=tb: e.tensor_scalar(
                            out=hT[:, c, tb * 128:(tb + 1) * 128], in0=bkb[:, cc * 128:(cc + 1) * 128],
                            scalar1=pr("nw", c), scalar2=None, op0=ALU.mult),
                            r=["bank%d" % bi, "PR"], w=["hT"])

            if T > 0:
                B.op("pool", lambda e: e.tensor_copy(out=kT2[:, :, 0:128], in_=kT2[:, :, TT:TT + 128]),
                     r=["kT2"], w=["kT2"])
                B.op("pool", lambda e: e.tensor_copy(out=vtok[:, 0, :], in_=vtok[:, 4, :]), r=["vtok"], w=["vtok"])
            sv = [w_next(), w_next()]
            for tb in range(4):
                bi = 2 + (tb % 2)
                for half in range(2):
                    o0 = half * 128
                    B.op("pe", lambda e, bi=bi, o0=o0: e.matmul(
                        banks[bi][:, o0:o0 + 128], lhsT=CB[0:1, CB_OFF["onesrow"][0]:CB_OFF["onesrow"][0] + 128],
                        rhs=CB[0:1, CB_OFF["bv"][0] + o0:CB_OFF["bv"][0] + o0 + 128], start=True, stop=False),
                        r=["CB"], w=["bank%d" % bi])
                    for c in range(16):
                        B.op("pe", lambda e, bi=bi, o0=o0, c=c, tb=tb, sl=sv[half]: e.matmul(
                            banks[bi][:, o0:o0 + 128], lhsT=hT[:, c, tb * 128:(tb + 1) * 128], rhs=ring[:, sl, c, :],
                            start=False, stop=(c == 15)), r=["hT", "ring%d" % sv[half]], w=["bank%d" % bi])
                B.op("act", lambda e, bi=bi, tb=tb: e.copy(out=vtok[:, 1 + tb, :], in_=banks[bi][:, 0:256]),
                     r=["bank%d" % bi], w=["vtok"])
            w_prefetch()

            for g in range(4):
                sl = w_next()
                bi = proj_fm(sl, lambda c: hT[:, c, :])
                B.op("dve", lambda e, bi=bi, g=g: e.tensor_scalar(
                    out=kT2[:, g, 128:128 + TT], in0=banks[bi][:], scalar1=pr("bk", g), scalar2=None, op0=ALU.add),
                    r=["bank%d" % bi, "PR"], w=["kT2"])
                w_prefetch()
                for j in range(4):
                    sl = w_next()
                    bi = proj_fm(sl, lambda c: hT[:, c, :])
                    B.op("dve", lambda e, bi=bi, g=g, j=j: e.tensor_scalar(
                        out=qT[:, j, :], in0=banks[bi][:], scalar1=pr("bq", 4 * g + j), scalar2=0.125,
                        op0=ALU.add, op1=ALU.mult), r=["bank%d" % bi, "PR"], w=["qT"])
                    w_prefetch()
                for j in range(4):
                    sl = w_next()
                    bi = proj_fm(sl, lambda c: hT[:, c, :])
                    B.op("act", lambda e, bi=bi, j=j: e.activation(out=zT[:, j, :], in_=banks[bi][:], func=AF.Silu),
                         r=["bank%d" % bi], w=["zT"])
                    w_prefetch()
                for qb in range(4):
                    nglob = 4 * T + qb
                    kinds = ["own"] + (["prev"] if nglob > 0 else [])
                    for half in range(2):
                        pp = slice(half * 64, half * 64 + 64)
                        for ki, kind in enumerate(kinds):
                            bi = 2 + half * 2 + ki
                            kcol = 128 + qb * 128 if kind == "own" else qb * 128
                            B.op("pe", lambda e, bi=bi, pp=pp, kcol=kcol, g=g, qb=qb: e.matmul(
                                banks[bi][:], lhsT=kT2[pp, g, kcol:kcol + 128],
                                rhs=qT[pp, :, qb * 128:(qb + 1) * 128], start=True, stop=True),
                                r=["kT2", "qT"], w=["bank%d" % bi])
                            pi = half * 2 + ki
                            B.op("act", lambda e, bi=bi, pi=pi: e.activation(
                                out=pT[:, pi, :], in_=banks[bi][:], func=AF.Exp), r=["bank%d" % bi], w=["pT%d" % pi])
                            mk = cb("mown") if kind == "own" else cb("mprev")
                            mk3 = AP(mk.tensor, mk.offset, [list(mk.ap[0]), [0, 4], [1, 128]])
                            B.op("pool", lambda e, pi=pi, mk3=mk3: e.tensor_tensor(
                                out=pT[:, pi, :].rearrange("p (h q) -> p h q", h=4),
                                in0=pT[:, pi, :].rearrange("p (h q) -> p h q", h=4), in1=mk3, op=ALU.mult),
                                r=["pT%d" % pi, "CB"], w=["pT%d" % pi])
                    for half in range(2):
                        po = slice(half * 64, half * 64 + 64)
                        for ki, kind in enumerate(kinds):
                            pi = half * 2 + ki
                            vb_ = 1 + qb if kind == "own" else qb
                            B.op("pe", lambda e, po=po, pi=pi, vb_=vb_, g=g, ki=ki, nk=len(kinds): e.matmul(
                                banks[6][po, :], lhsT=vtok[:, vb_, g * 64:(g + 1) * 64], rhs=pT[:, pi, :],
                                start=(ki == 0), stop=(ki == nk - 1)), r=["vtok", "pT%d" % pi], w=["bank6"])
                        for ki, kind in enumerate(kinds):
                            pi = half * 2 + ki
                            B.op("pe", lambda e, po=po, pi=pi, ki=ki, nk=len(kinds): e.matmul(
                                banks[7][po, :], lhsT=cb("ones64"), rhs=pT[:, pi, :],
                                start=(ki == 0), stop=(ki == nk - 1)), r=["CB", "pT%d" % pi], w=["bank7"])
                    es3 = AP(ES, 4 * g, [[16, 128], [1, 4], [0, 128]])
                    B.op("dve", lambda e, es3=es3: e.tensor_tensor(
                        out=dnb[:].rearrange("p (h q) -> p h q", h=4),
                        in0=banks[7][:].rearrange("p (h q) -> p h q", h=4), in1=es3, op=ALU.add),
                        r=["bank7", "ES"], w=["dnb"])
                    B.op("dve", lambda e: e.reciprocal(out=dnb[:], in_=dnb[:]), r=["dnb"], w=["dnb"])
                    B.op("dve", lambda e: e.tensor_tensor(out=ogf[:], in0=banks[6][:], in1=dnb[:], op=ALU.mult),
                         r=["bank6", "dnb"], w=["ogf"])
                    B.op("dve", lambda e, g=g, qb=qb: e.tensor_tensor(
                        out=XOb_view3(4 * g, 4, qb * 128, 128), in0=ogf[:].rearrange("p (h q) -> p h q", h=4),
                        in1=zT[:, :, qb * 128:(qb + 1) * 128], op=ALU.mult), r=["ogf", "zT"], w=["XO"])

            for d in range(16):
                sg = w_next()
                big = proj_fm(sg, lambda c: hT[:, c, :], bankset=(0, 1))
                gi = 0
                B.op("act", lambda e, big=big, gi=gi: e.activation(out=gsb[:, gi, :], in_=banks[big][:],
                                                                   func=AF.Sigmoid),
                     r=["bank%d" % big], w=["gsb%d" % gi])
                w_prefetch()
                sw = w_next()
                bia = 2 + (d % 2)
                for c in range(16):
                    B.op("pe", lambda e, c=c, bia=bia, sw=sw: e.matmul(
                        banks[bia][:], lhsT=ring[:, sw, c, :], rhs=XOb_view(c, 0, TT), start=(c == 0), stop=(c == 15)),
                        r=["ring%d" % sw, "XO"], w=["bank%d" % bia])
                B.op("dve", lambda e, bia=bia, gi=gi, d=d: e.tensor_tensor(
                    out=m1T[:, d, :], in0=banks[bia][:], in1=gsb[:, gi, :], op=ALU.mult),
                    r=["bank%d" % bia, "gsb%d" % gi], w=["m1T"])
                w_prefetch()

            if ENABLE_DN:
                DK_SCALE = 128.0 ** -0.5

                def pview(base, dims):
                    return AP(base.tensor, base.offset, [list(base.ap[0])] + [list(d_) for d_ in dims])

                sba = w_next()
                for kc in range(16):
                    B.op("pe", lambda e, kc=kc, sba=sba: e.matmul(
                        banks[2][0:64, :], lhsT=ring[:, sba, kc, 0:64], rhs=hT[:, kc, :],
                        start=(kc == 0), stop=(kc == 15)), r=["hT", "ring%d" % sba], w=["bank2"])
                w_prefetch()
                B.op("act", lambda e: e.copy(out=ACC[0][0:64, :], in_=banks[2][0:64, :]), r=["bank2"], w=["ACC0"])
                B.mute = DN_SUB < 2
                for c in range(8):
                    for h in range(2):
                        B.op("pe", lambda e, c=c, h=h: e.matmul(
                            banks[4][h * 64:(h + 1) * 64, c * 64:(c + 1) * 64], lhsT=ACC[0][0:64, c * 64:(c + 1) * 64],
                            rhs=CF[0:64, CF_OFF["ident"][0]:CF_OFF["ident"][0] + 64], start=True, stop=True),
                            r=["ACC0", "CF"], w=["bank4"])
                B.mute = DN_SUB < 3
                for h in range(2):
                    hp = slice(h * 64, (h + 1) * 64)
                    B.op("act", lambda e, h=h, hp=hp: e.activation(
                        out=BET[hp, :, :], in_=pview(banks[4][hp, h:h + 1], [[64, 8], [2, 16]]), func=AF.Sigmoid),
                        r=["bank4"], w=["BET"])
                    B.op("dve", lambda e, h=h, hp=hp: e.tensor_tensor(
                        out=GST[hp, :, :], in0=pview(banks[4][hp, 32 + h:33 + h], [[64, 8], [2, 16]]),
                        in1=pview(PR[hp, PR_OFF["dtb"][0]:PR_OFF["dtb"][0] + 1], [[0, 8], [1, 16]]), op=ALU.add),
                        r=["bank4", "PR"], w=["GST"])
                B.mute = DN_SUB < 4
                B.op("act", lambda e: e.activation(out=GST[:], in_=GST[:], func=AF.Exp), r=["GST"], w=["GST"])
                B.op("dve", lambda e: e.tensor_scalar(out=GST[:], in0=GST[:], scalar1=1.0, scalar2=None, op0=ALU.add),
                     r=["GST"], w=["GST"])
                B.op("act", lambda e: e.activation(out=GST[:], in_=GST[:], func=AF.Ln), r=["GST"], w=["GST"])
                B.op("dve", lambda e: e.tensor_tensor(
                    out=GST[:], in0=GST[:], in1=pview(NEGA[:, 0:1], [[0, 8], [1, 16]]), op=ALU.mult),
                    r=["GST", "NEGA"], w=["GST"])
                B.mute = DN_SUB < 5
                gflat = GST[:].rearrange("p c k -> p (c k)")
                for qi, nm in enumerate(["bdu", "bdsu", "hsel0", "hsel1"]):
                    B.op("pe", lambda e, qi=qi, nm=nm: e.matmul(
                        banks[3][:, qi * 128:(qi + 1) * 128], lhsT=cf(nm), rhs=gflat, start=True, stop=True),
                        r=["CF", "GST"], w=["bank3"])
                B.op("act", lambda e: e.activation(out=EX[:].rearrange("p a c k -> p (a c k)"), in_=banks[3][:],
                                                   func=AF.Exp), r=["bank3"], w=["EX"])

                B.mute = DN_SUB < 6
                for kh in range(16):
                    specs = [("k", 16 + kh, XQK[:, 0, :], "XQK"), ("q", kh, XQK[:, 1, :], "XQK")]
                    order = [("q", kh, XQK[:, 1, :], "XQKq"), ("k", 16 + kh, XQK[:, 0, :], "XQKk"),
                             ("v0", 32 + 2 * kh, VT[0][:], "VT0"), ("v1", 33 + 2 * kh, VT[1][:], "VT1")]
                    for oi, (nm, cblk, dst, dkey) in enumerate(order):
                        sl = w_next()
                        bi = proj_fm(sl, lambda c: hT[:, c, :])
                        ci = oi % 2
                        B.op("act", lambda e, bi=bi, ci=ci: e.copy(out=CS[ci][:, 3:515], in_=banks[bi][:]),
                             r=["bank%d" % bi], w=["CSm%d" % ci])
                        B.op("pool", lambda e, ci=ci, cblk=cblk: e.tensor_copy(out=CS[ci][:, 0:3], in_=HALO[:, cblk, :]),
                             r=["HALO%d" % cblk], w=["CSh%d" % ci])
                        B.op("pool", lambda e, ci=ci, cblk=cblk: e.tensor_copy(out=HALO[:, cblk, :], in_=CS[ci][:, 512:515]),
                             r=["CSm%d" % ci], w=["HALO%d" % cblk])
                        w_prefetch()
                        B.op("dve", lambda e, ci=ci, cblk=cblk: e.tensor_scalar(
                            out=ACC[ci][:], in0=CS[ci][:, 0:512], scalar1=pr("cw", cblk * 4 + 0), scalar2=None,
                            op0=ALU.mult), r=["CSm%d" % ci, "CSh%d" % ci, "PR"], w=["ACC%d" % ci])
                        for ti in range(1, 4):
                            B.op("dve", lambda e, ci=ci, cblk=cblk, ti=ti: e.scalar_tensor_tensor(
                                out=ACC[ci][:], in0=CS[ci][:, ti:ti + 512], scalar=pr("cw", cblk * 4 + ti),
                                in1=ACC[ci][:], op0=ALU.mult, op1=ALU.add),
                                r=["CSm%d" % ci, "CSh%d" % ci, "PR", "ACC%d" % ci], w=["ACC%d" % ci])
                        B.op("act", lambda e, ci=ci, dst=dst: e.activation(out=dst, in_=ACC[ci][:], func=AF.Silu),
                             r=["ACC%d" % ci], w=[dkey])
                    for h in range(2):
                        sl = w_next()
                        bi = proj_fm(sl, lambda c: hT[:, c, :])
                        B.op("act", lambda e, bi=bi, h=h: e.activation(out=ZS[h][:], in_=banks[bi][:], func=AF.Silu),
                             r=["bank%d" % bi], w=["ZS%d" % h])
                        w_prefetch()

                    if DN_LEVEL >= 3:
                        B.op("pool", lambda e: e.tensor_tensor(out=SQ[:], in0=XQK[:], in1=XQK[:], op=ALU.mult),
                             r=["XQKq", "XQKk"], w=["SQ"])
                        for c in range(8):
                            for qi in range(2):
                                for h in range(2):
                                    B.op("pe", lambda e, c=c, qi=qi, h=h: e.matmul(
                                        banks[7][h * 64:(h + 1) * 64, 256 + c * 2 + qi:256 + c * 2 + qi + 1],
                                        lhsT=SQ[:, 1 - qi, c * 64:(c + 1) * 64],
                                        rhs=CB[:, CB_OFF["onescol"][0]:CB_OFF["onescol"][0] + 1], start=True, stop=True),
                                        r=["SQ", "CB"], w=["bank7"])
                        B.op("dve", lambda e: e.tensor_scalar(
                            out=RS[:].rearrange("p c q -> p (c q)"), in0=banks[7][:, 256:272], scalar1=EPS, scalar2=None,
                            op0=ALU.add), r=["bank7"], w=["RS"])
                        B.op("act", lambda e: e.activation(out=RS[:], in_=RS[:], func=AF.Sqrt), r=["RS"], w=["RS"])
                        B.op("dve", lambda e: e.reciprocal(out=RS[:], in_=RS[:]), r=["RS"], w=["RS"])
                        rq = RS[:, :, 0]
                        rk = RS[:, :, 1]
                        betk = BET[:, :, kh]
                        egck = EX[:, 0, :, kh]
                        C1, CA, C2, CO, COE = (SC[:, i, :] for i in range(5))
                        B.op("dve", lambda e, betk=betk: e.tensor_tensor(out=C1, in0=betk, in1=rk, op=ALU.mult),
                             r=["BET", "RS"], w=["SC0"])
                        B.op("dve", lambda e: e.tensor_tensor(out=CA, in0=C1, in1=rk, op=ALU.mult),
                             r=["SC0", "RS"], w=["SC1"])
                        B.op("dve", lambda e, egck=egck: e.scalar_tensor_tensor(out=C2, in0=CA, scalar=-1.0, in1=egck,
                                                                     op0=ALU.mult, op1=ALU.mult),
                             r=["SC1", "EX"], w=["SC2"])
                        B.op("dve", lambda e: e.tensor_scalar(out=CO, in0=rq, scalar1=DK_SCALE, scalar2=None, op0=ALU.mult),
                             r=["RS"], w=["SC3"])
                        B.op("dve", lambda e, egck=egck: e.tensor_tensor(out=COE, in0=CO, in1=egck, op=ALU.mult),
                             r=["SC3", "EX"], w=["SC4"])

                    if DN_LEVEL >= 4:
                        for c4 in range(2):
                            bi = 4 + c4
                            for cc in range(4):
                                c = c4 * 4 + cc
                                co_ = cc * 128
                                for h in range(2):
                                    B.op("pe", lambda e, c=c, bi=bi, co_=co_, h=h: e.matmul(
                                        banks[bi][h * 64:(h + 1) * 64, co_:co_ + 128], lhsT=XQK[:, 0, c * 64:(c + 1) * 64],
                                        rhs=cb("ident"), start=True, stop=True), r=["XQKk", "CB"], w=["bank%d" % bi])
                            for cc in range(4):
                                c = c4 * 4 + cc
                                co_ = cc * 128
                                B.op("dve", lambda e, c=c, bi=bi, co_=co_, kh=kh: e.tensor_scalar(
                                    out=XKD[:, c, :], in0=banks[bi][:, co_:co_ + 128], scalar1=EX[:, 1, c, kh:kh + 1],
                                    scalar2=None, op0=ALU.mult), r=["bank%d" % bi, "EX"], w=["XKD"])
                        for c4 in range(2):
                            bi = 4 + c4
                            for cc in range(4):
                                c = c4 * 4 + cc
                                co_ = cc * 128
                                for h in range(2):
                                    B.op("pe", lambda e, c=c, bi=bi, co_=co_, h=h: e.matmul(
                                        banks[bi][h * 64:(h + 1) * 64, co_:co_ + 128], lhsT=VT[h][:, c * 64:(c + 1) * 64],
                                        rhs=cb("ident"), start=True, stop=True), r=["VT%d" % h, "CB"], w=["bank%d" % bi])
                            for cc in range(4):
                                c = c4 * 4 + cc
                                co_ = cc * 128
                                B.op("dve", lambda e, c=c, bi=bi, co_=co_: e.tensor_scalar(
                                    out=VB[:, c, :], in0=banks[bi][:, co_:co_ + 128], scalar1=SC[:, 0, c:c + 1],
                                    scalar2=None, op0=ALU.mult), r=["bank%d" % bi, "SC0"], w=["VB"])

                    if DN_LEVEL >= 5:
                        for c in range(8):
                            gcol = GST[:, c, kh:kh + 1]
                            B.op("pool", lambda e, gcol=gcol: e.tensor_scalar(out=GP[:], in0=cf("u2"), scalar1=gcol,
                                                                             scalar2=None, op0=ALU.mult),
                                 r=["CF", "GST"], w=["GP"])
                            B.op("pool", lambda e, gcol=gcol: e.tensor_scalar(out=GBD[:], in0=cf("bdu"), scalar1=gcol,
                                                                             scalar2=None, op0=ALU.mult),
                                 r=["CF", "GST"], w=["GBD"])
                            for h in range(2):
                                B.op("pe", lambda e, c=c, h=h: e.matmul(
                                    banks[2][h * 64:(h + 1) * 64, 0:128], lhsT=XQK[:, 0, c * 64:(c + 1) * 64],
                                    rhs=XQK[:, :, c * 64:(c + 1) * 64], start=True, stop=True),
                                    r=["XQKk", "XQKq"], w=["bank2"])
                            B.op("pe", lambda e: e.matmul(banks[2][:, 128:192], lhsT=GBD[:], rhs=cf("ones64"),
                                                          start=True, stop=False), r=["GBD", "CF"], w=["bank2"])
                            B.op("pe", lambda e: e.matmul(banks[2][:, 128:192], lhsT=cf("bdneg"), rhs=GP[:],
                                                          start=False, stop=False), r=["GP", "CF"], w=["bank2"])
                            B.op("pe", lambda e: e.matmul(banks[2][:, 128:192], lhsT=cf("ident"), rhs=cf("masks"),
                                                          start=False, stop=True), r=["CF"], w=["bank2"])
                            B.op("pe", lambda e: e.matmul(banks[2][:, 192:256], lhsT=cf("bdones"), rhs=GP[:],
                                                          start=True, stop=False), r=["GP", "CF"], w=["bank2"])
                            B.op("pe", lambda e: e.matmul(banks[2][:, 192:256], lhsT=GBD[:], rhs=cf("neg64"),
                                                          start=False, stop=False), r=["GBD", "CF"], w=["bank2"])
                            B.op("pe", lambda e: e.matmul(banks[2][:, 192:256], lhsT=cf("ident"), rhs=cf("maskt"),
                                                          start=False, stop=True), r=["CF"], w=["bank2"])
                            B.op("act", lambda e: e.activation(out=DSS[:], in_=banks[2][:, 128:256], func=AF.Exp),
                                 r=["bank2"], w=["DSS"])
                            p0 = P0B[c % 2]
                            p0k = "P0B%d" % (c % 2)
                            for h in range(2):
                                hp = slice(h * 64, (h + 1) * 64)
                                B.op("dve", lambda e, hp=hp, c=c, p0=p0: e.scalar_tensor_tensor(
                                    out=p0[hp, hp], in0=banks[2][hp, 0:64], scalar=SC[hp, 1, c:c + 1], in1=DSS[hp, 0:64],
                                    op0=ALU.mult, op1=ALU.mult), r=["bank2", "SC1", "DSS"], w=[p0k])
                                B.op("dve", lambda e, hp=hp, c=c: e.tensor_tensor(
                                    out=MQ[hp, c, hp], in0=banks[2][hp, 64:128], in1=DSS[hp, 64:128], op=ALU.mult),
                                    r=["bank2", "DSS"], w=["MQ"])
                            B.op("pe", lambda e, p0=p0: e.matmul(banks[3][:, 0:128], lhsT=p0[:], rhs=cb("ident"),
                                                                 start=True, stop=True), r=[p0k, "CB"], w=["bank3"])
                            B.op("act", lambda e: e.copy(out=Q0B[:], in_=banks[3][:, 0:128]), r=["bank3"], w=["Q0B"])
                            B.op("dve", lambda e: e.tensor_tensor(out=Y32[:], in0=cf("ident"), in1=banks[3][:, 0:128],
                                                                  op=ALU.subtract), r=["bank3", "CF"], w=["Y32"])
                            B.op("act", lambda e: e.copy(out=Yb[:], in_=Y32[:]), r=["Y32"], w=["Yb"])
                            for k in range(5):
                                if k == 0:
                                    Pk, Qk, pk_keys = p0[:], Q0B[:], [p0k, "Q0B"]
                                else:
                                    Pk, Qk, pk_keys = PQ[k % 2][:, 0:128], PQ[k % 2][:, 128:256], ["PQ%d" % (k % 2)]
                                nxt = PQ[(k + 1) % 2]
                                nk_ = "PQ%d" % ((k + 1) % 2)
                                B.op("pe", lambda e, Pk=Pk, Qk=Qk: e.matmul(banks[4][:, 0:128], lhsT=Qk, rhs=Pk,
                                                                            start=True, stop=True),
                                     r=pk_keys, w=["bank4"])
                                if k < 4:
                                    B.op("pe", lambda e, Pk=Pk, Qk=Qk: e.matmul(banks[4][:, 128:256], lhsT=Pk, rhs=Qk,
                                                                                start=True, stop=True),
                                         r=pk_keys, w=["bank4"])
                                    B.op("act", lambda e, nxt=nxt: e.copy(out=nxt[:], in_=banks[4][:, 0:256]),
                                         r=["bank4"], w=[nk_])
                                else:
                                    B.op("act", lambda e, nxt=nxt: e.copy(out=nxt[:, 0:128], in_=banks[4][:, 0:128]),
                                         r=["bank4"], w=[nk_])
                                B.op("pe", lambda e, nxt=nxt: e.matmul(banks[3][:, 128:256], lhsT=nxt[:, 0:128], rhs=Yb[:],
                                                                       start=True, stop=True),
                                     r=[nk_, "Yb"], w=["bank3"])
                                B.op("dve", lambda e: e.tensor_tensor(out=Y32[:], in0=banks[3][:, 128:256], in1=Y32[:],
                                                                      op=ALU.add), r=["bank3", "Y32"], w=["Y32"])
                                if k < 4:
                                    B.op("act", lambda e: e.copy(out=Yb[:], in_=Y32[:]), r=["Y32"], w=["Yb"])
                                else:
                                    B.op("act", lambda e, c=c: e.copy(out=YT[:, c, :], in_=Y32[:]), r=["Y32"], w=["YT"])

                    if DN_LEVEL >= 6:
                        for c in range(8):
                            for h in range(2):
                                hp = slice(h * 64, (h + 1) * 64)
                                B.op("pe", lambda e, c=c, h=h, hp=hp, kh=kh: e.matmul(
                                    banks[6][hp, 0:128], lhsT=XQK[:, 0, c * 64:(c + 1) * 64],
                                    rhs=S2b[:, kh, h * 128:(h + 1) * 128], start=True, stop=True),
                                    r=["XQKk", "S2b"], w=["bank6"])
                                B.op("pe", lambda e, c=c, h=h, hp=hp, kh=kh: e.matmul(
                                    banks[6][hp, 128:256], lhsT=XQK[:, 1, c * 64:(c + 1) * 64],
                                    rhs=S2b[:, kh, h * 128:(h + 1) * 128], start=True, stop=True),
                                    r=["XQKq", "S2b"], w=["bank6"])
                            B.op("dve", lambda e, c=c: e.scalar_tensor_tensor(
                                out=RB[:], in0=banks[6][:, 0:128], scalar=SC[:, 2, c:c + 1], in1=VB[:, c, :],
                                op0=ALU.mult, op1=ALU.add), r=["bank6", "SC2", "VB"], w=["RB"])
                            B.op("pe", lambda e, c=c: e.matmul(banks[5][:, 0:128], lhsT=YT[:, c, :], rhs=RB[:],
                                                               start=True, stop=True), r=["YT", "RB"], w=["bank5"])
                            B.op("act", lambda e: e.copy(out=VPB[:], in_=banks[5][:, 0:128]), r=["bank5"], w=["VPB"])
                            B.op("pe", lambda e, c=c: e.matmul(banks[7][:, 0:128], lhsT=MQ[:, c, :], rhs=VPB[:],
                                                               start=True, stop=True), r=["MQ", "VPB"], w=["bank7"])
                            B.op("act", lambda e, c=c: e.activation(out=T1[:], in_=banks[6][:, 128:256], func=AF.Identity,
                                                                    scale=SC[:, 4, c:c + 1]),
                                 r=["bank6", "SC4"], w=["T1"])
                            B.op("dve", lambda e, c=c: e.scalar_tensor_tensor(
                                out=OB[:, c, :], in0=banks[7][:, 0:128], scalar=SC[:, 3, c:c + 1], in1=T1[:],
                                op0=ALU.mult, op1=ALU.add), r=["bank7", "SC3", "T1"], w=["OB"])
                            sbk = [banks[3][:, 0:128], banks[4][:, 0:128]]
                            sbk_key = ["bank3", "bank4"]
                            for h in range(2):
                                hp = slice(h * 64, (h + 1) * 64)
                                B.op("pe", lambda e, c=c, h=h, hp=hp, sbk=sbk: e.matmul(
                                    sbk[h], lhsT=XKD[hp, c, :], rhs=VPB[hp, :],
                                    start=True, stop=True), r=["XKD", "VPB"], w=[sbk_key[h]])
                            for h in range(2):
                                B.op("dve", lambda e, c=c, h=h, kh=kh, sbk=sbk: e.scalar_tensor_tensor(
                                    out=S2[:, kh, h * 128:(h + 1) * 128], in0=S2[:, kh, h * 128:(h + 1) * 128],
                                    scalar=EX[:, 2 + h, c, kh:kh + 1], in1=sbk[h],
                                    op0=ALU.mult, op1=ALU.add), r=["S2", "EX", sbk_key[h]], w=["S2"])
                            B.op("act", lambda e, kh=kh: e.copy(out=S2b[:, kh, :], in_=S2[:, kh, :]), r=["S2"], w=["S2b"])
                            B.op("act", lambda e, c=c: e.activation(out=JK[:], in_=OB[:, c, :], func=AF.Square,
                                                                    accum_out=SSO[:, c, 0:1]), r=["OB"], w=["JK", "SSO"])

                    if DN_LEVEL >= 7:
                        B.op("dve", lambda e: e.tensor_scalar(out=SSO[:, :, 1], in0=SSO[:, :, 0], scalar1=1.0 / 128.0,
                                                              scalar2=EPS, op0=ALU.mult, op1=ALU.add),
                             r=["SSO"], w=["SSO1"])
                        B.op("act", lambda e: e.activation(out=SSO[:, :, 2], in_=SSO[:, :, 1], func=AF.Sqrt),
                             r=["SSO1"], w=["SSO2"])
                        B.op("dve", lambda e: e.reciprocal(out=SSO[:, :, 3], in_=SSO[:, :, 2]), r=["SSO2"], w=["SSO3"])
                        B.op("dve", lambda e: e.tensor_tensor(
                            out=ON[:], in0=OB[:], in1=pview(SSO[:, 0, 3:4], [[4, 8], [0, 128]]), op=ALU.mult),
                            r=["OB", "SSO3"], w=["ON"])
                        for c4 in range(2):
                            bi = 4 + c4
                            bkb = banks[bi][:].bitcast(BF16)
                            for cc in range(4):
                                c = c4 * 4 + cc
                                B.op("pe", lambda e, c=c, cc=cc, bkb=bkb: e.transpose(
                                    out=bkb[:, cc * 128:(cc + 1) * 128], in_=ON[:, c, :], identity=cb("ident")),
                                    r=["ON", "CB"], w=["bank%d" % bi])
                            for h in range(2):
                                B.op("dve", lambda e, bkb=bkb, h=h, c4=c4, kh=kh: e.scalar_tensor_tensor(
                                    out=XOb_view(2 * kh + h, c4 * 256, 256).rearrange("p (c i) -> p c i", c=4),
                                    in0=pview(bkb[:, h * 64:h * 64 + 1], [[128, 4], [1, 64]]), scalar=pr("dnw", 0),
                                    in1=ZS[h][:, c4 * 256:(c4 + 1) * 256].rearrange("p (c i) -> p c i", c=4),
                                    op0=ALU.mult, op1=ALU.mult), r=["bank%d" % bi, "PR", "ZS%d" % h], w=["XO"])

                B.mute = DN_SUB < 7
                for d in range(16):
                    sg = w_next()
                    big = proj_fm(sg, lambda c: hT[:, c, :], bankset=(0, 1))
                    gi = 0
                    B.op("act", lambda e, big=big, gi=gi: e.activation(out=gsb[:, gi, :], in_=banks[big][:],
                                                                       func=AF.Sigmoid),
                         r=["bank%d" % big], w=["gsb%d" % gi])
                    w_prefetch()
                    s1 = w_next()
                    s2 = w_next()
                    bia = 2 + (d % 2)
                    for c in range(32):
                        sl_ = s1 if c < 16 else s2
                        B.op("pe", lambda e, c=c, bia=bia, sl_=sl_: e.matmul(
                            banks[bia][:], lhsT=ring[:, sl_, c % 16, :], rhs=XOb_view(c, 0, TT),
                            start=(c == 0), stop=(c == 31)), r=["ring%d" % sl_, "XO"], w=["bank%d" % bia])
                    B.op("dve", lambda e, bia=bia, gi=gi: e.tensor_tensor(
                        out=gsb[:, gi, :], in0=banks[bia][:], in1=gsb[:, gi, :], op=ALU.mult),
                        r=["bank%d" % bia, "gsb%d" % gi], w=["gsb%d" % gi])
                    B.op("pool", lambda e, gi=gi, d=d: e.tensor_tensor(
                        out=m1T[:, d, :], in0=gsb[:, gi, :], in1=m1T[:, d, :], op=ALU.add),
                        r=["gsb%d" % gi, "m1T"], w=["m1T"])
                    w_prefetch()
            else:
                for _ in range(N_DN_BLK):
                    w_next()
                    w_prefetch()
            B.mute = False
            while wstate["next_use"] % 4 != 0:
                w_next()
                w_prefetch()

            if SKIP_P3:
                continue
            B.op("sp", lambda e, t0=t0: e.dma_start(
                out=XO[:], in_=x_d[t0:t0 + TT, :].rearrange("(t p) d -> p t d", p=128)), r=[], w=["XO"], dma="x")
            for cg in range(4):
                sls = [w_next() for _ in range(4)]
                assert sls[0] % 4 == 0 and sls == list(range(sls[0], sls[0] + 4))
                for tb in range(4):
                    bi = 2 + ((cg * 4 + tb) % 4)
                    for c in range(16):
                        rhs = AP(ring, ring[:, sls[0], c, :].offset, [list(ring[:].ap[0]), [16 * 128, 4], [1, 128]])
                        B.op("pe", lambda e, bi=bi, c=c, tb=tb, rhs=rhs: e.matmul(
                            banks[bi][:], lhsT=m1T[:, c, tb * 128:(tb + 1) * 128], rhs=rhs,
                            start=(c == 0), stop=(c == 15)),
                            r=["m1T"] + ["ring%d" % s_ for s_ in sls], w=["bank%d" % bi])
                    B.op("dve", lambda e, bi=bi, tb=tb, cg=cg: e.tensor_tensor(
                        out=XO[:, tb, cg * 512:(cg + 1) * 512], in0=banks[bi][:], in1=XO[:, tb, cg * 512:(cg + 1) * 512],
                        op=ALU.add), r=["bank%d" % bi, "XO"], w=["XO"])
                w_prefetch()
            for tb in range(4):
                B.op("act", lambda e, tb=tb: e.activation(out=hb[:], in_=XO[:, tb, :], func=AF.Square,
                                                          accum_out=st1[:, 0:1]), r=["XO"], w=["hb", "st1"])
                B.op("dve", lambda e: e.tensor_scalar(out=st1[:, 1:2], in0=st1[:, 0:1], scalar1=1.0 / D, scalar2=EPS,
                                                      op0=ALU.mult, op1=ALU.add), r=["st1"], w=["st1b"])
                B.op("act", lambda e: e.activation(out=st1[:, 2:3], in_=st1[:, 1:2], func=AF.Sqrt),
                     r=["st1b"], w=["st1c"])
                B.op("dve", lambda e: e.reciprocal(out=st1[:, 3:4], in_=st1[:, 2:3]), r=["st1c"], w=["st1d"])
                B.op("dve", lambda e, tb=tb: e.scalar_tensor_tensor(
                    out=XO[:, tb, :], in0=XO[:, tb, :], scalar=st1[:, 3:4], in1=FW[:], op0=ALU.mult, op1=ALU.mult),
                    r=["XO", "st1d", "FW"], w=["XO"])
                tok = B.op("sp", lambda e, tb=tb, t0=t0: e.dma_start(
                    out=y_d[t0 + tb * 128:t0 + (tb + 1) * 128, :], in_=XO[:, tb, :]), r=["XO"], dma="o%d" % tb)
                out_toks.append(tok)

        B.wait_all("sp", out_toks)
        block = stack.enter_context(nc.Block())
        B.emit(nc, block, stack)
    return nc


_CACHE = {}


def kernel(x, norm_w, w_in, b_qkv, sinks, conv_w, a_log, dt_bias, dn_norm_w,
           w_att_branch, w_dn_branch, w_out, final_norm_w):
    x = np.asarray(x, np.float32)
    wst = build_wstream(np.asarray(w_in[0], np.float32), np.asarray(w_att_branch[0], np.float32),
                        np.asarray(w_dn_branch[0], np.float32), np.asarray(w_out[0], np.float32))
    cfa = consts_f32()
    cba = consts_b16(np.asarray(b_qkv[0], np.float32))
    pra = params_f32(np.asarray(norm_w[0]), np.asarray(b_qkv[0]), np.asarray(sinks[0]), np.asarray(conv_w[0]),
                     np.asarray(a_log[0]), np.asarray(dt_bias[0]), np.asarray(dn_norm_w[0]))
    fwa = np.ascontiguousarray(np.broadcast_to(np.asarray(final_norm_w, np.float32)[None, :], (128, D)))
    n_wblk = wst.shape[0]
    if "nc" not in _CACHE:
        _CACHE["nc"] = build_program(n_wblk)
    nc = _CACHE["nc"]
    in_maps = [{"x": np.ascontiguousarray(x[b]), "wst": wst, "cf": cfa, "cb": cba, "pr": pra, "fw": fwa}
               for b in range(8)]
    res = run_bass_kernel_spmd(nc, in_maps, core_ids=list(range(8)))
    return np.stack([res.results[b]["y"] for b in range(8)], axis=0).astype(np.float32)
```

```python
import contextlib
import numpy as np
import concourse.bass as bass
import concourse.mybir as mybir
from concourse.bass_utils import run_bass_kernel_spmd

F32 = mybir.dt.float32
BF16 = mybir.dt.bfloat16
AF = mybir.ActivationFunctionType
ALU = mybir.AluOpType
AX = mybir.AxisListType

S = 2048
D = 2048
TT = 512
NT = S // TT
P = 128
NSLOT = 8
EPS = 1e-6
OFF_AQ = 0
OFF_AK = 2048
OFF_AV = 2304
OFF_AZ = 2560
OFF_DQKV = 4608
OFF_DZ = 12800
OFF_DB = 16896
OFF_DA = 16928
OFF_G = 16960
NEG = -30000.0
DGE_SCRATCH = 4096

ENABLE_DN = True
NT_RUN = NT
SKIP_P3 = False
DN_LEVEL = 9
DN_SUB = 99
import os as _os
BISV = int(_os.environ.get('BISV', '0'))


class Builder:
    ENG = ["pe", "act", "dve", "pool", "sp"]

    def __init__(self):
        self.ops = {e: [] for e in self.ENG}
        self.lastw = {}
        self.readers = {}
        self.dma_cnt = {}
        self.mute = False

    def op(self, eng, fn, r=(), w=(), dma=None):
        if self.mute and dma is None:
            return None
        deps = set()
        for k in r:
            t = self.lastw.get(k)
            if t is not None:
                deps.add(t)
            if k.startswith("bank"):
                for t in self.readers.get(k, ()):
                    if not (t[0] == "e" and t[1] == eng):
                        deps.add(t)
        for k in w:
            t = self.lastw.get(k)
            if t is not None:
                deps.add(t)
            for t in self.readers.get(k, ()):
                deps.add(t)
        idx = len(self.ops[eng])
        if dma is not None:
            self.dma_cnt[dma] = self.dma_cnt.get(dma, 0) + 1
            tok = ("d", dma, 16 * self.dma_cnt[dma])
        else:
            tok = ("e", eng, idx)
        self.ops[eng].append({"fn": fn, "deps": deps, "dma": dma, "sig": False})
        for k in w:
            self.lastw[k] = tok
            self.readers[k] = []
        for k in r:
            if k in w:
                continue
            lst = self.readers.setdefault(k, [])
            if tok[0] == "e":
                lst[:] = [t for t in lst if not (t[0] == "e" and t[1] == eng)]
            lst.append(tok)
        return tok

    def wait_all(self, eng, toks):
        self.ops[eng].append({"fn": None, "deps": set(toks), "dma": None, "sig": False})

    def emit(self, nc, block, stack):
        for e in self.ENG:
            for rec in self.ops[e]:
                for t in rec["deps"]:
                    if t[0] == "e" and not (t[1] == "pe" and e == "pe"):
                        self.ops[t[1]][t[2]]["sig"] = True
        sem_e = {e: stack.enter_context(nc.semaphore("se_" + e)) for e in self.ENG}
        sem_d = {k: stack.enter_context(nc.semaphore("sd_" + k)) for k in self.dma_cnt}
        for e in self.ENG:
            c = 0
            for rec in self.ops[e]:
                if rec["sig"]:
                    c += 1
                rec["sv"] = c
        ops = self.ops

        def gen(e):
            def run(eh):
                waited = {}
                for rec in ops[e]:
                    need = {}
                    for t in rec["deps"]:
                        if t[0] == "e":
                            if t[1] == "pe" and e == "pe":
                                continue
                            key = ("e", t[1])
                            val = ops[t[1]][t[2]]["sv"]
                        else:
                            key = ("d", t[1])
                            val = t[2]
                        if val > need.get(key, 0):
                            need[key] = val
                    for key, val in need.items():
                        if waited.get(key, 0) >= val:
                            continue
                        waited[key] = val
                        sem = sem_e[key[1]] if key[0] == "e" else sem_d[key[1]]
                        eh.wait_ge(sem, val)
                    if rec["fn"] is None:
                        continue
                    ins = rec["fn"](eh)
                    if rec["dma"] is not None:
                        ins.then_inc(sem_d[rec["dma"]], 16)
                    elif rec["sig"]:
                        ins.then_inc(sem_e[e], 1)
            return run

        block.tensor(gen("pe"))
        block.scalar(gen("act"))
        block.vector(gen("dve"))
        block.gpsimd(gen("pool"))
        block.sync(gen("sp"))


def AP(t, off, dims):
    return bass.AP(t, off, [list(d) for d in dims])


def dup2(ap):
    a = [list(d) for d in ap.ap]
    assert len(a) == 2
    return bass.AP(ap.tensor, ap.offset, [a[0], [0, 2], a[1]])


def wblock(w, r0, cols):
    sub = w[r0:r0 + 2048][:, cols]
    return np.ascontiguousarray(sub.reshape(16, 128, 128).transpose(1, 0, 2))


def build_wstream(w_in, w_att, w_dn, w_out):
    blocks = []
    ar = np.arange

    def cin(c0, n=128):
        return ar(c0, c0 + n)

    blocks.append(wblock(w_in, 0, cin(OFF_AV)))
    blocks.append(wblock(w_in, 0, cin(OFF_AV + 128)))
    for g in range(4):
        kc = cin(OFF_AK + 64 * g, 64)
        blocks.append(wblock(w_in, 0, np.concatenate([kc, kc])))
        for j in range(4):
            blocks.append(wblock(w_in, 0, cin(OFF_AQ + 128 * (4 * g + j))))
        for j in range(4):
            blocks.append(wblock(w_in, 0, cin(OFF_AZ + 128 * (4 * g + j))))
    for d in range(16):
        blocks.append(wblock(w_in, 0, cin(OFF_G + 128 * d)))
        blocks.append(wblock(w_att, 0, cin(128 * d)))
    ba = cin(OFF_DB, 64)
    blocks.append(wblock(w_in, 0, np.concatenate([ba, ba])))
    for kh in range(16):
        blocks.append(wblock(w_in, 0, cin(OFF_DQKV + 128 * kh)))
        blocks.append(wblock(w_in, 0, cin(OFF_DQKV + 2048 + 128 * kh)))
        for h in range(2):
            blocks.append(wblock(w_in, 0, cin(OFF_DQKV + 4096 + 128 * (2 * kh + h))))
        for h in range(2):
            blocks.append(wblock(w_in, 0, cin(OFF_DZ + 128 * (2 * kh + h))))
    for d in range(16):
        blocks.append(wblock(w_in, 0, cin(OFF_G + 2048 + 128 * d)))
        blocks.append(wblock(w_dn, 0, cin(128 * d)))
        blocks.append(wblock(w_dn, 2048, cin(128 * d)))
    while len(blocks) % 4 != 0:
        blocks.append(blocks[-1])
    for cg in range(4):
        for j in range(4):
            blocks.append(wblock(w_out, 0, cin(512 * cg + 128 * j)))
    return np.ascontiguousarray(np.stack(blocks).reshape(len(blocks), 128, 2048))


N_ATT_BLK = 2 + 4 * 9 + 32
N_DN_BLK = 1 + 96 + 48


def consts_f32():
    p = np.arange(128)
    h = p // 64
    t = p % 64
    same = (h[:, None] == h[None, :])
    bdu = (same & (t[:, None] <= t[None, :])).astype(np.float32)
    bdsu = (same & (t[:, None] > t[None, :])).astype(np.float32)
    hsel0 = np.repeat((h == 0)[:, None], 128, 1).astype(np.float32)
    hsel1 = np.repeat((h == 1)[:, None], 128, 1).astype(np.float32)
    bdones = same.astype(np.float32)
    ident = np.eye(128, dtype=np.float32)
    i64 = np.arange(64)
    u2 = (t[:, None] <= i64[None, :]).astype(np.float32)
    ones64 = np.ones((128, 64), np.float32)
    masks = np.where(i64[None, :] < t[:, None], 0.0, NEG).astype(np.float32)
    maskt = np.where(i64[None, :] >= t[:, None], 0.0, NEG).astype(np.float32)
    parts = [ident, bdu, bdsu, hsel0, hsel1, bdones, -bdones, u2, ones64, -ones64, masks, maskt]
    return np.ascontiguousarray(np.concatenate(parts, axis=1))


CF_OFF = {}
_o = 0
for _n, _w in [("ident", 128), ("bdu", 128), ("bdsu", 128), ("hsel0", 128), ("hsel1", 128), ("bdones", 128),
               ("bdneg", 128), ("u2", 64), ("ones64", 64), ("neg64", 64), ("masks", 64), ("maskt", 64)]:
    CF_OFF[_n] = (_o, _w)
    _o += _w
NCF = _o

CB_OFF = {}
_o = 0
for _n, _w in [("ident", 128), ("mown", 128), ("mprev", 128), ("ones64", 64), ("onescol", 2), ("onesrow", 128),
               ("bv", 256), ("bdones", 128), ("bdneg", 128), ("neg64", 64), ("masks", 64), ("maskt", 64)]:
    CB_OFF[_n] = (_o, _w)
    _o += _w
NCB = _o


def consts_b16(b_qkv):
    j = np.arange(128)[:, None]
    i = np.arange(128)[None, :]
    ident = np.eye(128, dtype=np.float32)
    mown = (j <= i).astype(np.float32)
    mprev = (j > i).astype(np.float32)
    ones64 = np.ones((128, 64), np.float32)
    onescol = np.ones((128, 2), np.float32)
    onesrow = np.zeros((128, 128), np.float32)
    onesrow[0, :] = 1.0
    bv = np.zeros((128, 256), np.float32)
    bv[0, :] = b_qkv[2304:2560]
    cfull = consts_f32()
    extra = [cfull[:, CF_OFF[n][0]:CF_OFF[n][0] + CF_OFF[n][1]] for n in ("bdones", "bdneg", "neg64", "masks", "maskt")]
    return np.ascontiguousarray(np.concatenate([ident, mown, mprev, ones64, onescol, onesrow, bv] + extra, axis=1))


PR_OFF = {}
_o = 0
for _n, _w in [("nw", 16), ("bq", 16), ("bk", 4), ("sk", 16), ("cw", 256), ("dtb", 16), ("alg", 16), ("dnw", 1)]:
    PR_OFF[_n] = (_o, _w)
    _o += _w
NPR = _o


def params_f32(norm_w, b_qkv, sinks, conv_w, a_log, dt_bias, dn_norm_w):
    p = np.arange(128)
    hi = (p >= 64).astype(np.int64)
    nw = norm_w.reshape(16, 128).T
    bq = b_qkv[0:2048].reshape(16, 128).T
    bk = b_qkv[2048:2304].reshape(4, 64)[:, p % 64].T
    sk = sinks[(2 * np.arange(16))[None, :] + hi[:, None]]
    cw = conv_w.reshape(4, 64, 128).transpose(2, 1, 0).reshape(128, 256)
    dtb = dt_bias[(2 * np.arange(16))[None, :] + hi[:, None]]
    alg = a_log[(2 * np.arange(16))[None, :] + hi[:, None]]
    dnw = dn_norm_w.reshape(128, 1)
    return np.ascontiguousarray(np.concatenate([nw, bq, bk, sk, cw, dtb, alg, dnw], axis=1).astype(np.float32))


def build_program(n_wblk):
    nc = bass.Bass("TRN2", target_bir_lowering=False, dynamic_dma_scratch_size=DGE_SCRATCH)
    x_d = nc.dram_tensor("x", [S, D], F32, kind="ExternalInput").ap()
    w_d = nc.dram_tensor("wst", [n_wblk, 128, 2048], F32, kind="ExternalInput").ap()
    cf_d = nc.dram_tensor("cf", [128, NCF], F32, kind="ExternalInput").ap()
    cb_d = nc.dram_tensor("cb", [128, NCB], F32, kind="ExternalInput").ap()
    pr_d = nc.dram_tensor("pr", [128, NPR], F32, kind="ExternalInput").ap()
    fw_d = nc.dram_tensor("fw", [128, D], F32, kind="ExternalInput").ap()
    y_d = nc.dram_tensor("y", [S, D], F32, kind="ExternalOutput").ap()

    B = Builder()
    stack = contextlib.ExitStack()
    with stack:
        def sb(name, shape, dt):
            return stack.enter_context(nc.sbuf_tensor(name, shape, dt))

        def ps(name, shape=(128, 512), dt=F32):
            return stack.enter_context(nc.psum_tensor(name, list(shape), dt))

        ring = sb("ring", [128, NSLOT, 16, 128], BF16)
        hT = sb("hT", [128, 16, TT], BF16)
        m1T = sb("m1T", [128, 16, TT], BF16)
        XO = sb("XO", [128, 4, D], F32)
        XOb = XO[:].bitcast(BF16)
        FW = sb("FW", [128, D], F32)
        CF = sb("CF", [128, NCF], F32)
        CB = sb("CB", [128, NCB], BF16)
        PR = sb("PR", [128, NPR], F32)
        hb = sb("hb", [128, D], BF16)
        st1 = sb("st1", [128, 8], F32)
        qT = sb("qT", [128, 4, TT], BF16)
        zT = sb("zT", [128, 4, TT], BF16)
        kT2 = sb("kT2", [128, 4, 128 + TT], BF16)
        vtok = sb("vtok", [128, 5, 256], BF16)
        pT = sb("pT", [128, 4, 512], BF16)
        dnb = sb("dnb", [128, 512], F32)
        ogf = sb("ogf", [128, 512], F32)
        gsb = sb("gsb", [128, 1, 512], F32)
        ES = sb("ES", [128, 16], F32)
        NEGA = sb("NEGA", [128, 16], F32)

        S2 = sb("S2", [128, 16, 256], F32)
        S2b = sb("S2b", [128, 16, 256], BF16)
        HALO = sb("HALO", [128, 64, 3], F32)
        BET = sb("BET", [128, 8, 16], F32)
        GST = sb("GST", [128, 8, 16], F32)
        EX = sb("EX", [128, 4, 8, 16], F32)
        CS = [sb("CS%d" % i, [128, 516], F32) for i in range(2)]
        ACC = [sb("ACC%d" % i, [128, 512], F32) for i in range(2)]
        XQK = sb("XQK", [128, 2, TT], BF16)
        VT = [sb("VT%d" % i, [128, TT], BF16) for i in range(2)]
        ZS = [sb("ZS%d" % i, [128, TT], BF16) for i in range(2)]
        SQ = sb("SQ", [128, 2, TT], BF16)
        RS = sb("RS", [128, 8, 2], F32)
        SC = sb("SC", [128, 5, 8], F32)
        XKD = sb("XKD", [128, 8, 128], BF16)
        VB = sb("VB", [128, 8, 128], F32)
        GP8h = sb("GP8h", [128, 8, 64], BF16)
        GP8l = sb("GP8l", [128, 8, 64], BF16)
        GHL = sb("GHL", [128, 2, 8], F32)
        GHb = sb("GHb", [128, 8], BF16)
        P0B8 = sb("P0B8", [128, 8, 128], BF16)
        Q0B8 = sb("Q0B8", [128, 8, 128], BF16)
        PQ8 = [sb("PQ8_%d" % i, [128, 2, 8, 128], BF16) for i in range(2)]
        Yb8 = sb("Yb8", [128, 8, 128], BF16)
        YT = sb("YT", [128, 8, 128], BF16)
        MQ = sb("MQ", [128, 8, 128], BF16)
        RB = sb("RB", [128, 128], BF16)
        VPB = sb("VPB", [128, 128], BF16)
        T1 = sb("T1", [128, 128], F32)
        OB = sb("OB", [128, 8, 128], F32)
        JK = sb("JK", [128, 128], BF16)
        SSO = sb("SSO", [128, 8, 4], F32)
        ON = sb("ON", [128, 8, 128], BF16)

        class V3:
            def __init__(self, base_ap, C, J):
                self.t = base_ap.tensor
                self.off = base_ap.offset
                self.ps = base_ap.ap[0][0]
                self.C, self.J = C, J

            def ap(self, p0=0, p1=128, c0=0, c1=None, j0=0, j1=None):
                c1 = self.C if c1 is None else c1
                j1 = self.J if j1 is None else j1
                dims = [[self.ps, p1 - p0]]
                if c1 - c0 > 1:
                    dims.append([self.J, c1 - c0])
                dims.append([1, j1 - j0])
                return AP(self.t, self.off + p0 * self.ps + c0 * self.J + j0, dims)

        _qf = qT[:].rearrange("p a b -> p (a b)")
        GBD8h = V3(_qf[:, 0:1024], 8, 128)
        GBD8l = V3(_qf[:, 1024:2048], 8, 128)
        _zf = zT[:].bitcast(F32).rearrange("p a b -> p (a b)")
        DSSa = V3(_zf[:, 0:512], 8, 64)
        DSSb = V3(_zf[:, 512:1024], 8, 64)
        Y32_8 = V3(pT[:].bitcast(F32), 8, 128)

        banks = [ps("bank%d" % i) for i in range(8)]

        def cf(name):
            o, w = CF_OFF[name]
            return CF[:, o:o + w]

        def cb(name):
            o, w = CB_OFF[name]
            return CB[:, o:o + w]

        def pr(name, j=None, n=1):
            o, w = PR_OFF[name]
            if j is None:
                return PR[:, o:o + w]
            return PR[:, o + j:o + j + n]

        xob_t = XOb.tensor
        xob_ps = XOb.ap[0][0]

        def XOb_view(b, t0, n):
            return AP(xob_t, XOb.offset + b * TT + t0, [[xob_ps, 128], [1, n]])

        def XOb_view3(b0, nb, t0, n):
            return AP(xob_t, XOb.offset + b0 * TT + t0, [[xob_ps, 128], [TT, nb], [1, n]])

        B.op("sp", lambda e: e.dma_start(out=CF[:], in_=cf_d), w=["CF"], dma="c0")
        B.op("sp", lambda e: e.dma_start(out=PR[:], in_=pr_d), w=["PR"], dma="c1")
        B.op("sp", lambda e: e.dma_start(out=FW[:], in_=fw_d), w=["FW"], dma="c2")
        B.op("pool", lambda e: e.dma_start(out=CB[:], in_=cb_d), w=["CB"], dma="c3")
        B.op("act", lambda e: e.activation(out=ES[:], in_=pr("sk"), func=AF.Exp), r=["PR"], w=["ES"])
        B.op("act", lambda e: e.activation(out=NEGA[:], in_=pr("alg"), func=AF.Exp), r=["PR"], w=["NEGA"])
        B.op("dve", lambda e: e.tensor_scalar(out=NEGA[:], in0=NEGA[:], scalar1=-1.0, scalar2=None, op0=ALU.mult),
             r=["NEGA"], w=["NEGA"])
        B.op("pool", lambda e: e.memset(kT2[:], 0.0), w=["kT2"])
        B.op("pool", lambda e: e.memset(vtok[:], 0.0), w=["vtok"])
        B.op("pool", lambda e: e.memset(S2[:], 0.0), w=["S2"])
        B.op("pool", lambda e: e.memset(S2b[:], 0.0), w=["S2b"])
        B.op("pool", lambda e: e.memset(HALO[:], 0.0), w=["HALO"])
        B.op("pool", lambda e: e.memset(MQ[:], 0.0), w=["MQ"])
        B.op("pool", lambda e: e.memset(P0B8[:], 0.0), w=["P0B8"])

        wstate = {"next_load": 0, "next_use": 0}
        n_total = n_wblk * 0 + 0

        def w_load_upto(k):
            while wstate["next_load"] < k:
                i = wstate["next_load"]
                sl = i % NSLOT
                src = w_d[i % n_wblk]
                B.op("pool", (lambda e, sl=sl, src=src: e.dma_start(
                    out=ring[:, sl, :, :].rearrange("p c n -> p (c n)"), in_=src)),
                    w=["ring%d" % sl], dma="w%d" % sl)
                wstate["next_load"] += 1

        def w_next():
            i = wstate["next_use"]
            wstate["next_use"] += 1
            w_load_upto(i + 1)
            return i % NSLOT

        def w_prefetch():
            lim = min(wstate["next_use"] + NSLOT, n_wblk * NT_RUN)
            w_load_upto(min(lim, wstate["next_use"] + NSLOT))

        pbank = {"i": 0}

        def proj_fm(slot, rhs_fn, nk=16, bankset=(0, 1)):
            bi = bankset[pbank["i"] % len(bankset)]
            pbank["i"] += 1
            bk = banks[bi]
            for c in range(nk):
                B.op("pe", (lambda e, c=c, bk=bk: e.matmul(bk[:], lhsT=ring[:, slot, c, :], rhs=rhs_fn(c),
                                                           start=(c == 0), stop=(c == nk - 1))),
                     r=["ring%d" % slot, "hT"], w=["bank%d" % bi])
            return bi

        out_toks = []

        for T in range(NT_RUN):
            t0 = T * TT
            B.op("sp", lambda e, t0=t0: e.dma_start(
                out=XO[:], in_=x_d[t0:t0 + TT, :].rearrange("(t p) d -> p t d", p=128)), w=["XO"], dma="x")
            for tb in range(4):
                B.op("act", lambda e, tb=tb: e.activation(out=hb[:], in_=XO[:, tb, :], func=AF.Square,
                                                          accum_out=st1[:, 0:1]), r=["XO"], w=["hb", "st1"])
                B.op("dve", lambda e: e.tensor_scalar(out=st1[:, 1:2], in0=st1[:, 0:1], scalar1=1.0 / D, scalar2=EPS,
                                                      op0=ALU.mult, op1=ALU.add), r=["st1"], w=["st1b"])
                B.op("act", lambda e: e.activation(out=st1[:, 2:3], in_=st1[:, 1:2], func=AF.Sqrt),
                     r=["st1b"], w=["st1c"])
                B.op("dve", lambda e: e.reciprocal(out=st1[:, 3:4], in_=st1[:, 2:3]), r=["st1c"], w=["st1d"])
                B.op("dve", lambda e, tb=tb: e.tensor_scalar(out=hb[:], in0=XO[:, tb, :], scalar1=st1[:, 3:4],
                                                             scalar2=None, op0=ALU.mult),
                     r=["XO", "st1d"], w=["hb"])
                for c4 in range(4):
                    bi = 2 + (c4 % 2)
                    bkb = banks[bi][:].bitcast(BF16)
                    for cc in range(4):
                        c = c4 * 4 + cc
                        B.op("pe", lambda e, c=c, cc=cc, bkb=bkb: e.transpose(
                            out=bkb[:, cc * 128:(cc + 1) * 128], in_=hb[:, c * 128:(c + 1) * 128], identity=cb("ident")),
                            r=["hb", "CB"], w=["bank%d" % bi])
                    for cc in range(4):
                        c = c4 * 4 + cc
                        B.op("dve", lambda e, c=c, cc=cc, bkb=bkb, tb=tb: e.tensor_scalar(
                            out=hT[:, c, tb * 128:(tb + 1) * 128], in0=bkb[:, cc * 128:(cc + 1) * 128],
                            scalar1=pr("nw", c), scalar2=None, op0=ALU.mult),
                            r=["bank%d" % bi, "PR"], w=["hT"])

            if T > 0:
                B.op("pool", lambda e: e.tensor_copy(out=kT2[:, :, 0:128], in_=kT2[:, :, TT:TT + 128]),
                     r=["kT2"], w=["kT2"])
                B.op("pool", lambda e: e.tensor_copy(out=vtok[:, 0, :], in_=vtok[:, 4, :]), r=["vtok"], w=["vtok"])
            sv = [w_next(), w_next()]
            for tb in range(4):
                bi = 2 + (tb % 2)
                for half in range(2):
                    o0 = half * 128
                    B.op("pe", lambda e, bi=bi, o0=o0: e.matmul(
                        banks[bi][:, o0:o0 + 128], lhsT=CB[0:1, CB_OFF["onesrow"][0]:CB_OFF["onesrow"][0] + 128],
                        rhs=CB[0:1, CB_OFF["bv"][0] + o0:CB_OFF["bv"][0] + o0 + 128], start=True, stop=False),
                        r=["CB"], w=["bank%d" % bi])
                    for c in range(16):
                        B.op("pe", lambda e, bi=bi, o0=o0, c=c, tb=tb, sl=sv[half]: e.matmul(
                            banks[bi][:, o0:o0 + 128], lhsT=hT[:, c, tb * 128:(tb + 1) * 128], rhs=ring[:, sl, c, :],
                            start=False, stop=(c == 15)), r=["hT", "ring%d" % sv[half]], w=["bank%d" % bi])
                B.op("act", lambda e, bi=bi, tb=tb: e.copy(out=vtok[:, 1 + tb, :], in_=banks[bi][:, 0:256]),
                     r=["bank%d" % bi], w=["vtok"])
            w_prefetch()

            for g in range(4):
                sl = w_next()
                bi = proj_fm(sl, lambda c: hT[:, c, :])
                B.op("dve", lambda e, bi=bi, g=g: e.tensor_scalar(
                    out=kT2[:, g, 128:128 + TT], in0=banks[bi][:], scalar1=pr("bk", g), scalar2=None, op0=ALU.add),
                    r=["bank%d" % bi, "PR"], w=["kT2"])
                w_prefetch()
                for j in range(4):
                    sl = w_next()
                    bi = proj_fm(sl, lambda c: hT[:, c, :])
                    B.op("dve", lambda e, bi=bi, g=g, j=j: e.tensor_scalar(
                        out=qT[:, j, :], in0=banks[bi][:], scalar1=pr("bq", 4 * g + j), scalar2=0.125,
                        op0=ALU.add, op1=ALU.mult), r=["bank%d" % bi, "PR"], w=["qT"])
                    w_prefetch()
                for j in range(4):
                    sl = w_next()
                    bi = proj_fm(sl, lambda c: hT[:, c, :])
                    B.op("act", lambda e, bi=bi, j=j: e.activation(out=zT[:, j, :], in_=banks[bi][:], func=AF.Silu),
                         r=["bank%d" % bi], w=["zT"])
                    w_prefetch()
                for qb in range(4):
                    nglob = 4 * T + qb
                    kinds = ["own"] + (["prev"] if nglob > 0 else [])
                    for half in range(2):
                        pp = slice(half * 64, half * 64 + 64)
                        for ki, kind in enumerate(kinds):
                            bi = 2 + half * 2 + ki
                            kcol = 128 + qb * 128 if kind == "own" else qb * 128
                            B.op("pe", lambda e, bi=bi, pp=pp, kcol=kcol, g=g, qb=qb: e.matmul(
                                banks[bi][:], lhsT=kT2[pp, g, kcol:kcol + 128],
                                rhs=qT[pp, :, qb * 128:(qb + 1) * 128], start=True, stop=True),
                                r=["kT2", "qT"], w=["bank%d" % bi])
                            pi = half * 2 + ki
                            B.op("act", lambda e, bi=bi, pi=pi: e.activation(
                                out=pT[:, pi, :], in_=banks[bi][:], func=AF.Exp), r=["bank%d" % bi], w=["pT%d" % pi])
                            mk = cb("mown") if kind == "own" else cb("mprev")
                            mk3 = AP(mk.tensor, mk.offset, [list(mk.ap[0]), [0, 4], [1, 128]])
                            B.op("pool", lambda e, pi=pi, mk3=mk3: e.tensor_tensor(
                                out=pT[:, pi, :].rearrange("p (h q) -> p h q", h=4),
                                in0=pT[:, pi, :].rearrange("p (h q) -> p h q", h=4), in1=mk3, op=ALU.mult),
                                r=["pT%d" % pi, "CB"], w=["pT%d" % pi])
                    for half in range(2):
                        po = slice(half * 64, half * 64 + 64)
                        for ki, kind in enumerate(kinds):
                            pi = half * 2 + ki
                            vb_ = 1 + qb if kind == "own" else qb
                            B.op("pe", lambda e, po=po, pi=pi, vb_=vb_, g=g, ki=ki, nk=len(kinds): e.matmul(
                                banks[6][po, :], lhsT=vtok[:, vb_, g * 64:(g + 1) * 64], rhs=pT[:, pi, :],
                                start=(ki == 0), stop=(ki == nk - 1)), r=["vtok", "pT%d" % pi], w=["bank6"])
                        for ki, kind in enumerate(kinds):
                            pi = half * 2 + ki
                            B.op("pe", lambda e, po=po, pi=pi, ki=ki, nk=len(kinds): e.matmul(
                                banks[7][po, :], lhsT=cb("ones64"), rhs=pT[:, pi, :],
                                start=(ki == 0), stop=(ki == nk - 1)), r=["CB", "pT%d" % pi], w=["bank7"])
                    es3 = AP(ES, 4 * g, [[16, 128], [1, 4], [0, 128]])
                    B.op("dve", lambda e, es3=es3: e.tensor_tensor(
                        out=dnb[:].rearrange("p (h q) -> p h q", h=4),
                        in0=banks[7][:].rearrange("p (h q) -> p h q", h=4), in1=es3, op=ALU.add),
                        r=["bank7", "ES"], w=["dnb"])
                    B.op("dve", lambda e: e.reciprocal(out=dnb[:], in_=dnb[:]), r=["dnb"], w=["dnb"])
                    B.op("dve", lambda e: e.tensor_tensor(out=ogf[:], in0=banks[6][:], in1=dnb[:], op=ALU.mult),
                         r=["bank6", "dnb"], w=["ogf"])
                    B.op("dve", lambda e, g=g, qb=qb: e.tensor_tensor(
                        out=XOb_view3(4 * g, 4, qb * 128, 128), in0=ogf[:].rearrange("p (h q) -> p h q", h=4),
                        in1=zT[:, :, qb * 128:(qb + 1) * 128], op=ALU.mult), r=["ogf", "zT"], w=["XO"])

            for d in range(16):
                sg = w_next()
                big = proj_fm(sg, lambda c: hT[:, c, :], bankset=(0, 1))
                gi = 0
                B.op("act", lambda e, big=big, gi=gi: e.activation(out=gsb[:, gi, :], in_=banks[big][:],
                                                                   func=AF.Sigmoid),
                     r=["bank%d" % big], w=["gsb%d" % gi])
                w_prefetch()
                sw = w_next()
                bia = 2 + (d % 2)
                for c in range(16):
                    B.op("pe", lambda e, c=c, bia=bia, sw=sw: e.matmul(
                        banks[bia][:], lhsT=ring[:, sw, c, :], rhs=XOb_view(c, 0, TT), start=(c == 0), stop=(c == 15)),
                        r=["ring%d" % sw, "XO"], w=["bank%d" % bia])
                B.op("dve", lambda e, bia=bia, gi=gi, d=d: e.tensor_tensor(
                    out=m1T[:, d, :], in0=banks[bia][:], in1=gsb[:, gi, :], op=ALU.mult),
                    r=["bank%d" % bia, "gsb%d" % gi], w=["m1T"])
                w_prefetch()

            if ENABLE_DN:
                DK_SCALE = 128.0 ** -0.5

                def pview(base, dims):
                    return AP(base.tensor, base.offset, [list(base.ap[0])] + [list(d_) for d_ in dims])

                sba = w_next()
                for kc in range(16):
                    B.op("pe", lambda e, kc=kc, sba=sba: e.matmul(
                        banks[2][0:64, :], lhsT=ring[:, sba, kc, 0:64], rhs=hT[:, kc, :],
                        start=(kc == 0), stop=(kc == 15)), r=["hT", "ring%d" % sba], w=["bank2"])
                w_prefetch()
                B.op("act", lambda e: e.copy(out=ACC[0][0:64, :], in_=banks[2][0:64, :]), r=["bank2"], w=["ACC0"])
                B.mute = DN_SUB < 2
                for c in range(8):
                    for h in range(2):
                        B.op("pe", lambda e, c=c, h=h: e.matmul(
                            banks[4][h * 64:(h + 1) * 64, c * 64:(c + 1) * 64], lhsT=ACC[0][0:64, c * 64:(c + 1) * 64],
                            rhs=CF[0:64, CF_OFF["ident"][0]:CF_OFF["ident"][0] + 64], start=True, stop=True),
                            r=["ACC0", "CF"], w=["bank4"])
                B.mute = DN_SUB < 3
                for h in range(2):
                    hp = slice(h * 64, (h + 1) * 64)
                    B.op("act", lambda e, h=h, hp=hp: e.activation(
                        out=BET[hp, :, :], in_=pview(banks[4][hp, h:h + 1], [[64, 8], [2, 16]]), func=AF.Sigmoid),
                        r=["bank4"], w=["BET"])
                    B.op("dve", lambda e, h=h, hp=hp: e.tensor_tensor(
                        out=GST[hp, :, :], in0=pview(banks[4][hp, 32 + h:33 + h], [[64, 8], [2, 16]]),
                        in1=pview(PR[hp, PR_OFF["dtb"][0]:PR_OFF["dtb"][0] + 1], [[0, 8], [1, 16]]), op=ALU.add),
                        r=["bank4", "PR"], w=["GST"])
                B.mute = DN_SUB < 4
                B.op("act", lambda e: e.activation(out=GST[:], in_=GST[:], func=AF.Exp), r=["GST"], w=["GST"])
                B.op("dve", lambda e: e.tensor_scalar(out=GST[:], in0=GST[:], scalar1=1.0, scalar2=None, op0=ALU.add),
                     r=["GST"], w=["GST"])
                B.op("act", lambda e: e.activation(out=GST[:], in_=GST[:], func=AF.Ln), r=["GST"], w=["GST"])
                B.op("dve", lambda e: e.tensor_tensor(
                    out=GST[:], in0=GST[:], in1=pview(NEGA[:, 0:1], [[0, 8], [1, 16]]), op=ALU.mult),
                    r=["GST", "NEGA"], w=["GST"])
                B.mute = DN_SUB < 5
                gflat = GST[:].rearrange("p c k -> p (c k)")
                for qi, nm in enumerate(["bdu", "bdsu", "hsel0", "hsel1"]):
                    B.op("pe", lambda e, qi=qi, nm=nm: e.matmul(
                        banks[3][:, qi * 128:(qi + 1) * 128], lhsT=cf(nm), rhs=gflat, start=True, stop=True),
                        r=["CF", "GST"], w=["bank3"])
                B.op("act", lambda e: e.activation(out=EX[:].rearrange("p a c k -> p (a c k)"), in_=banks[3][:],
                                                   func=AF.Exp), r=["bank3"], w=["EX"])

                B.mute = DN_SUB < 6
                for kh in range(16):
                    specs = [("k", 16 + kh, XQK[:, 0, :], "XQK"), ("q", kh, XQK[:, 1, :], "XQK")]
                    order = [("q", kh, XQK[:, 1, :], "XQKq"), ("k", 16 + kh, XQK[:, 0, :], "XQKk"),
                             ("v0", 32 + 2 * kh, VT[0][:], "VT0"), ("v1", 33 + 2 * kh, VT[1][:], "VT1")]
                    for oi, (nm, cblk, dst, dkey) in enumerate(order):
                        sl = w_next()
                        bi = proj_fm(sl, lambda c: hT[:, c, :])
                        ci = oi % 2
                        B.op("act", lambda e, bi=bi, ci=ci: e.copy(out=CS[ci][:, 3:515], in_=banks[bi][:]),
                             r=["bank%d" % bi], w=["CSm%d" % ci])
                        B.op("dve", lambda e, ci=ci, cblk=cblk: e.tensor_copy(out=CS[ci][:, 0:3], in_=HALO[:, cblk, :]),
                             r=["HALO%d" % cblk], w=["CSh%d" % ci])
                        B.op("dve", lambda e, ci=ci, cblk=cblk: e.tensor_copy(out=HALO[:, cblk, :], in_=CS[ci][:, 512:515]),
                             r=["CSm%d" % ci], w=["HALO%d" % cblk])
                        w_prefetch()
                        B.op("dve", lambda e, ci=ci, cblk=cblk: e.tensor_scalar(
                            out=ACC[ci][:], in0=CS[ci][:, 0:512], scalar1=pr("cw", cblk * 4 + 0), scalar2=None,
                            op0=ALU.mult), r=["CSm%d" % ci, "CSh%d" % ci, "PR"], w=["ACC%d" % ci])
                        for ti in range(1, 4):
                            B.op("dve", lambda e, ci=ci, cblk=cblk, ti=ti: e.scalar_tensor_tensor(
                                out=ACC[ci][:], in0=CS[ci][:, ti:ti + 512], scalar=pr("cw", cblk * 4 + ti),
                                in1=ACC[ci][:], op0=ALU.mult, op1=ALU.add),
                                r=["CSm%d" % ci, "CSh%d" % ci, "PR", "ACC%d" % ci], w=["ACC%d" % ci])
                        B.op("act", lambda e, ci=ci, dst=dst: e.activation(out=dst, in_=ACC[ci][:], func=AF.Silu),
                             r=["ACC%d" % ci], w=[dkey])
                    for h in range(2):
                        sl = w_next()
                        bi = proj_fm(sl, lambda c: hT[:, c, :])
                        B.op("act", lambda e, bi=bi, h=h: e.activation(out=ZS[h][:], in_=banks[bi][:], func=AF.Silu),
                             r=["bank%d" % bi], w=["ZS%d" % h])
                        w_prefetch()

                    if DN_LEVEL >= 3:
                        B.op("pool", lambda e: e.tensor_tensor(out=SQ[:], in0=XQK[:], in1=XQK[:], op=ALU.mult),
                             r=["XQKq", "XQKk"], w=["SQ"])
                        for c in range(8):
                            for qi in range(2):
                                for h in range(2):
                                    B.op("pe", lambda e, c=c, qi=qi, h=h: e.matmul(
                                        banks[7][h * 64:(h + 1) * 64, 256 + c * 2 + qi:256 + c * 2 + qi + 1],
                                        lhsT=SQ[:, 1 - qi, c * 64:(c + 1) * 64],
                                        rhs=CB[:, CB_OFF["onescol"][0]:CB_OFF["onescol"][0] + 1], start=True, stop=True),
                                        r=["SQ", "CB"], w=["bank7"])
                        B.op("dve", lambda e: e.tensor_scalar(
                            out=RS[:].rearrange("p c q -> p (c q)"), in0=banks[7][:, 256:272], scalar1=EPS, scalar2=None,
                            op0=ALU.add), r=["bank7"], w=["RS"])
                        B.op("act", lambda e: e.activation(out=RS[:], in_=RS[:], func=AF.Ln), r=["RS"], w=["RS"])
                        B.op("act", lambda e: e.activation(out=RS[:], in_=RS[:], func=AF.Exp, scale=-0.5), r=["RS"], w=["RS"])
                        rq = RS[:, :, 0]
                        rk = RS[:, :, 1]
                        betk = BET[:, :, kh]
                        egck = EX[:, 0, :, kh]
                        C1, CA, C2, CO, COE = (SC[:, i, :] for i in range(5))
                        B.op("dve", lambda e, betk=betk: e.tensor_tensor(out=C1, in0=betk, in1=rk, op=ALU.mult),
                             r=["BET", "RS"], w=["SC0"])
                        B.op("dve", lambda e: e.tensor_tensor(out=CA, in0=C1, in1=rk, op=ALU.mult),
                             r=["SC0", "RS"], w=["SC1"])
                        B.op("dve", lambda e, egck=egck: e.scalar_tensor_tensor(out=C2, in0=CA, scalar=-1.0, in1=egck,
                                                                     op0=ALU.mult, op1=ALU.mult),
                             r=["SC1", "EX"], w=["SC2"])
                        B.op("dve", lambda e: e.tensor_scalar(out=CO, in0=rq, scalar1=DK_SCALE, scalar2=None, op0=ALU.mult),
                             r=["RS"], w=["SC3"])
                        B.op("dve", lambda e, egck=egck: e.tensor_tensor(out=COE, in0=CO, in1=egck, op=ALU.mult),
                             r=["SC3", "EX"], w=["SC4"])

                    if DN_LEVEL >= 4:
                        for c4 in range(2):
                            bi = 4 + c4
                            for cc in range(4):
                                c = c4 * 4 + cc
                                co_ = cc * 128
                                for h in range(2):
                                    B.op("pe", lambda e, c=c, bi=bi, co_=co_, h=h: e.matmul(
                                        banks[bi][h * 64:(h + 1) * 64, co_:co_ + 128], lhsT=XQK[:, 0, c * 64:(c + 1) * 64],
                                        rhs=cb("ident"), start=True, stop=True), r=["XQKk", "CB"], w=["bank%d" % bi])
                            for cc in range(4):
                                c = c4 * 4 + cc
                                co_ = cc * 128
                                B.op("dve", lambda e, c=c, bi=bi, co_=co_, kh=kh: e.tensor_scalar(
                                    out=XKD[:, c, :], in0=banks[bi][:, co_:co_ + 128], scalar1=EX[:, 1, c, kh:kh + 1],
                                    scalar2=None, op0=ALU.mult), r=["bank%d" % bi, "EX"], w=["XKD"])
                        for c4 in range(2):
                            bi = 4 + c4
                            for cc in range(4):
                                c = c4 * 4 + cc
                                co_ = cc * 128
                                for h in range(2):
                                    B.op("pe", lambda e, c=c, bi=bi, co_=co_, h=h: e.matmul(
                                        banks[bi][h * 64:(h + 1) * 64, co_:co_ + 128], lhsT=VT[h][:, c * 64:(c + 1) * 64],
                                        rhs=cb("ident"), start=True, stop=True), r=["VT%d" % h, "CB"], w=["bank%d" % bi])
                            for cc in range(4):
                                c = c4 * 4 + cc
                                co_ = cc * 128
                                B.op("dve", lambda e, c=c, bi=bi, co_=co_: e.tensor_scalar(
                                    out=VB[:, c, :], in0=banks[bi][:, co_:co_ + 128], scalar1=SC[:, 0, c:c + 1],
                                    scalar2=None, op0=ALU.mult), r=["bank%d" % bi, "SC0"], w=["VB"])

                    if DN_LEVEL >= 5:
                        KEY_GBD = ["qT"]
                        KEY_DSS = ["zT"]
                        KEY_Y32 = ["pT0", "pT1", "pT2", "pT3"]
                        gb = pview(GST[:, 0, kh:kh + 1], [[16, 8]])
                        B.op("dve", lambda e, gb=gb: e.tensor_copy(out=GHb[:], in_=gb), r=["GST"], w=["GHb"])
                        B.op("dve", lambda e, gb=gb: e.tensor_tensor(out=GHL[:, 1, :], in0=gb, in1=GHb[:], op=ALU.subtract),
                             r=["GST", "GHb"], w=["GHL"])
                        for (dst, dkey, cname, width, src, skey) in [
                                (GP8h[:], "GP8h", "u2", 64, GHb, "GHb"), (GP8l[:], "GP8l", "u2", 64, None, "GHL"),
                                (GBD8h.ap(), "qT", "bdu", 128, GHb, "GHb"), (GBD8l.ap(), "qT", "bdu", 128, None, "GHL")]:
                            if src is not None:
                                in1 = pview(GHb[:, 0:1], [[1, 8], [0, width]])
                            else:
                                in1 = pview(GHL[:, 1, 0:1], [[1, 8], [0, width]])
                            B.op("dve", lambda e, dst=dst, cname=cname, width=width, in1=in1: e.tensor_tensor(
                                out=dst, in0=pview(cf(cname)[:, 0:1], [[0, 8], [1, width]]), in1=in1, op=ALU.mult),
                                r=["CF", skey], w=[dkey])
                        for c in range(8):
                            bi = 2 + c // 4
                            co_ = (c % 4) * 128
                            for h in range(2):
                                B.op("pe", lambda e, c=c, h=h, bi=bi, co_=co_: e.matmul(
                                    banks[bi][h * 64:(h + 1) * 64, co_:co_ + 128], lhsT=XQK[:, 0, c * 64:(c + 1) * 64],
                                    rhs=XQK[:, :, c * 64:(c + 1) * 64], start=True, stop=True),
                                    r=["XQKk", "XQKq"], w=["bank%d" % bi])
                        for (bi, cconst, cmask, cones) in [(4, "bdneg", "masks", "ones64"), (5, "bdones", "maskt", "neg64")]:
                            bk_ = "bank%d" % bi
                            B.op("pe", lambda e, bi=bi, cconst=cconst: e.matmul(
                                banks[bi][:], lhsT=cb(cconst), rhs=GP8h[:], start=True, stop=False, skip_group_check=True),
                                r=["CB", "GP8h"], w=[bk_])
                            B.op("pe", lambda e, bi=bi, cconst=cconst: e.matmul(
                                banks[bi][:], lhsT=cb(cconst), rhs=GP8l[:], start=False, stop=False, skip_group_check=True),
                                r=["CB", "GP8l"], w=[bk_])
                            B.op("pe", lambda e, bi=bi, cmask=cmask: e.matmul(
                                banks[bi][:], lhsT=cb("ident"), rhs=pview(cb(cmask)[:, 0:1], [[0, 8], [1, 64]]),
                                start=False, stop=False, skip_group_check=True), r=["CB"], w=[bk_])
                            for c in range(8):
                                B.op("pe", lambda e, bi=bi, c=c, cones=cones: e.matmul(
                                    banks[bi][:, c * 64:(c + 1) * 64], lhsT=GBD8h.ap(c0=c, c1=c + 1), rhs=cb(cones),
                                    start=False, stop=False, skip_group_check=True), r=["CB", "qT"], w=[bk_])
                                B.op("pe", lambda e, bi=bi, c=c, cones=cones: e.matmul(
                                    banks[bi][:, c * 64:(c + 1) * 64], lhsT=GBD8l.ap(c0=c, c1=c + 1), rhs=cb(cones),
                                    start=False, stop=(c == 7), skip_group_check=True), r=["CB", "qT"], w=[bk_])
                        B.op("act", lambda e: e.activation(out=DSSa.ap(), in_=banks[4][:].rearrange("p (c j) -> p c j", c=8),
                                                           func=AF.Exp), r=["bank4"], w=KEY_DSS)
                        B.op("act", lambda e: e.activation(out=DSSb.ap(), in_=banks[5][:].rearrange("p (c j) -> p c j", c=8),
                                                           func=AF.Exp), r=["bank5"], w=KEY_DSS)
                        B.op("dve", lambda e: e.tensor_tensor(
                            out=DSSa.ap(), in0=DSSa.ap(), in1=pview(SC[:, 1, 0:1], [[1, 8], [0, 64]]), op=ALU.mult),
                            r=KEY_DSS + ["SC1"], w=KEY_DSS)
                        for b2 in range(2):
                            for h in range(2):
                                hp = slice(h * 64, (h + 1) * 64)
                                B.op("dve", lambda e, b2=b2, h=h, hp=hp: e.tensor_tensor(
                                    out=P0B8[hp, 4 * b2:4 * b2 + 4, h * 64:(h + 1) * 64],
                                    in0=pview(banks[2 + b2][hp, 0:1], [[128, 4], [1, 64]]),
                                    in1=DSSa.ap(p0=h * 64, p1=(h + 1) * 64, c0=4 * b2, c1=4 * b2 + 4),
                                    op=ALU.mult), r=["bank%d" % (2 + b2)] + KEY_DSS, w=["P0B8"])
                                B.op("dve", lambda e, b2=b2, h=h, hp=hp: e.tensor_tensor(
                                    out=MQ[hp, 4 * b2:4 * b2 + 4, h * 64:(h + 1) * 64],
                                    in0=pview(banks[2 + b2][hp, 64:65], [[128, 4], [1, 64]]),
                                    in1=DSSb.ap(p0=h * 64, p1=(h + 1) * 64, c0=4 * b2, c1=4 * b2 + 4),
                                    op=ALU.mult), r=["bank%d" % (2 + b2)] + KEY_DSS, w=["MQ"])
                        for c in range(8):
                            bi = 6 + c // 4
                            co_ = (c % 4) * 128
                            B.op("pe", lambda e, c=c, bi=bi, co_=co_: e.matmul(
                                banks[bi][:, co_:co_ + 128], lhsT=P0B8[:, c, :], rhs=cb("ident"), start=True, stop=True),
                                r=["P0B8", "CB"], w=["bank%d" % bi])
                        for b2 in range(2):
                            B.op("act", lambda e, b2=b2: e.copy(
                                out=Q0B8[:, 4 * b2:4 * b2 + 4, :], in_=banks[6 + b2][:].rearrange("p (c j) -> p c j", c=4)),
                                r=["bank%d" % (6 + b2)], w=["Q0B8"])
                            B.op("dve", lambda e, b2=b2: e.tensor_tensor(
                                out=Y32_8.ap(c0=4 * b2, c1=4 * b2 + 4), in0=pview(cf("ident")[:, 0:1], [[0, 4], [1, 128]]),
                                in1=banks[6 + b2][:].rearrange("p (c j) -> p c j", c=4), op=ALU.subtract),
                                r=["bank%d" % (6 + b2), "CF"], w=KEY_Y32)
                        B.op("act", lambda e: e.copy(out=Yb8[:], in_=Y32_8.ap()), r=KEY_Y32, w=["Yb8"])
                        def PQk(k):
                            if k == 0:
                                return (lambda c: P0B8[:, c, :]), (lambda c: Q0B8[:, c, :]), ["P0B8", "Q0B8"]
                            t_ = PQ8[k % 2]
                            return (lambda c: t_[:, 0, c, :]), (lambda c: t_[:, 1, c, :]), ["PQ8_%d" % (k % 2)]

                        def squaring(k):
                            Pk, Qk, pk_keys = PQk(k)
                            nxt = PQ8[(k + 1) % 2]
                            nk_ = "PQ8_%d" % ((k + 1) % 2)
                            for c in range(8):
                                bi = 2 + c // 4
                                co_ = (c % 4) * 128
                                B.op("pe", lambda e, c=c, bi=bi, co_=co_: e.matmul(
                                    banks[bi][:, co_:co_ + 128], lhsT=Qk(c), rhs=Pk(c), start=True, stop=True),
                                    r=pk_keys, w=["bank%d" % bi])
                            if k < 4:
                                for c in range(8):
                                    bi = 4 + c // 4
                                    co_ = (c % 4) * 128
                                    B.op("pe", lambda e, c=c, bi=bi, co_=co_: e.matmul(
                                        banks[bi][:, co_:co_ + 128], lhsT=Pk(c), rhs=Qk(c), start=True, stop=True),
                                        r=pk_keys, w=["bank%d" % bi])
                            for b2 in range(2):
                                B.op("act", lambda e, b2=b2: e.copy(
                                    out=nxt[:, 0, 4 * b2:4 * b2 + 4, :],
                                    in_=banks[2 + b2][:].rearrange("p (c j) -> p c j", c=4)),
                                    r=["bank%d" % (2 + b2)], w=[nk_])
                            if k < 4:
                                for b2 in range(2):
                                    B.op("dve", lambda e, b2=b2: e.tensor_copy(
                                        out=nxt[:, 1, 4 * b2:4 * b2 + 4, :],
                                        in_=banks[4 + b2][:].rearrange("p (c j) -> p c j", c=4)),
                                        r=["bank%d" % (4 + b2)], w=[nk_])

                        def apply(k):
                            nxt = PQ8[(k + 1) % 2]
                            nk_ = "PQ8_%d" % ((k + 1) % 2)
                            for c in range(8):
                                bi = 6 + c // 4
                                co_ = (c % 4) * 128
                                B.op("pe", lambda e, c=c, bi=bi, co_=co_: e.matmul(
                                    banks[bi][:, co_:co_ + 128], lhsT=nxt[:, 0, c, :], rhs=Yb8[:, c, :],
                                    start=True, stop=True), r=[nk_, "Yb8"], w=["bank%d" % bi])
                            for b2 in range(2):
                                B.op("dve", lambda e, b2=b2: e.tensor_tensor(
                                    out=Y32_8.ap(c0=4 * b2, c1=4 * b2 + 4),
                                    in0=banks[6 + b2][:].rearrange("p (c j) -> p c j", c=4),
                                    in1=Y32_8.ap(c0=4 * b2, c1=4 * b2 + 4), op=ALU.add),
                                    r=["bank%d" % (6 + b2)] + KEY_Y32, w=KEY_Y32)
                            if k < 4:
                                B.op("act", lambda e: e.copy(out=Yb8[:], in_=Y32_8.ap()), r=KEY_Y32, w=["Yb8"])
                            else:
                                B.op("act", lambda e: e.copy(out=YT[:], in_=Y32_8.ap()), r=KEY_Y32, w=["YT"])

                        squaring(0)
                        for k in range(1, 5):
                            squaring(k)
                            apply(k - 1)
                        apply(4)
                    if DN_LEVEL >= 6:
                        for c in range(8):
                            for h in range(2):
                                hp = slice(h * 64, (h + 1) * 64)
                                B.op("pe", lambda e, c=c, h=h, hp=hp, kh=kh: e.matmul(
                                    banks[6][hp, 0:128], lhsT=XQK[:, 0, c * 64:(c + 1) * 64],
                                    rhs=S2b[:, kh, h * 128:(h + 1) * 128], start=True, stop=True),
                                    r=["XQKk", "S2b"], w=["bank6"])
                                B.op("pe", lambda e, c=c, h=h, hp=hp, kh=kh: e.matmul(
                                    banks[6][hp, 128:256], lhsT=XQK[:, 1, c * 64:(c + 1) * 64],
                                    rhs=S2b[:, kh, h * 128:(h + 1) * 128], start=True, stop=True),
                                    r=["XQKq", "S2b"], w=["bank6"])
                            B.op("dve", lambda e, c=c: e.scalar_tensor_tensor(
                                out=RB[:], in0=banks[6][:, 0:128], scalar=SC[:, 2, c:c + 1], in1=VB[:, c, :],
                                op0=ALU.mult, op1=ALU.add), r=["bank6", "SC2", "VB"], w=["RB"])
                            B.op("pe", lambda e, c=c: e.matmul(banks[5][:, 0:128], lhsT=YT[:, c, :], rhs=RB[:],
                                                               start=True, stop=True), r=["YT", "RB"], w=["bank5"])
                            B.op("act", lambda e: e.copy(out=VPB[:], in_=banks[5][:, 0:128]), r=["bank5"], w=["VPB"])
                            B.op("pe", lambda e, c=c: e.matmul(banks[7][:, 0:128], lhsT=MQ[:, c, :], rhs=VPB[:],
                                                               start=True, stop=True), r=["MQ", "VPB"], w=["bank7"])
                            B.op("act", lambda e, c=c: e.activation(out=T1[:], in_=banks[6][:, 128:256], func=AF.Identity,
                                                                    scale=SC[:, 4, c:c + 1]),
                                 r=["bank6", "SC4"], w=["T1"])
                            B.op("dve", lambda e, c=c: e.scalar_tensor_tensor(
                                out=OB[:, c, :], in0=banks[7][:, 0:128], scalar=SC[:, 3, c:c + 1], in1=T1[:],
                                op0=ALU.mult, op1=ALU.add), r=["bank7", "SC3", "T1"], w=["OB"])
                            sbk = [banks[3][:, 0:128], banks[4][:, 0:128]]
                            sbk_key = ["bank3", "bank4"]
                            for h in range(2):
                                hp = slice(h * 64, (h + 1) * 64)
                                B.op("pe", lambda e, c=c, h=h, hp=hp, sbk=sbk: e.matmul(
                                    sbk[h], lhsT=XKD[hp, c, :], rhs=VPB[hp, :],
                                    start=True, stop=True), r=["XKD", "VPB"], w=[sbk_key[h]])
                            for h in range(2):
                                B.op("dve", lambda e, c=c, h=h, kh=kh, sbk=sbk: e.scalar_tensor_tensor(
                                    out=S2b[:, kh, h * 128:(h + 1) * 128], in0=S2[:, kh, h * 128:(h + 1) * 128],
                                    scalar=EX[:, 2 + h, c, kh:kh + 1], in1=sbk[h],
                                    op0=ALU.mult, op1=ALU.add), r=["S2", "EX", sbk_key[h]], w=["S2b"])
                            for h in range(2):
                                B.op("dve", lambda e, c=c, h=h, kh=kh, sbk=sbk: e.scalar_tensor_tensor(
                                    out=S2[:, kh, h * 128:(h + 1) * 128], in0=S2[:, kh, h * 128:(h + 1) * 128],
                                    scalar=EX[:, 2 + h, c, kh:kh + 1], in1=sbk[h],
                                    op0=ALU.mult, op1=ALU.add), r=["S2", "EX", sbk_key[h]], w=["S2"])
                            B.op("act", lambda e, c=c: e.activation(out=JK[:], in_=OB[:, c, :], func=AF.Square,
                                                                    accum_out=SSO[:, c, 0:1]), r=["OB"], w=["JK", "SSO"])

                    if DN_LEVEL >= 7:
                        B.op("dve", lambda e: e.tensor_scalar(out=SSO[:, :, 1], in0=SSO[:, :, 0], scalar1=1.0 / 128.0,
                                                              scalar2=EPS, op0=ALU.mult, op1=ALU.add),
                             r=["SSO"], w=["SSO1"])
                        B.op("act", lambda e: e.activation(out=SSO[:, :, 2], in_=SSO[:, :, 1], func=AF.Ln),
                             r=["SSO1"], w=["SSO2"])
                        B.op("act", lambda e: e.activation(out=SSO[:, :, 3], in_=SSO[:, :, 2], func=AF.Exp, scale=-0.5),
                             r=["SSO2"], w=["SSO3"])
                        B.op("dve", lambda e: e.tensor_tensor(
                            out=ON[:], in0=OB[:], in1=pview(SSO[:, 0, 3:4], [[4, 8], [0, 128]]), op=ALU.mult),
                            r=["OB", "SSO3"], w=["ON"])
                        for c4 in range(2):
                            bi = 4 + c4
                            bkb = banks[bi][:].bitcast(BF16)
                            for cc in range(4):
                                c = c4 * 4 + cc
                                B.op("pe", lambda e, c=c, cc=cc, bkb=bkb: e.transpose(
                                    out=bkb[:, cc * 128:(cc + 1) * 128], in_=ON[:, c, :], identity=cb("ident")),
                                    r=["ON", "CB"], w=["bank%d" % bi])
                            for h in range(2):
                                B.op("dve", lambda e, bkb=bkb, h=h, c4=c4, kh=kh: e.scalar_tensor_tensor(
                                    out=XOb_view(2 * kh + h, c4 * 256, 256).rearrange("p (c i) -> p c i", c=4),
                                    in0=pview(bkb[:, h * 64:h * 64 + 1], [[128, 4], [1, 64]]), scalar=pr("dnw", 0),
                                    in1=ZS[h][:, c4 * 256:(c4 + 1) * 256].rearrange("p (c i) -> p c i", c=4),
                                    op0=ALU.mult, op1=ALU.mult), r=["bank%d" % bi, "PR", "ZS%d" % h], w=["XO"])

                B.mute = DN_SUB < 7
                for d in range(16):
                    sg = w_next()
                    big = proj_fm(sg, lambda c: hT[:, c, :], bankset=(0, 1))
                    gi = 0
                    B.op("act", lambda e, big=big, gi=gi: e.activation(out=gsb[:, gi, :], in_=banks[big][:],
                                                                       func=AF.Sigmoid),
                         r=["bank%d" % big], w=["gsb%d" % gi])
                    w_prefetch()
                    s1 = w_next()
                    s2 = w_next()
                    bia = 2 + (d % 2)
                    for c in range(32):
                        sl_ = s1 if c < 16 else s2
                        B.op("pe", lambda e, c=c, bia=bia, sl_=sl_: e.matmul(
                            banks[bia][:], lhsT=ring[:, sl_, c % 16, :], rhs=XOb_view(c, 0, TT),
                            start=(c == 0), stop=(c == 31)), r=["ring%d" % sl_, "XO"], w=["bank%d" % bia])
                    B.op("dve", lambda e, bia=bia, gi=gi: e.tensor_tensor(
                        out=gsb[:, gi, :], in0=banks[bia][:], in1=gsb[:, gi, :], op=ALU.mult),
                        r=["bank%d" % bia, "gsb%d" % gi], w=["gsb%d" % gi])
                    B.op("pool", lambda e, gi=gi, d=d: e.tensor_tensor(
                        out=m1T[:, d, :], in0=gsb[:, gi, :], in1=m1T[:, d, :], op=ALU.add),
                        r=["gsb%d" % gi, "m1T"], w=["m1T"])
                    w_prefetch()
            else:
                for _ in range(N_DN_BLK):
                    w_next()
                    w_prefetch()
            B.mute = False
            while wstate["next_use"] % 4 != 0:
                w_next()
                w_prefetch()

            if SKIP_P3:
                continue
            B.op("sp", lambda e, t0=t0: e.dma_start(
                out=XO[:], in_=x_d[t0:t0 + TT, :].rearrange("(t p) d -> p t d", p=128)), r=[], w=["XO"], dma="x")
            for cg in range(4):
                sls = [w_next() for _ in range(4)]
                assert sls[0] % 4 == 0 and sls == list(range(sls[0], sls[0] + 4))
                for tb in range(4):
                    bi = 2 + ((cg * 4 + tb) % 4)
                    for c in range(16):
                        rhs = AP(ring, ring[:, sls[0], c, :].offset, [list(ring[:].ap[0]), [16 * 128, 4], [1, 128]])
                        B.op("pe", lambda e, bi=bi, c=c, tb=tb, rhs=rhs: e.matmul(
                            banks[bi][:], lhsT=m1T[:, c, tb * 128:(tb + 1) * 128], rhs=rhs,
                            start=(c == 0), stop=(c == 15)),
                            r=["m1T"] + ["ring%d" % s_ for s_ in sls], w=["bank%d" % bi])
                    B.op("dve", lambda e, bi=bi, tb=tb, cg=cg: e.tensor_tensor(
                        out=XO[:, tb, cg * 512:(cg + 1) * 512], in0=banks[bi][:], in1=XO[:, tb, cg * 512:(cg + 1) * 512],
                        op=ALU.add), r=["bank%d" % bi, "XO"], w=["XO"])
                w_prefetch()
            for tb in range(4):
                B.op("act", lambda e, tb=tb: e.activation(out=hb[:], in_=XO[:, tb, :], func=AF.Square,
                                                          accum_out=st1[:, 0:1]), r=["XO"], w=["hb", "st1"])
                B.op("dve", lambda e: e.tensor_scalar(out=st1[:, 1:2], in0=st1[:, 0:1], scalar1=1.0 / D, scalar2=EPS,
                                                      op0=ALU.mult, op1=ALU.add), r=["st1"], w=["st1b"])
                B.op("act", lambda e: e.activation(out=st1[:, 2:3], in_=st1[:, 1:2], func=AF.Sqrt),
                     r=["st1b"], w=["st1c"])
                B.op("dve", lambda e: e.reciprocal(out=st1[:, 3:4], in_=st1[:, 2:3]), r=["st1c"], w=["st1d"])
                B.op("dve", lambda e, tb=tb: e.scalar_tensor_tensor(
                    out=XO[:, tb, :], in0=XO[:, tb, :], scalar=st1[:, 3:4], in1=FW[:], op0=ALU.mult, op1=ALU.mult),
                    r=["XO", "st1d", "FW"], w=["XO"])
                tok = B.op("sp", lambda e, tb=tb, t0=t0: e.dma_start(
                    out=y_d[t0 + tb * 128:t0 + (tb + 1) * 128, :], in_=XO[:, tb, :]), r=["XO"], dma="o%d" % tb)
                out_toks.append(tok)

        B.wait_all("sp", out_toks)
        block = stack.enter_context(nc.Block())
        B.emit(nc, block, stack)
    return nc


_CACHE = {}


def kernel(x, norm_w, w_in, b_qkv, sinks, conv_w, a_log, dt_bias, dn_norm_w,
           w_att_branch, w_dn_branch, w_out, final_norm_w):
    x = np.asarray(x, np.float32)
    wst = build_wstream(np.asarray(w_in[0], np.float32), np.asarray(w_att_branch[0], np.float32),
                        np.asarray(w_dn_branch[0], np.float32), np.asarray(w_out[0], np.float32))
    cfa = consts_f32()
    cba = consts_b16(np.asarray(b_qkv[0], np.float32))
    pra = params_f32(np.asarray(norm_w[0]), np.asarray(b_qkv[0]), np.asarray(sinks[0]), np.asarray(conv_w[0]),
                     np.asarray(a_log[0]), np.asarray(dt_bias[0]), np.asarray(dn_norm_w[0]))
    fwa = np.ascontiguousarray(np.broadcast_to(np.asarray(final_norm_w, np.float32)[None, :], (128, D)))
    n_wblk = wst.shape[0]
    if "nc" not in _CACHE:
        _CACHE["nc"] = build_program(n_wblk)
    nc = _CACHE["nc"]
    in_maps = [{"x": np.ascontiguousarray(x[b]), "wst": wst, "cf": cfa, "cb": cba, "pr": pra, "fw": fwa}
               for b in range(8)]
    res = run_bass_kernel_spmd(nc, in_maps, core_ids=list(range(8)))
    return np.stack([res.results[b]["y"] for b in range(8)], axis=0).astype(np.float32)
```

```python
import contextlib
import numpy as np
import concourse.bass as bass
import concourse.mybir as mybir
from concourse.bass_utils import run_bass_kernel_spmd

F32 = mybir.dt.float32
BF16 = mybir.dt.bfloat16
AF = mybir.ActivationFunctionType
ALU = mybir.AluOpType
AX = mybir.AxisListType

S = 2048
D = 2048
TT = 512
NT = S // TT
P = 128
NSLOT = 8
EPS = 1e-6
OFF_AQ = 0
OFF_AK = 2048
OFF_AV = 2304
OFF_AZ = 2560
OFF_DQKV = 4608
OFF_DZ = 12800
OFF_DB = 16896
OFF_DA = 16928
OFF_G = 16960
NEG = -30000.0
DGE_SCRATCH = 4096

ENABLE_DN = True
NT_RUN = NT
SKIP_P3 = False
DN_LEVEL = 9
DN_SUB = 99
import os as _os
BISV = int(_os.environ.get('BISV', '0'))


class Builder:
    ENG = ["pe", "act", "dve", "pool", "sp"]

    def __init__(self):
        self.ops = {e: [] for e in self.ENG}
        self.lastw = {}
        self.readers = {}
        self.dma_cnt = {}
        self.mute = False

    def op(self, eng, fn, r=(), w=(), dma=None):
        if self.mute and dma is None:
            return None
        deps = set()
        for k in r:
            t = self.lastw.get(k)
            if t is not None:
                deps.add(t)
            if k.startswith("bank"):
                for t in self.readers.get(k, ()):
                    if not (t[0] == "e" and t[1] == eng):
                        deps.add(t)
        for k in w:
            t = self.lastw.get(k)
            if t is not None:
                deps.add(t)
            for t in self.readers.get(k, ()):
                deps.add(t)
        idx = len(self.ops[eng])
        if dma is not None:
            self.dma_cnt[dma] = self.dma_cnt.get(dma, 0) + 1
            tok = ("d", dma, 16 * self.dma_cnt[dma])
        else:
            tok = ("e", eng, idx)
        self.ops[eng].append({"fn": fn, "deps": deps, "dma": dma, "sig": False})
        for k in w:
            self.lastw[k] = tok
            self.readers[k] = []
        for k in r:
            if k in w:
                continue
            lst = self.readers.setdefault(k, [])
            if tok[0] == "e":
                lst[:] = [t for t in lst if not (t[0] == "e" and t[1] == eng)]
            lst.append(tok)
        return tok

    def wait_all(self, eng, toks):
        self.ops[eng].append({"fn": None, "deps": set(toks), "dma": None, "sig": False})

    def emit(self, nc, block, stack):
        for e in self.ENG:
            for rec in self.ops[e]:
                for t in rec["deps"]:
                    if t[0] == "e" and not (t[1] == "pe" and e == "pe"):
                        self.ops[t[1]][t[2]]["sig"] = True
        sem_e = {e: stack.enter_context(nc.semaphore("se_" + e)) for e in self.ENG}
        sem_d = {k: stack.enter_context(nc.semaphore("sd_" + k)) for k in self.dma_cnt}
        for e in self.ENG:
            c = 0
            for rec in self.ops[e]:
                if rec["sig"]:
                    c += 1
                rec["sv"] = c
        ops = self.ops

        def gen(e):
            def run(eh):
                waited = {}
                for rec in ops[e]:
                    need = {}
                    for t in rec["deps"]:
                        if t[0] == "e":
                            if t[1] == "pe" and e == "pe":
                                continue
                            key = ("e", t[1])
                            val = ops[t[1]][t[2]]["sv"]
                        else:
                            key = ("d", t[1])
                            val = t[2]
                        if val > need.get(key, 0):
                            need[key] = val
                    for key, val in need.items():
                        if waited.get(key, 0) >= val:
                            continue
                        waited[key] = val
                        sem = sem_e[key[1]] if key[0] == "e" else sem_d[key[1]]
                        eh.wait_ge(sem, val)
                    if rec["fn"] is None:
                        continue
                    ins = rec["fn"](eh)
                    if rec["dma"] is not None:
                        ins.then_inc(sem_d[rec["dma"]], 16)
                    elif rec["sig"]:
                        ins.then_inc(sem_e[e], 1)
            return run

        block.tensor(gen("pe"))
        block.scalar(gen("act"))
        block.vector(gen("dve"))
        block.gpsimd(gen("pool"))
        block.sync(gen("sp"))


def AP(t, off, dims):
    return bass.AP(t, off, [list(d) for d in dims])


def dup2(ap):
    a = [list(d) for d in ap.ap]
    assert len(a) == 2
    return bass.AP(ap.tensor, ap.offset, [a[0], [0, 2], a[1]])


def wblock(w, r0, cols):
    sub = w[r0:r0 + 2048][:, cols]
    return np.ascontiguousarray(sub.reshape(16, 128, 128).transpose(1, 0, 2))


def build_wstream(w_in, w_att, w_dn, w_out):
    blocks = []
    ar = np.arange

    def cin(c0, n=128):
        return ar(c0, c0 + n)

    blocks.append(wblock(w_in, 0, cin(OFF_AV)))
    blocks.append(wblock(w_in, 0, cin(OFF_AV + 128)))
    for g in range(4):
        kc = cin(OFF_AK + 64 * g, 64)
        blocks.append(wblock(w_in, 0, np.concatenate([kc, kc])))
        for j in range(4):
            blocks.append(wblock(w_in, 0, cin(OFF_AQ + 128 * (4 * g + j))))
        for j in range(4):
            blocks.append(wblock(w_in, 0, cin(OFF_AZ + 128 * (4 * g + j))))
    for d in range(16):
        blocks.append(wblock(w_in, 0, cin(OFF_G + 128 * d)))
        blocks.append(wblock(w_att, 0, cin(128 * d)))
    ba = cin(OFF_DB, 64)
    blocks.append(wblock(w_in, 0, np.concatenate([ba, ba])))
    for kh in range(16):
        blocks.append(wblock(w_in, 0, cin(OFF_DQKV + 128 * kh)))
        blocks.append(wblock(w_in, 0, cin(OFF_DQKV + 2048 + 128 * kh)))
        for h in range(2):
            blocks.append(wblock(w_in, 0, cin(OFF_DQKV + 4096 + 128 * (2 * kh + h))))
        for h in range(2):
            blocks.append(wblock(w_in, 0, cin(OFF_DZ + 128 * (2 * kh + h))))
    for d in range(16):
        blocks.append(wblock(w_in, 0, cin(OFF_G + 2048 + 128 * d)))
        blocks.append(wblock(w_dn, 0, cin(128 * d)))
        blocks.append(wblock(w_dn, 2048, cin(128 * d)))
    while len(blocks) % 4 != 0:
        blocks.append(blocks[-1])
    for cg in range(4):
        for j in range(4):
            blocks.append(wblock(w_out, 0, cin(512 * cg + 128 * j)))
    return np.ascontiguousarray(np.stack(blocks).reshape(len(blocks), 128, 2048))


N_ATT_BLK = 2 + 4 * 9 + 32
N_DN_BLK = 1 + 96 + 48


def consts_f32():
    p = np.arange(128)
    h = p // 64
    t = p % 64
    same = (h[:, None] == h[None, :])
    bdu = (same & (t[:, None] <= t[None, :])).astype(np.float32)
    bdsu = (same & (t[:, None] > t[None, :])).astype(np.float32)
    hsel0 = np.repeat((h == 0)[:, None], 128, 1).astype(np.float32)
    hsel1 = np.repeat((h == 1)[:, None], 128, 1).astype(np.float32)
    bdones = same.astype(np.float32)
    ident = np.eye(128, dtype=np.float32)
    i64 = np.arange(64)
    u2 = (t[:, None] <= i64[None, :]).astype(np.float32)
    ones64 = np.ones((128, 64), np.float32)
    masks = np.where(i64[None, :] < t[:, None], 0.0, NEG).astype(np.float32)
    maskt = np.where(i64[None, :] >= t[:, None], 0.0, NEG).astype(np.float32)
    parts = [ident, bdu, bdsu, hsel0, hsel1, bdones, -bdones, u2, ones64, -ones64, masks, maskt]
    return np.ascontiguousarray(np.concatenate(parts, axis=1))


CF_OFF = {}
_o = 0
for _n, _w in [("ident", 128), ("bdu", 128), ("bdsu", 128), ("hsel0", 128), ("hsel1", 128), ("bdones", 128),
               ("bdneg", 128), ("u2", 64), ("ones64", 64), ("neg64", 64), ("masks", 64), ("maskt", 64)]:
    CF_OFF[_n] = (_o, _w)
    _o += _w
NCF = _o

CB_OFF = {}
_o = 0
for _n, _w in [("ident", 128), ("mown", 128), ("mprev", 128), ("ones64", 64), ("onescol", 2), ("onesrow", 128),
               ("bv", 256), ("bdones", 128), ("bdneg", 128), ("neg64", 64), ("masks", 64), ("maskt", 64)]:
    CB_OFF[_n] = (_o, _w)
    _o += _w
NCB = _o


def consts_b16(b_qkv):
    j = np.arange(128)[:, None]
    i = np.arange(128)[None, :]
    ident = np.eye(128, dtype=np.float32)
    mown = (j <= i).astype(np.float32)
    mprev = (j > i).astype(np.float32)
    ones64 = np.ones((128, 64), np.float32)
    onescol = np.ones((128, 2), np.float32)
    onesrow = np.zeros((128, 128), np.float32)
    onesrow[0, :] = 1.0
    bv = np.zeros((128, 256), np.float32)
    bv[0, :] = b_qkv[2304:2560]
    cfull = consts_f32()
    extra = [cfull[:, CF_OFF[n][0]:CF_OFF[n][0] + CF_OFF[n][1]] for n in ("bdones", "bdneg", "neg64", "masks", "maskt")]
    return np.ascontiguousarray(np.concatenate([ident, mown, mprev, ones64, onescol, onesrow, bv] + extra, axis=1))


PR_OFF = {}
_o = 0
for _n, _w in [("nw", 16), ("bq", 16), ("bk", 4), ("sk", 16), ("cw", 256), ("dtb", 16), ("alg", 16), ("dnw", 1)]:
    PR_OFF[_n] = (_o, _w)
    _o += _w
NPR = _o


def params_f32(norm_w, b_qkv, sinks, conv_w, a_log, dt_bias, dn_norm_w):
    p = np.arange(128)
    hi = (p >= 64).astype(np.int64)
    nw = norm_w.reshape(16, 128).T
    bq = b_qkv[0:2048].reshape(16, 128).T
    bk = b_qkv[2048:2304].reshape(4, 64)[:, p % 64].T
    sk = sinks[(2 * np.arange(16))[None, :] + hi[:, None]]
    cw = conv_w.reshape(4, 64, 128).transpose(2, 1, 0).reshape(128, 256)
    dtb = dt_bias[(2 * np.arange(16))[None, :] + hi[:, None]]
    alg = a_log[(2 * np.arange(16))[None, :] + hi[:, None]]
    dnw = dn_norm_w.reshape(128, 1)
    return np.ascontiguousarray(np.concatenate([nw, bq, bk, sk, cw, dtb, alg, dnw], axis=1).astype(np.float32))


def build_program(n_wblk):
    nc = bass.Bass("TRN2", target_bir_lowering=False, dynamic_dma_scratch_size=DGE_SCRATCH)
    x_d = nc.dram_tensor("x", [S, D], F32, kind="ExternalInput").ap()
    w_d = nc.dram_tensor("wst", [n_wblk, 128, 2048], F32, kind="ExternalInput").ap()
    cf_d = nc.dram_tensor("cf", [128, NCF], F32, kind="ExternalInput").ap()
    cb_d = nc.dram_tensor("cb", [128, NCB], F32, kind="ExternalInput").ap()
    pr_d = nc.dram_tensor("pr", [128, NPR], F32, kind="ExternalInput").ap()
    fw_d = nc.dram_tensor("fw", [128, D], F32, kind="ExternalInput").ap()
    y_d = nc.dram_tensor("y", [S, D], F32, kind="ExternalOutput").ap()

    B = Builder()
    stack = contextlib.ExitStack()
    with stack:
        def sb(name, shape, dt):
            return stack.enter_context(nc.sbuf_tensor(name, shape, dt))

        def ps(name, shape=(128, 512), dt=F32):
            return stack.enter_context(nc.psum_tensor(name, list(shape), dt))

        ring = sb("ring", [128, NSLOT, 16, 128], BF16)
        hT = sb("hT", [128, 16, TT], BF16)
        m1T = sb("m1T", [128, 16, TT], BF16)
        XO = sb("XO", [128, 4, D], F32)
        XOb = XO[:].bitcast(BF16)
        FW = sb("FW", [128, D], F32)
        CF = sb("CF", [128, NCF], F32)
        CB = sb("CB", [128, NCB], BF16)
        PR = sb("PR", [128, NPR], F32)
        hb = sb("hb", [128, D], BF16)
        st1 = sb("st1", [128, 8], F32)
        qT = sb("qT", [128, 4, TT], BF16)
        zT = sb("zT", [128, 4, TT], BF16)
        kT2 = sb("kT2", [128, 4, 128 + TT], BF16)
        vtok = sb("vtok", [128, 5, 256], BF16)
        pT = sb("pT", [128, 4, 512], BF16)
        dnb = sb("dnb", [128, 512], F32)
        ogf = sb("ogf", [128, 512], F32)
        gsb = sb("gsb", [128, 1, 512], F32)
        ES = sb("ES", [128, 16], F32)
        NEGA = sb("NEGA", [128, 16], F32)

        S2 = sb("S2", [128, 16, 256], F32)
        S2b = sb("S2b", [128, 16, 256], BF16)
        HALO = sb("HALO", [128, 64, 4], BF16)
        BET = sb("BET", [128, 8, 16], F32)
        GST = sb("GST", [128, 8, 16], F32)
        EX = sb("EX", [128, 4, 8, 16], F32)
        CSb = [sb("CSb%d" % i, [128, 516], BF16) for i in range(2)]
        DIAG = [sb("DIAG%d" % i, [128, 4, 128], BF16) for i in range(2)]
        ACC = [sb("ACC0", [128, 512], F32)]
        XQKs = [sb("XQK%d" % i, [128, 2, TT], BF16) for i in range(2)]
        VT = [sb("VT%d" % i, [128, TT], BF16) for i in range(2)]
        ZSs = [[sb("ZS%d_%d" % (p_, i), [128, TT], BF16) for i in range(2)] for p_ in range(2)]
        RS = sb("RS", [128, 8, 2], F32)
        SC = sb("SC", [128, 5, 8], F32)
        XKD = sb("XKD", [128, 8, 128], BF16)
        VB = sb("VB", [128, 8, 128], F32)
        GP8h = sb("GP8h", [128, 8, 64], BF16)
        GP8l = sb("GP8l", [128, 8, 64], BF16)
        GHL = sb("GHL", [128, 2, 8], F32)
        GHb = sb("GHb", [128, 8], BF16)
        P0B8 = sb("P0B8", [128, 8, 128], BF16)
        Q0B8 = sb("Q0B8", [128, 8, 128], BF16)
        PQ8 = [sb("PQ8_%d" % i, [128, 2, 8, 128], BF16) for i in range(2)]
        Yb8 = sb("Yb8", [128, 8, 128], BF16)
        YT = sb("YT", [128, 8, 128], BF16)
        MQ = sb("MQ", [128, 8, 128], BF16)
        RB = sb("RB", [128, 128], BF16)
        VPB = sb("VPB", [128, 128], BF16)
        T1 = sb("T1", [128, 128], F32)
        OB = sb("OB", [128, 8, 128], F32)
        JK = sb("JK", [128, 128], BF16)
        SSO = sb("SSO", [128, 8, 4], F32)

        class V3:
            def __init__(self, base_ap, C, J):
                self.t = base_ap.tensor
                self.off = base_ap.offset
                self.ps = base_ap.ap[0][0]
                self.C, self.J = C, J

            def ap(self, p0=0, p1=128, c0=0, c1=None, j0=0, j1=None):
                c1 = self.C if c1 is None else c1
                j1 = self.J if j1 is None else j1
                dims = [[self.ps, p1 - p0]]
                if c1 - c0 > 1:
                    dims.append([self.J, c1 - c0])
                dims.append([1, j1 - j0])
                return AP(self.t, self.off + p0 * self.ps + c0 * self.J + j0, dims)

        _qf = qT[:].rearrange("p a b -> p (a b)")
        GBD8h = V3(_qf[:, 0:1024], 8, 128)
        GBD8l = V3(_qf[:, 1024:2048], 8, 128)
        _zf = zT[:].bitcast(F32).rearrange("p a b -> p (a b)")
        DSSa = V3(_zf[:, 0:512], 8, 64)
        DSSb = V3(_zf[:, 512:1024], 8, 64)
        Y32_8 = V3(pT[:].bitcast(F32), 8, 128)
        ON3 = V3(dnb[:].bitcast(BF16), 8, 128)
        SQ3 = V3(ogf[:].bitcast(BF16), 2, TT)

        banks = [ps("bank%d" % i) for i in range(8)]

        def cf(name):
            o, w = CF_OFF[name]
            return CF[:, o:o + w]

        def cb(name):
            o, w = CB_OFF[name]
            return CB[:, o:o + w]

        def pr(name, j=None, n=1):
            o, w = PR_OFF[name]
            if j is None:
                return PR[:, o:o + w]
            return PR[:, o + j:o + j + n]

        xob_t = XOb.tensor
        xob_ps = XOb.ap[0][0]

        def XOb_view(b, t0, n):
            return AP(xob_t, XOb.offset + b * TT + t0, [[xob_ps, 128], [1, n]])

        def XOb_view3(b0, nb, t0, n):
            return AP(xob_t, XOb.offset + b0 * TT + t0, [[xob_ps, 128], [TT, nb], [1, n]])

        B.op("sp", lambda e: e.dma_start(out=CF[:], in_=cf_d), w=["CF"], dma="c0")
        B.op("sp", lambda e: e.dma_start(out=PR[:], in_=pr_d), w=["PR"], dma="c1")
        B.op("sp", lambda e: e.dma_start(out=FW[:], in_=fw_d), w=["FW"], dma="c2")
        B.op("pool", lambda e: e.dma_start(out=CB[:], in_=cb_d), w=["CB"], dma="c3")
        B.op("act", lambda e: e.activation(out=ES[:], in_=pr("sk"), func=AF.Exp), r=["PR"], w=["ES"])
        B.op("act", lambda e: e.activation(out=NEGA[:], in_=pr("alg"), func=AF.Exp), r=["PR"], w=["NEGA"])
        B.op("dve", lambda e: e.tensor_scalar(out=NEGA[:], in0=NEGA[:], scalar1=-1.0, scalar2=None, op0=ALU.mult),
             r=["NEGA"], w=["NEGA"])
        B.op("pool", lambda e: e.memset(kT2[:], 0.0), w=["kT2"])
        B.op("pool", lambda e: e.memset(vtok[:], 0.0), w=["vtok"])
        B.op("pool", lambda e: e.memset(S2[:], 0.0), w=["S2"])
        B.op("pool", lambda e: e.memset(S2b[:], 0.0), w=["S2b"])
        B.op("pool", lambda e: e.memset(HALO[:], 0.0), w=["HALO"])
        B.op("pool", lambda e: e.memset(MQ[:], 0.0), w=["MQ"])
        B.op("pool", lambda e: e.memset(P0B8[:], 0.0), w=["P0B8"])

        wstate = {"next_load": 0, "next_use": 0}
        n_total = n_wblk * 0 + 0

        def w_load_upto(k):
            while wstate["next_load"] < k:
                i = wstate["next_load"]
                sl = i % NSLOT
                src = w_d[i % n_wblk]
                B.op("pool", (lambda e, sl=sl, src=src: e.dma_start(
                    out=ring[:, sl, :, :].rearrange("p c n -> p (c n)"), in_=src)),
                    w=["ring%d" % sl], dma="w%d" % sl)
                wstate["next_load"] += 1

        def w_next():
            i = wstate["next_use"]
            wstate["next_use"] += 1
            w_load_upto(i + 1)
            return i % NSLOT

        def w_prefetch():
            lim = min(wstate["next_use"] + NSLOT, n_wblk * NT_RUN)
            w_load_upto(min(lim, wstate["next_use"] + NSLOT))

        pbank = {"i": 0}

        def proj_fm(slot, rhs_fn, nk=16, bankset=(0, 1)):
            bi = bankset[pbank["i"] % len(bankset)]
            pbank["i"] += 1
            bk = banks[bi]
            for c in range(nk):
                B.op("pe", (lambda e, c=c, bk=bk: e.matmul(bk[:], lhsT=ring[:, slot, c, :], rhs=rhs_fn(c),
                                                           start=(c == 0), stop=(c == nk - 1))),
                     r=["ring%d" % slot, "hT"], w=["bank%d" % bi])
            return bi

        out_toks = []

        for T in range(NT_RUN):
            t0 = T * TT
            B.op("sp", lambda e, t0=t0: e.dma_start(
                out=XO[:], in_=x_d[t0:t0 + TT, :].rearrange("(t p) d -> p t d", p=128)), w=["XO"], dma="x")
            for tb in range(4):
                B.op("act", lambda e, tb=tb: e.activation(out=hb[:], in_=XO[:, tb, :], func=AF.Square,
                                                          accum_out=st1[:, 0:1]), r=["XO"], w=["hb", "st1"])
                B.op("dve", lambda e: e.tensor_scalar(out=st1[:, 1:2], in0=st1[:, 0:1], scalar1=1.0 / D, scalar2=EPS,
                                                      op0=ALU.mult, op1=ALU.add), r=["st1"], w=["st1b"])
                B.op("act", lambda e: e.activation(out=st1[:, 2:3], in_=st1[:, 1:2], func=AF.Sqrt),
                     r=["st1b"], w=["st1c"])
                B.op("dve", lambda e: e.reciprocal(out=st1[:, 3:4], in_=st1[:, 2:3]), r=["st1c"], w=["st1d"])
                B.op("dve", lambda e, tb=tb: e.tensor_scalar(out=hb[:], in0=XO[:, tb, :], scalar1=st1[:, 3:4],
                                                             scalar2=None, op0=ALU.mult),
                     r=["XO", "st1d"], w=["hb"])
                for c4 in range(4):
                    bi = 2 + (c4 % 2)
                    bkb = banks[bi][:].bitcast(BF16)
                    for cc in range(4):
                        c = c4 * 4 + cc
                        B.op("pe", lambda e, c=c, cc=cc, bkb=bkb: e.transpose(
                            out=bkb[:, cc * 128:(cc + 1) * 128], in_=hb[:, c * 128:(c + 1) * 128], identity=cb("ident")),
                            r=["hb", "CB"], w=["bank%d" % bi])
                    for cc in range(4):
                        c = c4 * 4 + cc
                        B.op("dve", lambda e, c=c, cc=cc, bkb=bkb, tb=tb: e.tensor_scalar(
                            out=hT[:, c, tb * 128:(tb + 1) * 128], in0=bkb[:, cc * 128:(cc + 1) * 128],
                            scalar1=pr("nw", c), scalar2=None, op0=ALU.mult),
                            r=["bank%d" % bi, "PR"], w=["hT"])

            if T > 0:
                B.op("pool", lambda e: e.tensor_copy(out=kT2[:, :, 0:128], in_=kT2[:, :, TT:TT + 128]),
                     r=["kT2"], w=["kT2"])
                B.op("pool", lambda e: e.tensor_copy(out=vtok[:, 0, :], in_=vtok[:, 4, :]), r=["vtok"], w=["vtok"])
            sv = [w_next(), w_next()]
            for tb in range(4):
                bi = 2 + (tb % 2)
                for half in range(2):
                    o0 = half * 128
                    B.op("pe", lambda e, bi=bi, o0=o0: e.matmul(
                        banks[bi][:, o0:o0 + 128], lhsT=CB[0:1, CB_OFF["onesrow"][0]:CB_OFF["onesrow"][0] + 128],
                        rhs=CB[0:1, CB_OFF["bv"][0] + o0:CB_OFF["bv"][0] + o0 + 128], start=True, stop=False),
                        r=["CB"], w=["bank%d" % bi])
                    for c in range(16):
                        B.op("pe", lambda e, bi=bi, o0=o0, c=c, tb=tb, sl=sv[half]: e.matmul(
                            banks[bi][:, o0:o0 + 128], lhsT=hT[:, c, tb * 128:(tb + 1) * 128], rhs=ring[:, sl, c, :],
                            start=False, stop=(c == 15)), r=["hT", "ring%d" % sv[half]], w=["bank%d" % bi])
                B.op("act", lambda e, bi=bi, tb=tb: e.copy(out=vtok[:, 1 + tb, :], in_=banks[bi][:, 0:256]),
                     r=["bank%d" % bi], w=["vtok"])
            w_prefetch()

            for g in range(4):
                sl = w_next()
                bi = proj_fm(sl, lambda c: hT[:, c, :])
                B.op("dve", lambda e, bi=bi, g=g: e.tensor_scalar(
                    out=kT2[:, g, 128:128 + TT], in0=banks[bi][:], scalar1=pr("bk", g), scalar2=None, op0=ALU.add),
                    r=["bank%d" % bi, "PR"], w=["kT2"])
                w_prefetch()
                for j in range(4):
                    sl = w_next()
                    bi = proj_fm(sl, lambda c: hT[:, c, :])
                    B.op("dve", lambda e, bi=bi, g=g, j=j: e.tensor_scalar(
                        out=qT[:, j, :], in0=banks[bi][:], scalar1=pr("bq", 4 * g + j), scalar2=0.125,
                        op0=ALU.add, op1=ALU.mult), r=["bank%d" % bi, "PR"], w=["qT"])
                    w_prefetch()
                for j in range(4):
                    sl = w_next()
                    bi = proj_fm(sl, lambda c: hT[:, c, :])
                    B.op("act", lambda e, bi=bi, j=j: e.activation(out=zT[:, j, :], in_=banks[bi][:], func=AF.Silu),
                         r=["bank%d" % bi], w=["zT"])
                    w_prefetch()
                for qb in range(4):
                    nglob = 4 * T + qb
                    kinds = ["own"] + (["prev"] if nglob > 0 else [])
                    for half in range(2):
                        pp = slice(half * 64, half * 64 + 64)
                        for ki, kind in enumerate(kinds):
                            bi = 2 + half * 2 + ki
                            kcol = 128 + qb * 128 if kind == "own" else qb * 128
                            B.op("pe", lambda e, bi=bi, pp=pp, kcol=kcol, g=g, qb=qb: e.matmul(
                                banks[bi][:], lhsT=kT2[pp, g, kcol:kcol + 128],
                                rhs=qT[pp, :, qb * 128:(qb + 1) * 128], start=True, stop=True),
                                r=["kT2", "qT"], w=["bank%d" % bi])
                            pi = half * 2 + ki
                            B.op("act", lambda e, bi=bi, pi=pi: e.activation(
                                out=pT[:, pi, :], in_=banks[bi][:], func=AF.Exp), r=["bank%d" % bi], w=["pT%d" % pi])
                            mk = cb("mown") if kind == "own" else cb("mprev")
                            mk3 = AP(mk.tensor, mk.offset, [list(mk.ap[0]), [0, 4], [1, 128]])
                            B.op("pool", lambda e, pi=pi, mk3=mk3: e.tensor_tensor(
                                out=pT[:, pi, :].rearrange("p (h q) -> p h q", h=4),
                                in0=pT[:, pi, :].rearrange("p (h q) -> p h q", h=4), in1=mk3, op=ALU.mult),
                                r=["pT%d" % pi, "CB"], w=["pT%d" % pi])
                    for half in range(2):
                        po = slice(half * 64, half * 64 + 64)
                        for ki, kind in enumerate(kinds):
                            pi = half * 2 + ki
                            vb_ = 1 + qb if kind == "own" else qb
                            B.op("pe", lambda e, po=po, pi=pi, vb_=vb_, g=g, ki=ki, nk=len(kinds): e.matmul(
                                banks[6][po, :], lhsT=vtok[:, vb_, g * 64:(g + 1) * 64], rhs=pT[:, pi, :],
                                start=(ki == 0), stop=(ki == nk - 1)), r=["vtok", "pT%d" % pi], w=["bank6"])
                        for ki, kind in enumerate(kinds):
                            pi = half * 2 + ki
                            B.op("pe", lambda e, po=po, pi=pi, ki=ki, nk=len(kinds): e.matmul(
                                banks[7][po, :], lhsT=cb("ones64"), rhs=pT[:, pi, :],
                                start=(ki == 0), stop=(ki == nk - 1)), r=["CB", "pT%d" % pi], w=["bank7"])
                    es3 = AP(ES, 4 * g, [[16, 128], [1, 4], [0, 128]])
                    B.op("dve", lambda e, es3=es3: e.tensor_tensor(
                        out=dnb[:].rearrange("p (h q) -> p h q", h=4),
                        in0=banks[7][:].rearrange("p (h q) -> p h q", h=4), in1=es3, op=ALU.add),
                        r=["bank7", "ES"], w=["dnb"])
                    B.op("dve", lambda e: e.reciprocal(out=dnb[:], in_=dnb[:]), r=["dnb"], w=["dnb"])
                    B.op("dve", lambda e: e.tensor_tensor(out=ogf[:], in0=banks[6][:], in1=dnb[:], op=ALU.mult),
                         r=["bank6", "dnb"], w=["ogf"])
                    B.op("dve", lambda e, g=g, qb=qb: e.tensor_tensor(
                        out=XOb_view3(4 * g, 4, qb * 128, 128), in0=ogf[:].rearrange("p (h q) -> p h q", h=4),
                        in1=zT[:, :, qb * 128:(qb + 1) * 128], op=ALU.mult), r=["ogf", "zT"], w=["XO"])

            for d in range(16):
                sg = w_next()
                big = proj_fm(sg, lambda c: hT[:, c, :], bankset=(0, 1))
                gi = 0
                B.op("act", lambda e, big=big, gi=gi: e.activation(out=gsb[:, gi, :], in_=banks[big][:],
                                                                   func=AF.Sigmoid),
                     r=["bank%d" % big], w=["gsb%d" % gi])
                w_prefetch()
                sw = w_next()
                bia = 2 + (d % 2)
                for c in range(16):
                    B.op("pe", lambda e, c=c, bia=bia, sw=sw: e.matmul(
                        banks[bia][:], lhsT=ring[:, sw, c, :], rhs=XOb_view(c, 0, TT), start=(c == 0), stop=(c == 15)),
                        r=["ring%d" % sw, "XO"], w=["bank%d" % bia])
                B.op("dve", lambda e, bia=bia, gi=gi, d=d: e.tensor_tensor(
                    out=m1T[:, d, :], in0=banks[bia][:], in1=gsb[:, gi, :], op=ALU.mult),
                    r=["bank%d" % bia, "gsb%d" % gi], w=["m1T"])
                w_prefetch()

            if ENABLE_DN:
                DK_SCALE = 128.0 ** -0.5

                def pview(base, dims):
                    return AP(base.tensor, base.offset, [list(base.ap[0])] + [list(d_) for d_ in dims])

                sba = w_next()
                for kc in range(16):
                    B.op("pe", lambda e, kc=kc, sba=sba: e.matmul(
                        banks[2][0:64, :], lhsT=ring[:, sba, kc, 0:64], rhs=hT[:, kc, :],
                        start=(kc == 0), stop=(kc == 15)), r=["hT", "ring%d" % sba], w=["bank2"])
                w_prefetch()
                B.op("act", lambda e: e.copy(out=ACC[0][0:64, :], in_=banks[2][0:64, :]), r=["bank2"], w=["ACC0"])
                B.mute = DN_SUB < 2
                for c in range(8):
                    for h in range(2):
                        B.op("pe", lambda e, c=c, h=h: e.matmul(
                            banks[4][h * 64:(h + 1) * 64, c * 64:(c + 1) * 64], lhsT=ACC[0][0:64, c * 64:(c + 1) * 64],
                            rhs=CF[0:64, CF_OFF["ident"][0]:CF_OFF["ident"][0] + 64], start=True, stop=True),
                            r=["ACC0", "CF"], w=["bank4"])
                B.mute = DN_SUB < 3
                for h in range(2):
                    hp = slice(h * 64, (h + 1) * 64)
                    B.op("act", lambda e, h=h, hp=hp: e.activation(
                        out=BET[hp, :, :], in_=pview(banks[4][hp, h:h + 1], [[64, 8], [2, 16]]), func=AF.Sigmoid),
                        r=["bank4"], w=["BET"])
                    B.op("dve", lambda e, h=h, hp=hp: e.tensor_tensor(
                        out=GST[hp, :, :], in0=pview(banks[4][hp, 32 + h:33 + h], [[64, 8], [2, 16]]),
                        in1=pview(PR[hp, PR_OFF["dtb"][0]:PR_OFF["dtb"][0] + 1], [[0, 8], [1, 16]]), op=ALU.add),
                        r=["bank4", "PR"], w=["GST"])
                B.mute = DN_SUB < 4
                B.op("act", lambda e: e.activation(out=GST[:], in_=GST[:], func=AF.Exp), r=["GST"], w=["GST"])
                B.op("dve", lambda e: e.tensor_scalar(out=GST[:], in0=GST[:], scalar1=1.0, scalar2=None, op0=ALU.add),
                     r=["GST"], w=["GST"])
                B.op("act", lambda e: e.activation(out=GST[:], in_=GST[:], func=AF.Ln), r=["GST"], w=["GST"])
                B.op("dve", lambda e: e.tensor_tensor(
                    out=GST[:], in0=GST[:], in1=pview(NEGA[:, 0:1], [[0, 8], [1, 16]]), op=ALU.mult),
                    r=["GST", "NEGA"], w=["GST"])
                B.mute = DN_SUB < 5
                gflat = GST[:].rearrange("p c k -> p (c k)")
                for qi, nm in enumerate(["bdu", "bdsu", "hsel0", "hsel1"]):
                    B.op("pe", lambda e, qi=qi, nm=nm: e.matmul(
                        banks[3][:, qi * 128:(qi + 1) * 128], lhsT=cf(nm), rhs=gflat, start=True, stop=True),
                        r=["CF", "GST"], w=["bank3"])
                B.op("act", lambda e: e.activation(out=EX[:].rearrange("p a c k -> p (a c k)"), in_=banks[3][:],
                                                   func=AF.Exp), r=["bank3"], w=["EX"])

                B.mute = DN_SUB < 6
                def proj_fm_gen(slot):
                    bi = (0, 1)[pbank["i"] % 2]
                    pbank["i"] += 1
                    bk = banks[bi]
                    for c in range(16):
                        B.op("pe", (lambda e, c=c, bk=bk: e.matmul(bk[:], lhsT=ring[:, slot, c, :], rhs=hT[:, c, :],
                                                                   start=(c == 0), stop=(c == 15))),
                             r=["ring%d" % slot, "hT"], w=["bank%d" % bi])
                        if c % 4 == 3 and c != 15:
                            yield
                    return bi

                def step_bg(bg):
                    if bg is None:
                        return False
                    try:
                        next(bg)
                        return True
                    except StopIteration:
                        return False

                def gen_proj(kh, XQK, ZS, KQ, KK_, KZ, lag=True):
                    deferred = []
                    stepno = [0]

                    def tick():
                        stepno[0] += 1
                        for d_ in [d_ for d_ in deferred if d_[0] <= stepno[0]]:
                            d_[1]()
                            deferred.remove(d_)

                    def later(n, fn):
                        if lag:
                            deferred.append((stepno[0] + n, fn))
                        else:
                            fn()

                    order = [("q", kh, XQK[:, 1, :], KQ), ("k", 16 + kh, XQK[:, 0, :], KK_),
                             ("v0", 32 + 2 * kh, VT[0][:], "VT0"), ("v1", 33 + 2 * kh, VT[1][:], "VT1"),
                             ("z0", None, ZS[0][:], KZ[0]), ("z1", None, ZS[1][:], KZ[1])]
                    for oi, (nm, cblk, dst, dkey) in enumerate(order):
                        sl = w_next()
                        bi = (0, 1)[pbank["i"] % 2]
                        pbank["i"] += 1
                        ci = oi % 2

                        if cblk is not None:
                            B.op("pool", lambda e, ci=ci, cblk=cblk: e.tensor_tensor(
                                out=DIAG[ci][:], in0=pview(cb("ident")[:, 0:1], [[0, 4], [1, 128]]),
                                in1=pview(pr("cw", cblk * 4), [[1, 4], [0, 128]]), op=ALU.mult),
                                r=["CB", "PR"], w=["DIAG%d" % ci])

                        def evac(bi=bi, ci=ci, cblk=cblk):
                            B.op("act", lambda e: e.copy(out=CSb[ci][:, 3:515], in_=banks[bi][:]),
                                 r=["bank%d" % bi], w=["CSm%d" % ci])
                            B.op("dve", lambda e: e.tensor_copy(out=CSb[ci][:, 0:3], in_=HALO[:, cblk, 0:3]),
                                 r=["HALO%d" % cblk], w=["CSh%d" % ci])
                            B.op("dve", lambda e: e.tensor_copy(out=HALO[:, cblk, 0:3], in_=CSb[ci][:, 512:515]),
                                 r=["CSm%d" % ci], w=["HALO%d" % cblk])

                        def conv(ci=ci, cblk=cblk):
                            for ti in range(4):
                                B.op("pe", lambda e, ti=ti: e.matmul(
                                    banks[2][:], lhsT=DIAG[ci][:, ti, :], rhs=CSb[ci][:, ti:ti + 512],
                                    start=(ti == 0), stop=(ti == 3)),
                                    r=["DIAG%d" % ci, "CSm%d" % ci, "CSh%d" % ci], w=["bank2"])

                        def silu_c(ci=ci, dst=dst, dkey=dkey):
                            B.op("act", lambda e: e.activation(out=dst, in_=banks[2][:], func=AF.Silu),
                                 r=["bank2"], w=[dkey])

                        def silu_z(bi=bi, dst=dst, dkey=dkey):
                            B.op("act", lambda e: e.activation(out=dst, in_=banks[bi][:], func=AF.Silu),
                                 r=["bank%d" % bi], w=[dkey])

                        for c in range(16):
                            B.op("pe", (lambda e, c=c, bi=bi, sl=sl: e.matmul(
                                banks[bi][:], lhsT=ring[:, sl, c, :], rhs=hT[:, c, :], start=(c == 0), stop=(c == 15))),
                                r=["ring%d" % sl, "hT"], w=["bank%d" % bi])
                            if c % 4 == 3:
                                if c == 15:
                                    w_prefetch()
                                    if cblk is not None:
                                        later(0, evac)
                                        later(1, conv)
                                        later(2, silu_c)
                                    else:
                                        later(2, silu_z)
                                tick()
                                yield
                    while deferred:
                        tick()
                        yield

                def kh_body(kh, XQK, ZS, KQ, KK_, KZ, bg):
                    if DN_LEVEL >= 3:
                        B.op("pool", lambda e: e.tensor_tensor(out=SQ3.ap(), in0=XQK[:], in1=XQK[:], op=ALU.mult),
                             r=[KQ, KK_], w=["ogf"])
                        for c in range(8):
                            for qi in range(2):
                                for h in range(2):
                                    B.op("pe", lambda e, c=c, qi=qi, h=h: e.matmul(
                                        banks[7][h * 64:(h + 1) * 64, 256 + c * 2 + qi:256 + c * 2 + qi + 1],
                                        lhsT=SQ3.ap(c0=1 - qi, c1=2 - qi, j0=c * 64, j1=(c + 1) * 64),
                                        rhs=CB[:, CB_OFF["onescol"][0]:CB_OFF["onescol"][0] + 1], start=True, stop=True),
                                        r=["ogf", "CB"], w=["bank7"])
                        B.op("dve", lambda e: e.tensor_scalar(
                            out=RS[:].rearrange("p c q -> p (c q)"), in0=banks[7][:, 256:272], scalar1=EPS, scalar2=None,
                            op0=ALU.add), r=["bank7"], w=["RS"])
                        B.op("act", lambda e: e.activation(out=RS[:], in_=RS[:], func=AF.Ln), r=["RS"], w=["RS"])
                        B.op("act", lambda e: e.activation(out=RS[:], in_=RS[:], func=AF.Exp, scale=-0.5), r=["RS"], w=["RS"])
                        rq = RS[:, :, 0]
                        rk = RS[:, :, 1]
                        betk = BET[:, :, kh]
                        egck = EX[:, 0, :, kh]
                        C1, CA, C2, CO, COE = (SC[:, i, :] for i in range(5))
                        B.op("dve", lambda e, betk=betk: e.tensor_tensor(out=C1, in0=betk, in1=rk, op=ALU.mult),
                             r=["BET", "RS"], w=["SC0"])
                        B.op("dve", lambda e: e.tensor_tensor(out=CA, in0=C1, in1=rk, op=ALU.mult),
                             r=["SC0", "RS"], w=["SC1"])
                        B.op("dve", lambda e, egck=egck: e.scalar_tensor_tensor(out=C2, in0=CA, scalar=-1.0, in1=egck,
                                                                     op0=ALU.mult, op1=ALU.mult),
                             r=["SC1", "EX"], w=["SC2"])
                        B.op("dve", lambda e: e.tensor_scalar(out=CO, in0=rq, scalar1=DK_SCALE, scalar2=None, op0=ALU.mult),
                             r=["RS"], w=["SC3"])
                        B.op("dve", lambda e, egck=egck: e.tensor_tensor(out=COE, in0=CO, in1=egck, op=ALU.mult),
                             r=["SC3", "EX"], w=["SC4"])

                    if DN_LEVEL >= 4:
                        for c4 in range(2):
                            bi = 4 + c4
                            for cc in range(4):
                                c = c4 * 4 + cc
                                co_ = cc * 128
                                for h in range(2):
                                    B.op("pe", lambda e, c=c, bi=bi, co_=co_, h=h: e.matmul(
                                        banks[bi][h * 64:(h + 1) * 64, co_:co_ + 128], lhsT=XQK[:, 0, c * 64:(c + 1) * 64],
                                        rhs=cb("ident"), start=True, stop=True), r=[KK_, "CB"], w=["bank%d" % bi])
                            for cc in range(4):
                                c = c4 * 4 + cc
                                co_ = cc * 128
                                B.op("dve", lambda e, c=c, bi=bi, co_=co_, kh=kh: e.tensor_scalar(
                                    out=XKD[:, c, :], in0=banks[bi][:, co_:co_ + 128], scalar1=EX[:, 1, c, kh:kh + 1],
                                    scalar2=None, op0=ALU.mult), r=["bank%d" % bi, "EX"], w=["XKD"])
                        for c4 in range(2):
                            bi = 4 + c4
                            for cc in range(4):
                                c = c4 * 4 + cc
                                co_ = cc * 128
                                for h in range(2):
                                    B.op("pe", lambda e, c=c, bi=bi, co_=co_, h=h: e.matmul(
                                        banks[bi][h * 64:(h + 1) * 64, co_:co_ + 128], lhsT=VT[h][:, c * 64:(c + 1) * 64],
                                        rhs=cb("ident"), start=True, stop=True), r=["VT%d" % h, "CB"], w=["bank%d" % bi])
                            for cc in range(4):
                                c = c4 * 4 + cc
                                co_ = cc * 128
                                B.op("dve", lambda e, c=c, bi=bi, co_=co_: e.tensor_scalar(
                                    out=VB[:, c, :], in0=banks[bi][:, co_:co_ + 128], scalar1=SC[:, 0, c:c + 1],
                                    scalar2=None, op0=ALU.mult), r=["bank%d" % bi, "SC0"], w=["VB"])

                    if DN_LEVEL >= 5:
                        KEY_GBD = ["qT"]
                        KEY_DSS = ["zT"]
                        KEY_Y32 = ["pT0", "pT1", "pT2", "pT3"]
                        gb = pview(GST[:, 0, kh:kh + 1], [[16, 8]])
                        B.op("dve", lambda e, gb=gb: e.tensor_copy(out=GHb[:], in_=gb), r=["GST"], w=["GHb"])
                        B.op("dve", lambda e, gb=gb: e.tensor_tensor(out=GHL[:, 1, :], in0=gb, in1=GHb[:], op=ALU.subtract),
                             r=["GST", "GHb"], w=["GHL"])
                        for (dst, dkey, cname, width, src, skey) in [
                                (GP8h[:], "GP8h", "u2", 64, GHb, "GHb"), (GP8l[:], "GP8l", "u2", 64, None, "GHL"),
                                (GBD8h.ap(), "qT", "bdu", 128, GHb, "GHb"), (GBD8l.ap(), "qT", "bdu", 128, None, "GHL")]:
                            if src is not None:
                                in1 = pview(GHb[:, 0:1], [[1, 8], [0, width]])
                            else:
                                in1 = pview(GHL[:, 1, 0:1], [[1, 8], [0, width]])
                            B.op("dve", lambda e, dst=dst, cname=cname, width=width, in1=in1: e.tensor_tensor(
                                out=dst, in0=pview(cf(cname)[:, 0:1], [[0, 8], [1, width]]), in1=in1, op=ALU.mult),
                                r=["CF", skey], w=[dkey])
                        for c in range(8):
                            bi = 2 + c // 4
                            co_ = (c % 4) * 128
                            for h in range(2):
                                B.op("pe", lambda e, c=c, h=h, bi=bi, co_=co_: e.matmul(
                                    banks[bi][h * 64:(h + 1) * 64, co_:co_ + 128], lhsT=XQK[:, 0, c * 64:(c + 1) * 64],
                                    rhs=XQK[:, :, c * 64:(c + 1) * 64], start=True, stop=True),
                                    r=[KK_, KQ], w=["bank%d" % bi])
                        for (bi, cconst, cmask, cones) in [(4, "bdneg", "masks", "ones64"), (5, "bdones", "maskt", "neg64")]:
                            bk_ = "bank%d" % bi
                            B.op("pe", lambda e, bi=bi, cconst=cconst: e.matmul(
                                banks[bi][:], lhsT=cb(cconst), rhs=GP8h[:], start=True, stop=False, skip_group_check=True),
                                r=["CB", "GP8h"], w=[bk_])
                            B.op("pe", lambda e, bi=bi, cconst=cconst: e.matmul(
                                banks[bi][:], lhsT=cb(cconst), rhs=GP8l[:], start=False, stop=False, skip_group_check=True),
                                r=["CB", "GP8l"], w=[bk_])
                            B.op("pe", lambda e, bi=bi, cmask=cmask: e.matmul(
                                banks[bi][:], lhsT=cb("ident"), rhs=pview(cb(cmask)[:, 0:1], [[0, 8], [1, 64]]),
                                start=False, stop=False, skip_group_check=True), r=["CB"], w=[bk_])
                            for c in range(8):
                                B.op("pe", lambda e, bi=bi, c=c, cones=cones: e.matmul(
                                    banks[bi][:, c * 64:(c + 1) * 64], lhsT=GBD8h.ap(c0=c, c1=c + 1), rhs=cb(cones),
                                    start=False, stop=False, skip_group_check=True), r=["CB", "qT"], w=[bk_])
                                B.op("pe", lambda e, bi=bi, c=c, cones=cones: e.matmul(
                                    banks[bi][:, c * 64:(c + 1) * 64], lhsT=GBD8l.ap(c0=c, c1=c + 1), rhs=cb(cones),
                                    start=False, stop=(c == 7), skip_group_check=True), r=["CB", "qT"], w=[bk_])
                        B.op("act", lambda e: e.activation(out=DSSa.ap(), in_=banks[4][:].rearrange("p (c j) -> p c j", c=8),
                                                           func=AF.Exp), r=["bank4"], w=KEY_DSS)
                        B.op("act", lambda e: e.activation(out=DSSb.ap(), in_=banks[5][:].rearrange("p (c j) -> p c j", c=8),
                                                           func=AF.Exp), r=["bank5"], w=KEY_DSS)
                        B.op("dve", lambda e: e.tensor_tensor(
                            out=DSSa.ap(), in0=DSSa.ap(), in1=pview(SC[:, 1, 0:1], [[1, 8], [0, 64]]), op=ALU.mult),
                            r=KEY_DSS + ["SC1"], w=KEY_DSS)
                        for b2 in range(2):
                            for h in range(2):
                                hp = slice(h * 64, (h + 1) * 64)
                                B.op("dve", lambda e, b2=b2, h=h, hp=hp: e.tensor_tensor(
                                    out=P0B8[hp, 4 * b2:4 * b2 + 4, h * 64:(h + 1) * 64],
                                    in0=pview(banks[2 + b2][hp, 0:1], [[128, 4], [1, 64]]),
                                    in1=DSSa.ap(p0=h * 64, p1=(h + 1) * 64, c0=4 * b2, c1=4 * b2 + 4),
                                    op=ALU.mult), r=["bank%d" % (2 + b2)] + KEY_DSS, w=["P0B8"])
                                B.op("dve", lambda e, b2=b2, h=h, hp=hp: e.tensor_tensor(
                                    out=MQ[hp, 4 * b2:4 * b2 + 4, h * 64:(h + 1) * 64],
                                    in0=pview(banks[2 + b2][hp, 64:65], [[128, 4], [1, 64]]),
                                    in1=DSSb.ap(p0=h * 64, p1=(h + 1) * 64, c0=4 * b2, c1=4 * b2 + 4),
                                    op=ALU.mult), r=["bank%d" % (2 + b2)] + KEY_DSS, w=["MQ"])
                        for c in range(8):
                            bi = 6 + c // 4
                            co_ = (c % 4) * 128
                            B.op("pe", lambda e, c=c, bi=bi, co_=co_: e.matmul(
                                banks[bi][:, co_:co_ + 128], lhsT=P0B8[:, c, :], rhs=cb("ident"), start=True, stop=True),
                                r=["P0B8", "CB"], w=["bank%d" % bi])
                        for b2 in range(2):
                            B.op("act", lambda e, b2=b2: e.copy(
                                out=Q0B8[:, 4 * b2:4 * b2 + 4, :], in_=banks[6 + b2][:].rearrange("p (c j) -> p c j", c=4)),
                                r=["bank%d" % (6 + b2)], w=["Q0B8"])
                            B.op("dve", lambda e, b2=b2: e.tensor_tensor(
                                out=Y32_8.ap(c0=4 * b2, c1=4 * b2 + 4), in0=pview(cf("ident")[:, 0:1], [[0, 4], [1, 128]]),
                                in1=banks[6 + b2][:].rearrange("p (c j) -> p c j", c=4), op=ALU.subtract),
                                r=["bank%d" % (6 + b2), "CF"], w=KEY_Y32)
                        B.op("act", lambda e: e.copy(out=Yb8[:], in_=Y32_8.ap()), r=KEY_Y32, w=["Yb8"])
                        def PQk(k):
                            if k == 0:
                                return (lambda c: P0B8[:, c, :]), (lambda c: Q0B8[:, c, :]), ["P0B8", "Q0B8"]
                            t_ = PQ8[k % 2]
                            return (lambda c: t_[:, 0, c, :]), (lambda c: t_[:, 1, c, :]), ["PQ8_%d" % (k % 2)]

                        def squaring(k):
                            Pk, Qk, pk_keys = PQk(k)
                            nxt = PQ8[(k + 1) % 2]
                            nk_ = "PQ8_%d" % ((k + 1) % 2)
                            for c in range(8):
                                bi = 2 + c // 4
                                co_ = (c % 4) * 128
                                B.op("pe", lambda e, c=c, bi=bi, co_=co_: e.matmul(
                                    banks[bi][:, co_:co_ + 128], lhsT=Qk(c), rhs=Pk(c), start=True, stop=True),
                                    r=pk_keys, w=["bank%d" % bi])
                            if k < 4:
                                for c in range(8):
                                    bi = 4 + c // 4
                                    co_ = (c % 4) * 128
                                    B.op("pe", lambda e, c=c, bi=bi, co_=co_: e.matmul(
                                        banks[bi][:, co_:co_ + 128], lhsT=Pk(c), rhs=Qk(c), start=True, stop=True),
                                        r=pk_keys, w=["bank%d" % bi])
                            for b2 in range(2):
                                B.op("act", lambda e, b2=b2: e.copy(
                                    out=nxt[:, 0, 4 * b2:4 * b2 + 4, :],
                                    in_=banks[2 + b2][:].rearrange("p (c j) -> p c j", c=4)),
                                    r=["bank%d" % (2 + b2)], w=[nk_])
                            if k < 4:
                                for b2 in range(2):
                                    B.op("dve", lambda e, b2=b2: e.tensor_copy(
                                        out=nxt[:, 1, 4 * b2:4 * b2 + 4, :],
                                        in_=banks[4 + b2][:].rearrange("p (c j) -> p c j", c=4)),
                                        r=["bank%d" % (4 + b2)], w=[nk_])

                        def apply(k):
                            nxt = PQ8[(k + 1) % 2]
                            nk_ = "PQ8_%d" % ((k + 1) % 2)
                            for c in range(8):
                                bi = 6 + c // 4
                                co_ = (c % 4) * 128
                                B.op("pe", lambda e, c=c, bi=bi, co_=co_: e.matmul(
                                    banks[bi][:, co_:co_ + 128], lhsT=nxt[:, 0, c, :], rhs=Yb8[:, c, :],
                                    start=True, stop=True), r=[nk_, "Yb8"], w=["bank%d" % bi])
                            for b2 in range(2):
                                B.op("dve", lambda e, b2=b2: e.tensor_tensor(
                                    out=Y32_8.ap(c0=4 * b2, c1=4 * b2 + 4),
                                    in0=banks[6 + b2][:].rearrange("p (c j) -> p c j", c=4),
                                    in1=Y32_8.ap(c0=4 * b2, c1=4 * b2 + 4), op=ALU.add),
                                    r=["bank%d" % (6 + b2)] + KEY_Y32, w=KEY_Y32)
                            if k < 4:
                                B.op("act", lambda e: e.copy(out=Yb8[:], in_=Y32_8.ap()), r=KEY_Y32, w=["Yb8"])
                            else:
                                B.op("act", lambda e: e.copy(out=YT[:], in_=Y32_8.ap()), r=KEY_Y32, w=["YT"])

                        squaring(0)
                        for k in range(1, 5):
                            squaring(k)
                            apply(k - 1)
                        apply(4)
                    if DN_LEVEL >= 6:
                        for c in range(8):
                            for h in range(2):
                                hp = slice(h * 64, (h + 1) * 64)
                                B.op("pe", lambda e, c=c, h=h, hp=hp, kh=kh: e.matmul(
                                    banks[6][hp, 0:128], lhsT=XQK[:, 0, c * 64:(c + 1) * 64],
                                    rhs=S2b[:, kh, h * 128:(h + 1) * 128], start=True, stop=True),
                                    r=[KK_, "S2b"], w=["bank6"])
                                B.op("pe", lambda e, c=c, h=h, hp=hp, kh=kh: e.matmul(
                                    banks[6][hp, 128:256], lhsT=XQK[:, 1, c * 64:(c + 1) * 64],
                                    rhs=S2b[:, kh, h * 128:(h + 1) * 128], start=True, stop=True),
                                    r=[KQ, "S2b"], w=["bank6"])
                            step_bg(bg)
                            B.op("dve", lambda e, c=c: e.scalar_tensor_tensor(
                                out=RB[:], in0=banks[6][:, 0:128], scalar=SC[:, 2, c:c + 1], in1=VB[:, c, :],
                                op0=ALU.mult, op1=ALU.add), r=["bank6", "SC2", "VB"], w=["RB"])
                            B.op("pe", lambda e, c=c: e.matmul(banks[5][:, 0:128], lhsT=YT[:, c, :], rhs=RB[:],
                                                               start=True, stop=True), r=["YT", "RB"], w=["bank5"])
                            step_bg(bg)
                            B.op("act", lambda e: e.copy(out=VPB[:], in_=banks[5][:, 0:128]), r=["bank5"], w=["VPB"])
                            B.op("pe", lambda e, c=c: e.matmul(banks[7][:, 0:128], lhsT=MQ[:, c, :], rhs=VPB[:],
                                                               start=True, stop=True), r=["MQ", "VPB"], w=["bank7"])
                            B.op("act", lambda e, c=c: e.activation(out=T1[:], in_=banks[6][:, 128:256], func=AF.Identity,
                                                                    scale=SC[:, 4, c:c + 1]),
                                 r=["bank6", "SC4"], w=["T1"])
                            B.op("dve", lambda e, c=c: e.scalar_tensor_tensor(
                                out=OB[:, c, :], in0=banks[7][:, 0:128], scalar=SC[:, 3, c:c + 1], in1=T1[:],
                                op0=ALU.mult, op1=ALU.add), r=["bank7", "SC3", "T1"], w=["OB"])
                            sbk = [banks[3][:, 0:128], banks[4][:, 0:128]]
                            sbk_key = ["bank3", "bank4"]
                            for h in range(2):
                                hp = slice(h * 64, (h + 1) * 64)
                                B.op("pe", lambda e, c=c, h=h, hp=hp, sbk=sbk: e.matmul(
                                    sbk[h], lhsT=XKD[hp, c, :], rhs=VPB[hp, :],
                                    start=True, stop=True), r=["XKD", "VPB"], w=[sbk_key[h]])
                            step_bg(bg)
                            for h in range(2):
                                B.op("dve", lambda e, c=c, h=h, kh=kh, sbk=sbk: e.scalar_tensor_tensor(
                                    out=S2b[:, kh, h * 128:(h + 1) * 128], in0=S2[:, kh, h * 128:(h + 1) * 128],
                                    scalar=EX[:, 2 + h, c, kh:kh + 1], in1=sbk[h],
                                    op0=ALU.mult, op1=ALU.add), r=["S2", "EX", sbk_key[h]], w=["S2b"])
                            for h in range(2):
                                B.op("dve", lambda e, c=c, h=h, kh=kh, sbk=sbk: e.scalar_tensor_tensor(
                                    out=S2[:, kh, h * 128:(h + 1) * 128], in0=S2[:, kh, h * 128:(h + 1) * 128],
                                    scalar=EX[:, 2 + h, c, kh:kh + 1], in1=sbk[h],
                                    op0=ALU.mult, op1=ALU.add), r=["S2", "EX", sbk_key[h]], w=["S2"])
                            B.op("act", lambda e, c=c: e.activation(out=JK[:], in_=OB[:, c, :], func=AF.Square,
                                                                    accum_out=SSO[:, c, 0:1]), r=["OB"], w=["JK", "SSO"])

                    while step_bg(bg):
                        pass
                    if DN_LEVEL >= 7:
                        B.op("dve", lambda e: e.tensor_scalar(out=SSO[:, :, 1], in0=SSO[:, :, 0], scalar1=1.0 / 128.0,
                                                              scalar2=EPS, op0=ALU.mult, op1=ALU.add),
                             r=["SSO"], w=["SSO1"])
                        B.op("act", lambda e: e.activation(out=SSO[:, :, 2], in_=SSO[:, :, 1], func=AF.Ln),
                             r=["SSO1"], w=["SSO2"])
                        B.op("act", lambda e: e.activation(out=SSO[:, :, 3], in_=SSO[:, :, 2], func=AF.Exp, scale=-0.5),
                             r=["SSO2"], w=["SSO3"])
                        B.op("dve", lambda e: e.tensor_tensor(
                            out=ON3.ap(), in0=OB[:], in1=pview(SSO[:, 0, 3:4], [[4, 8], [0, 128]]), op=ALU.mult),
                            r=["OB", "SSO3"], w=["dnb"])
                        for c4 in range(2):
                            bi = 4 + c4
                            bkb = banks[bi][:].bitcast(BF16)
                            for cc in range(4):
                                c = c4 * 4 + cc
                                B.op("pe", lambda e, c=c, cc=cc, bkb=bkb: e.transpose(
                                    out=bkb[:, cc * 128:(cc + 1) * 128], in_=ON3.ap(c0=c, c1=c + 1), identity=cb("ident")),
                                    r=["dnb", "CB"], w=["bank%d" % bi])
                            for h in range(2):
                                B.op("dve", lambda e, bkb=bkb, h=h, c4=c4, kh=kh: e.scalar_tensor_tensor(
                                    out=XOb_view(2 * kh + h, c4 * 256, 256).rearrange("p (c i) -> p c i", c=4),
                                    in0=pview(bkb[:, h * 64:h * 64 + 1], [[128, 4], [1, 64]]), scalar=pr("dnw", 0),
                                    in1=ZS[h][:, c4 * 256:(c4 + 1) * 256].rearrange("p (c i) -> p c i", c=4),
                                    op0=ALU.mult, op1=ALU.mult), r=["bank%d" % bi, "PR", KZ[h]], w=["XO"])


                def pars(kh):
                    p_ = kh % 2
                    return (XQKs[p_], ZSs[p_], "XQKq%d" % p_, "XQKk%d" % p_, ["ZS%d_%d" % (p_, h) for h in range(2)])

                g0 = gen_proj(0, *pars(0), lag=False)
                while step_bg(g0):
                    pass
                for kh in range(16):
                    bg = gen_proj(kh + 1, *pars(kh + 1)) if kh < 15 else None
                    kh_body(kh, *pars(kh), bg)

                B.mute = DN_SUB < 7
                for d in range(16):
                    sg = w_next()
                    big = proj_fm(sg, lambda c: hT[:, c, :], bankset=(0, 1))
                    gi = 0
                    B.op("act", lambda e, big=big, gi=gi: e.activation(out=gsb[:, gi, :], in_=banks[big][:],
                                                                       func=AF.Sigmoid),
                         r=["bank%d" % big], w=["gsb%d" % gi])
                    w_prefetch()
                    s1 = w_next()
                    s2 = w_next()
                    bia = 2 + (d % 2)
                    for c in range(32):
                        sl_ = s1 if c < 16 else s2
                        B.op("pe", lambda e, c=c, bia=bia, sl_=sl_: e.matmul(
                            banks[bia][:], lhsT=ring[:, sl_, c % 16, :], rhs=XOb_view(c, 0, TT),
                            start=(c == 0), stop=(c == 31)), r=["ring%d" % sl_, "XO"], w=["bank%d" % bia])
                    B.op("dve", lambda e, bia=bia, gi=gi: e.tensor_tensor(
                        out=gsb[:, gi, :], in0=banks[bia][:], in1=gsb[:, gi, :], op=ALU.mult),
                        r=["bank%d" % bia, "gsb%d" % gi], w=["gsb%d" % gi])
                    B.op("pool", lambda e, gi=gi, d=d: e.tensor_tensor(
                        out=m1T[:, d, :], in0=gsb[:, gi, :], in1=m1T[:, d, :], op=ALU.add),
                        r=["gsb%d" % gi, "m1T"], w=["m1T"])
                    w_prefetch()
            else:
                for _ in range(N_DN_BLK):
                    w_next()
                    w_prefetch()
            B.mute = False
            while wstate["next_use"] % 4 != 0:
                w_next()
                w_prefetch()

            if SKIP_P3:
                continue
            B.op("sp", lambda e, t0=t0: e.dma_start(
                out=XO[:], in_=x_d[t0:t0 + TT, :].rearrange("(t p) d -> p t d", p=128)), r=[], w=["XO"], dma="x")
            for cg in range(4):
                sls = [w_next() for _ in range(4)]
                assert sls[0] % 4 == 0 and sls == list(range(sls[0], sls[0] + 4))
                for tb in range(4):
                    bi = 2 + ((cg * 4 + tb) % 4)
                    for c in range(16):
                        rhs = AP(ring, ring[:, sls[0], c, :].offset, [list(ring[:].ap[0]), [16 * 128, 4], [1, 128]])
                        B.op("pe", lambda e, bi=bi, c=c, tb=tb, rhs=rhs: e.matmul(
                            banks[bi][:], lhsT=m1T[:, c, tb * 128:(tb + 1) * 128], rhs=rhs,
                            start=(c == 0), stop=(c == 15)),
                            r=["m1T"] + ["ring%d" % s_ for s_ in sls], w=["bank%d" % bi])
                    B.op("dve", lambda e, bi=bi, tb=tb, cg=cg: e.tensor_tensor(
                        out=XO[:, tb, cg * 512:(cg + 1) * 512], in0=banks[bi][:], in1=XO[:, tb, cg * 512:(cg + 1) * 512],
                        op=ALU.add), r=["bank%d" % bi, "XO"], w=["XO"])
                w_prefetch()
            for tb in range(4):
                B.op("act", lambda e, tb=tb: e.activation(out=hb[:], in_=XO[:, tb, :], func=AF.Square,
                                                          accum_out=st1[:, 0:1]), r=["XO"], w=["hb", "st1"])
                B.op("dve", lambda e: e.tensor_scalar(out=st1[:, 1:2], in0=st1[:, 0:1], scalar1=1.0 / D, scalar2=EPS,
                                                      op0=ALU.mult, op1=ALU.add), r=["st1"], w=["st1b"])
                B.op("act", lambda e: e.activation(out=st1[:, 2:3], in_=st1[:, 1:2], func=AF.Sqrt),
                     r=["st1b"], w=["st1c"])
                B.op("dve", lambda e: e.reciprocal(out=st1[:, 3:4], in_=st1[:, 2:3]), r=["st1c"], w=["st1d"])
                B.op("dve", lambda e, tb=tb: e.scalar_tensor_tensor(
                    out=XO[:, tb, :], in0=XO[:, tb, :], scalar=st1[:, 3:4], in1=FW[:], op0=ALU.mult, op1=ALU.mult),
                    r=["XO", "st1d", "FW"], w=["XO"])
                tok = B.op("sp", lambda e, tb=tb, t0=t0: e.dma_start(
                    out=y_d[t0 + tb * 128:t0 + (tb + 1) * 128, :], in_=XO[:, tb, :]), r=["XO"], dma="o%d" % tb)
                out_toks.append(tok)

        B.wait_all("sp", out_toks)
        block = stack.enter_context(nc.Block())
        B.emit(nc, block, stack)
    return nc


_CACHE = {}


def kernel(x, norm_w, w_in, b_qkv, sinks, conv_w, a_log, dt_bias, dn_norm_w,
           w_att_branch, w_dn_branch, w_out, final_norm_w):
    x = np.asarray(x, np.float32)
    wst = build_wstream(np.asarray(w_in[0], np.float32), np.asarray(w_att_branch[0], np.float32),
                        np.asarray(w_dn_branch[0], np.float32), np.asarray(w_out[0], np.float32))
    cfa = consts_f32()
    cba = consts_b16(np.asarray(b_qkv[0], np.float32))
    pra = params_f32(np.asarray(norm_w[0]), np.asarray(b_qkv[0]), np.asarray(sinks[0]), np.asarray(conv_w[0]),
                     np.asarray(a_log[0]), np.asarray(dt_bias[0]), np.asarray(dn_norm_w[0]))
    fwa = np.ascontiguousarray(np.broadcast_to(np.asarray(final_norm_w, np.float32)[None, :], (128, D)))
    n_wblk = wst.shape[0]
    if "nc" not in _CACHE:
        _CACHE["nc"] = build_program(n_wblk)
    nc = _CACHE["nc"]
    in_maps = [{"x": np.ascontiguousarray(x[b]), "wst": wst, "cf": cfa, "cb": cba, "pr": pra, "fw": fwa}
               for b in range(8)]
    res = run_bass_kernel_spmd(nc, in_maps, core_ids=list(range(8)))
    return np.stack([res.results[b]["y"] for b in range(8)], axis=0).astype(np.float32)
```

```python
import contextlib
import numpy as np
import concourse.bass as bass
import concourse.mybir as mybir
from concourse.bass_utils import run_bass_kernel_spmd

F32 = mybir.dt.float32
BF16 = mybir.dt.bfloat16
AF = mybir.ActivationFunctionType
ALU = mybir.AluOpType
AX = mybir.AxisListType

S = 2048
D = 2048
TT = 512
NT = S // TT
P = 128
NSLOT = 8
EPS = 1e-6
OFF_AQ = 0
OFF_AK = 2048
OFF_AV = 2304
OFF_AZ = 2560
OFF_DQKV = 4608
OFF_DZ = 12800
OFF_DB = 16896
OFF_DA = 16928
OFF_G = 16960
NEG = -30000.0
DGE_SCRATCH = 4096

ENABLE_DN = True
NT_RUN = NT
SKIP_P3 = False
DN_LEVEL = 9
DN_SUB = 99
import os as _os
BISV = int(_os.environ.get('BISV', '0'))


class Builder:
    ENG = ["pe", "act", "dve", "pool", "sp"]

    def __init__(self):
        self.ops = {e: [] for e in self.ENG}
        self.lastw = {}
        self.readers = {}
        self.dma_cnt = {}
        self.mute = False

    def op(self, eng, fn, r=(), w=(), dma=None):
        if self.mute and dma is None:
            return None
        deps = set()
        for k in r:
            t = self.lastw.get(k)
            if t is not None:
                deps.add(t)
            if k.startswith("bank"):
                for t in self.readers.get(k, ()):
                    if not (t[0] == "e" and t[1] == eng):
                        deps.add(t)
        for k in w:
            t = self.lastw.get(k)
            if t is not None:
                deps.add(t)
            for t in self.readers.get(k, ()):
                deps.add(t)
        idx = len(self.ops[eng])
        if dma is not None:
            self.dma_cnt[dma] = self.dma_cnt.get(dma, 0) + 1
            tok = ("d", dma, 16 * self.dma_cnt[dma])
        else:
            tok = ("e", eng, idx)
        self.ops[eng].append({"fn": fn, "deps": deps, "dma": dma, "sig": False})
        for k in w:
            self.lastw[k] = tok
            self.readers[k] = []
        for k in r:
            if k in w:
                continue
            lst = self.readers.setdefault(k, [])
            if tok[0] == "e":
                lst[:] = [t for t in lst if not (t[0] == "e" and t[1] == eng)]
            lst.append(tok)
        return tok

    def wait_all(self, eng, toks):
        self.ops[eng].append({"fn": None, "deps": set(toks), "dma": None, "sig": False})

    def emit(self, nc, block, stack):
        for e in self.ENG:
            for rec in self.ops[e]:
                for t in rec["deps"]:
                    if t[0] == "e" and not (t[1] == "pe" and e == "pe"):
                        self.ops[t[1]][t[2]]["sig"] = True
        sem_e = {e: stack.enter_context(nc.semaphore("se_" + e)) for e in self.ENG}
        sem_d = {k: stack.enter_context(nc.semaphore("sd_" + k)) for k in self.dma_cnt}
        for e in self.ENG:
            c = 0
            for rec in self.ops[e]:
                if rec["sig"]:
                    c += 1
                rec["sv"] = c
        ops = self.ops

        def gen(e):
            def run(eh):
                waited = {}
                for rec in ops[e]:
                    need = {}
                    for t in rec["deps"]:
                        if t[0] == "e":
                            if t[1] == "pe" and e == "pe":
                                continue
                            key = ("e", t[1])
                            val = ops[t[1]][t[2]]["sv"]
                        else:
                            key = ("d", t[1])
                            val = t[2]
                        if val > need.get(key, 0):
                            need[key] = val
                    for key, val in need.items():
                        if waited.get(key, 0) >= val:
                            continue
                        waited[key] = val
                        sem = sem_e[key[1]] if key[0] == "e" else sem_d[key[1]]
                        eh.wait_ge(sem, val)
                    if rec["fn"] is None:
                        continue
                    ins = rec["fn"](eh)
                    if rec["dma"] is not None:
                        ins.then_inc(sem_d[rec["dma"]], 16)
                    elif rec["sig"]:
                        ins.then_inc(sem_e[e], 1)
            return run

        block.tensor(gen("pe"))
        block.scalar(gen("act"))
        block.vector(gen("dve"))
        block.gpsimd(gen("pool"))
        block.sync(gen("sp"))


def AP(t, off, dims):
    return bass.AP(t, off, [list(d) for d in dims])


def dup2(ap):
    a = [list(d) for d in ap.ap]
    assert len(a) == 2
    return bass.AP(ap.tensor, ap.offset, [a[0], [0, 2], a[1]])


def wblock(w, r0, cols):
    sub = w[r0:r0 + 2048][:, cols]
    return np.ascontiguousarray(sub.reshape(16, 128, 128).transpose(1, 0, 2))


def build_wstream(w_in, w_att, w_dn, w_out):
    blocks = []
    ar = np.arange

    def cin(c0, n=128):
        return ar(c0, c0 + n)

    blocks.append(wblock(w_in, 0, cin(OFF_AV)))
    blocks.append(wblock(w_in, 0, cin(OFF_AV + 128)))
    for g in range(4):
        kc = cin(OFF_AK + 64 * g, 64)
        blocks.append(wblock(w_in, 0, np.concatenate([kc, kc])))
        for j in range(4):
            blocks.append(wblock(w_in, 0, cin(OFF_AQ + 128 * (4 * g + j))))
        for j in range(4):
            blocks.append(wblock(w_in, 0, cin(OFF_AZ + 128 * (4 * g + j))))
    for d in range(16):
        blocks.append(wblock(w_in, 0, cin(OFF_G + 128 * d)))
        blocks.append(wblock(w_att, 0, cin(128 * d)))
    ba = cin(OFF_DB, 64)
    blocks.append(wblock(w_in, 0, np.concatenate([ba, ba])))
    for kh in range(16):
        blocks.append(wblock(w_in, 0, cin(OFF_DQKV + 128 * kh)))
        blocks.append(wblock(w_in, 0, cin(OFF_DQKV + 2048 + 128 * kh)))
        for h in range(2):
            blocks.append(wblock(w_in, 0, cin(OFF_DQKV + 4096 + 128 * (2 * kh + h))))
        for h in range(2):
            blocks.append(wblock(w_in, 0, cin(OFF_DZ + 128 * (2 * kh + h))))
    for d in range(16):
        blocks.append(wblock(w_in, 0, cin(OFF_G + 2048 + 128 * d)))
        blocks.append(wblock(w_dn, 0, cin(128 * d)))
        blocks.append(wblock(w_dn, 2048, cin(128 * d)))
    while len(blocks) % 4 != 0:
        blocks.append(blocks[-1])
    for cg in range(4):
        for j in range(4):
            blocks.append(wblock(w_out, 0, cin(512 * cg + 128 * j)))
    return np.ascontiguousarray(np.stack(blocks).reshape(len(blocks), 128, 2048))


N_ATT_BLK = 2 + 4 * 9 + 32
N_DN_BLK = 1 + 96 + 48


def consts_f32():
    p = np.arange(128)
    h = p // 64
    t = p % 64
    same = (h[:, None] == h[None, :])
    bdu = (same & (t[:, None] <= t[None, :])).astype(np.float32)
    bdsu = (same & (t[:, None] > t[None, :])).astype(np.float32)
    hsel0 = np.repeat((h == 0)[:, None], 128, 1).astype(np.float32)
    hsel1 = np.repeat((h == 1)[:, None], 128, 1).astype(np.float32)
    bdones = same.astype(np.float32)
    ident = np.eye(128, dtype=np.float32)
    i64 = np.arange(64)
    u2 = (t[:, None] <= i64[None, :]).astype(np.float32)
    ones64 = np.ones((128, 64), np.float32)
    masks = np.where(i64[None, :] < t[:, None], 0.0, NEG).astype(np.float32)
    maskt = np.where(i64[None, :] >= t[:, None], 0.0, NEG).astype(np.float32)
    parts = [ident, bdu, bdsu, hsel0, hsel1, bdones, -bdones, u2, ones64, -ones64, masks, maskt]
    return np.ascontiguousarray(np.concatenate(parts, axis=1))


CF_OFF = {}
_o = 0
for _n, _w in [("ident", 128), ("bdu", 128), ("bdsu", 128), ("hsel0", 128), ("hsel1", 128), ("bdones", 128),
               ("bdneg", 128), ("u2", 64), ("ones64", 64), ("neg64", 64), ("masks", 64), ("maskt", 64)]:
    CF_OFF[_n] = (_o, _w)
    _o += _w
NCF = _o

CB_OFF = {}
_o = 0
for _n, _w in [("ident", 128), ("mown", 128), ("mprev", 128), ("ones64", 64), ("onescol", 2), ("onesrow", 128),
               ("bv", 256), ("bdones", 128), ("bdneg", 128), ("neg64", 64), ("masks", 64), ("maskt", 64)]:
    CB_OFF[_n] = (_o, _w)
    _o += _w
NCB = _o


def consts_b16(b_qkv):
    j = np.arange(128)[:, None]
    i = np.arange(128)[None, :]
    ident = np.eye(128, dtype=np.float32)
    mown = (j <= i).astype(np.float32)
    mprev = (j > i).astype(np.float32)
    ones64 = np.ones((128, 64), np.float32)
    onescol = np.ones((128, 2), np.float32)
    onesrow = np.zeros((128, 128), np.float32)
    onesrow[0, :] = 1.0
    bv = np.zeros((128, 256), np.float32)
    bv[0, :] = b_qkv[2304:2560]
    cfull = consts_f32()
    extra = [cfull[:, CF_OFF[n][0]:CF_OFF[n][0] + CF_OFF[n][1]] for n in ("bdones", "bdneg", "neg64", "masks", "maskt")]
    return np.ascontiguousarray(np.concatenate([ident, mown, mprev, ones64, onescol, onesrow, bv] + extra, axis=1))


PR_OFF = {}
_o = 0
for _n, _w in [("nw", 16), ("bq", 16), ("bk", 4), ("sk", 16), ("cw", 256), ("dtb", 16), ("alg", 16), ("dnw", 1)]:
    PR_OFF[_n] = (_o, _w)
    _o += _w
NPR = _o


def params_f32(norm_w, b_qkv, sinks, conv_w, a_log, dt_bias, dn_norm_w):
    p = np.arange(128)
    hi = (p >= 64).astype(np.int64)
    nw = norm_w.reshape(16, 128).T
    bq = b_qkv[0:2048].reshape(16, 128).T
    bk = b_qkv[2048:2304].reshape(4, 64)[:, p % 64].T
    sk = sinks[(2 * np.arange(16))[None, :] + hi[:, None]]
    cw = conv_w.reshape(4, 64, 128).transpose(2, 1, 0).reshape(128, 256)
    dtb = dt_bias[(2 * np.arange(16))[None, :] + hi[:, None]]
    alg = a_log[(2 * np.arange(16))[None, :] + hi[:, None]]
    dnw = dn_norm_w.reshape(128, 1)
    return np.ascontiguousarray(np.concatenate([nw, bq, bk, sk, cw, dtb, alg, dnw], axis=1).astype(np.float32))


def build_program(n_wblk):
    nc = bass.Bass("TRN2", target_bir_lowering=False, dynamic_dma_scratch_size=DGE_SCRATCH)
    x_d = nc.dram_tensor("x", [S, D], F32, kind="ExternalInput").ap()
    w_d = nc.dram_tensor("wst", [n_wblk, 128, 2048], F32, kind="ExternalInput").ap()
    cf_d = nc.dram_tensor("cf", [128, NCF], F32, kind="ExternalInput").ap()
    cb_d = nc.dram_tensor("cb", [128, NCB], F32, kind="ExternalInput").ap()
    pr_d = nc.dram_tensor("pr", [128, NPR], F32, kind="ExternalInput").ap()
    fw_d = nc.dram_tensor("fw", [128, D], F32, kind="ExternalInput").ap()
    y_d = nc.dram_tensor("y", [S, D], F32, kind="ExternalOutput").ap()

    B = Builder()
    stack = contextlib.ExitStack()
    with stack:
        def sb(name, shape, dt):
            return stack.enter_context(nc.sbuf_tensor(name, shape, dt))

        def ps(name, shape=(128, 512), dt=F32):
            return stack.enter_context(nc.psum_tensor(name, list(shape), dt))

        ring = sb("ring", [128, NSLOT, 16, 128], BF16)
        hT = sb("hT", [128, 16, TT], BF16)
        m1T = sb("m1T", [128, 16, TT], BF16)
        XO = sb("XO", [128, 4, D], F32)
        XOb = XO[:].bitcast(BF16)
        FW = sb("FW", [128, D], F32)
        CF = sb("CF", [128, NCF], F32)
        CB = sb("CB", [128, NCB], BF16)
        PR = sb("PR", [128, NPR], F32)
        hb = sb("hb", [128, D], BF16)
        st1 = sb("st1", [128, 8], F32)
        qT = sb("qT", [128, 4, TT], BF16)
        zT = sb("zT", [128, 4, TT], BF16)
        kT2 = sb("kT2", [128, 4, 128 + TT], BF16)
        vtok = sb("vtok", [128, 5, 256], BF16)
        pT = sb("pT", [128, 4, 512], BF16)
        dnb = sb("dnb", [128, 512], F32)
        ogf = sb("ogf", [128, 512], F32)
        gsb = sb("gsb", [128, 1, 512], F32)
        ES = sb("ES", [128, 16], F32)
        NEGA = sb("NEGA", [128, 16], F32)

        S2 = sb("S2", [128, 16, 256], F32)
        S2b = sb("S2b", [128, 16, 256], BF16)
        HALO = sb("HALO", [128, 64, 4], BF16)
        BET = sb("BET", [128, 8, 16], F32)
        GST = sb("GST", [128, 8, 16], F32)
        EX = sb("EX", [128, 4, 8, 16], F32)
        CSb = [sb("CSb%d" % i, [128, 516], BF16) for i in range(2)]
        DIAG = [sb("DIAG%d" % i, [128, 4, 128], BF16) for i in range(2)]
        ACC = [sb("ACC0", [128, 512], F32)]
        XQKs = [sb("XQK%d" % i, [128, 2, TT], BF16) for i in range(2)]
        VT = [sb("VT%d" % i, [128, TT], BF16) for i in range(2)]
        ZSs = [[sb("ZS%d_%d" % (p_, i), [128, TT], BF16) for i in range(2)] for p_ in range(2)]
        RSs = [sb("RS%d" % i, [128, 8, 2], F32) for i in range(2)]
        SCs = [sb("SC%d" % i, [128, 5, 8], F32) for i in range(2)]
        XKD = sb("XKD", [128, 8, 128], BF16)
        VB = sb("VB", [128, 8, 128], F32)
        GP8h = sb("GP8h", [128, 8, 64], BF16)
        GP8l = sb("GP8l", [128, 8, 64], BF16)
        GHL = sb("GHL", [128, 2, 8], F32)
        GHb = sb("GHb", [128, 8], BF16)
        P0B8 = sb("P0B8", [128, 8, 128], BF16)
        Q0B8 = sb("Q0B8", [128, 8, 128], BF16)
        PQ8 = [sb("PQ8_%d" % i, [128, 2, 8, 128], BF16) for i in range(2)]
        Yb8 = sb("Yb8", [128, 8, 128], BF16)
        YT = sb("YT", [128, 8, 128], BF16)
        MQ = sb("MQ", [128, 8, 128], BF16)
        RB = sb("RB", [128, 128], BF16)
        VPB = sb("VPB", [128, 128], BF16)
        T1 = sb("T1", [128, 128], F32)
        OB = sb("OB", [128, 8, 128], F32)
        JK = sb("JK", [128, 128], BF16)
        SSO = sb("SSO", [128, 8, 4], F32)

        class V3:
            def __init__(self, base_ap, C, J):
                self.t = base_ap.tensor
                self.off = base_ap.offset
                self.ps = base_ap.ap[0][0]
                self.C, self.J = C, J

            def ap(self, p0=0, p1=128, c0=0, c1=None, j0=0, j1=None):
                c1 = self.C if c1 is None else c1
                j1 = self.J if j1 is None else j1
                dims = [[self.ps, p1 - p0]]
                if c1 - c0 > 1:
                    dims.append([self.J, c1 - c0])
                dims.append([1, j1 - j0])
                return AP(self.t, self.off + p0 * self.ps + c0 * self.J + j0, dims)

        _qf = qT[:].rearrange("p a b -> p (a b)")
        GBD8h = V3(_qf[:, 0:1024], 8, 128)
        GBD8l = V3(_qf[:, 1024:2048], 8, 128)
        _zf = zT[:].bitcast(F32).rearrange("p a b -> p (a b)")
        DSSa = V3(_zf[:, 0:512], 8, 64)
        DSSb = V3(_zf[:, 512:1024], 8, 64)
        Y32_8 = V3(pT[:].bitcast(F32), 8, 128)
        ON3 = V3(dnb[:].bitcast(BF16), 8, 128)
        SQ3 = V3(ogf[:].bitcast(BF16), 2, TT)

        banks = [ps("bank%d" % i) for i in range(8)]

        def cf(name):
            o, w = CF_OFF[name]
            return CF[:, o:o + w]

        def cb(name):
            o, w = CB_OFF[name]
            return CB[:, o:o + w]

        def pr(name, j=None, n=1):
            o, w = PR_OFF[name]
            if j is None:
                return PR[:, o:o + w]
            return PR[:, o + j:o + j + n]

        xob_t = XOb.tensor
        xob_ps = XOb.ap[0][0]

        def XOb_view(b, t0, n):
            return AP(xob_t, XOb.offset + b * TT + t0, [[xob_ps, 128], [1, n]])

        def XOb_view3(b0, nb, t0, n):
            return AP(xob_t, XOb.offset + b0 * TT + t0, [[xob_ps, 128], [TT, nb], [1, n]])

        B.op("sp", lambda e: e.dma_start(out=CF[:], in_=cf_d), w=["CF"], dma="c0")
        B.op("sp", lambda e: e.dma_start(out=PR[:], in_=pr_d), w=["PR"], dma="c1")
        B.op("sp", lambda e: e.dma_start(out=FW[:], in_=fw_d), w=["FW"], dma="c2")
        B.op("pool", lambda e: e.dma_start(out=CB[:], in_=cb_d), w=["CB"], dma="c3")
        B.op("act", lambda e: e.activation(out=ES[:], in_=pr("sk"), func=AF.Exp), r=["PR"], w=["ES"])
        B.op("act", lambda e: e.activation(out=NEGA[:], in_=pr("alg"), func=AF.Exp), r=["PR"], w=["NEGA"])
        B.op("dve", lambda e: e.tensor_scalar(out=NEGA[:], in0=NEGA[:], scalar1=-1.0, scalar2=None, op0=ALU.mult),
             r=["NEGA"], w=["NEGA"])
        B.op("pool", lambda e: e.memset(kT2[:], 0.0), w=["kT2"])
        B.op("pool", lambda e: e.memset(vtok[:], 0.0), w=["vtok"])
        B.op("pool", lambda e: e.memset(S2[:], 0.0), w=["S2"])
        B.op("pool", lambda e: e.memset(S2b[:], 0.0), w=["S2b"])
        B.op("pool", lambda e: e.memset(HALO[:], 0.0), w=["HALO"])
        B.op("pool", lambda e: e.memset(MQ[:], 0.0), w=["MQ"])
        B.op("pool", lambda e: e.memset(P0B8[:], 0.0), w=["P0B8"])

        wstate = {"next_load": 0, "next_use": 0}
        n_total = n_wblk * 0 + 0

        def w_load_upto(k):
            while wstate["next_load"] < k:
                i = wstate["next_load"]
                sl = i % NSLOT
                src = w_d[i % n_wblk]
                B.op("pool", (lambda e, sl=sl, src=src: e.dma_start(
                    out=ring[:, sl, :, :].rearrange("p c n -> p (c n)"), in_=src)),
                    w=["ring%d" % sl], dma="w%d" % sl)
                wstate["next_load"] += 1

        def w_next():
            i = wstate["next_use"]
            wstate["next_use"] += 1
            w_load_upto(i + 1)
            return i % NSLOT

        def w_prefetch():
            lim = min(wstate["next_use"] + NSLOT, n_wblk * NT_RUN)
            w_load_upto(min(lim, wstate["next_use"] + NSLOT))

        pbank = {"i": 0}

        def proj_fm(slot, rhs_fn, nk=16, bankset=(0, 1)):
            bi = bankset[pbank["i"] % len(bankset)]
            pbank["i"] += 1
            bk = banks[bi]
            for c in range(nk):
                B.op("pe", (lambda e, c=c, bk=bk: e.matmul(bk[:], lhsT=ring[:, slot, c, :], rhs=rhs_fn(c),
                                                           start=(c == 0), stop=(c == nk - 1))),
                     r=["ring%d" % slot, "hT"], w=["bank%d" % bi])
            return bi

        out_toks = []

        for T in range(NT_RUN):
            t0 = T * TT
            B.op("sp", lambda e, t0=t0: e.dma_start(
                out=XO[:], in_=x_d[t0:t0 + TT, :].rearrange("(t p) d -> p t d", p=128)), w=["XO"], dma="x")
            for tb in range(4):
                B.op("act", lambda e, tb=tb: e.activation(out=hb[:], in_=XO[:, tb, :], func=AF.Square,
                                                          accum_out=st1[:, 0:1]), r=["XO"], w=["hb", "st1"])
                B.op("dve", lambda e: e.tensor_scalar(out=st1[:, 1:2], in0=st1[:, 0:1], scalar1=1.0 / D, scalar2=EPS,
                                                      op0=ALU.mult, op1=ALU.add), r=["st1"], w=["st1b"])
                B.op("act", lambda e: e.activation(out=st1[:, 2:3], in_=st1[:, 1:2], func=AF.Sqrt),
                     r=["st1b"], w=["st1c"])
                B.op("dve", lambda e: e.reciprocal(out=st1[:, 3:4], in_=st1[:, 2:3]), r=["st1c"], w=["st1d"])
                B.op("dve", lambda e, tb=tb: e.tensor_scalar(out=hb[:], in0=XO[:, tb, :], scalar1=st1[:, 3:4],
                                                             scalar2=None, op0=ALU.mult),
                     r=["XO", "st1d"], w=["hb"])
                for c4 in range(4):
                    bi = 2 + (c4 % 2)
                    bkb = banks[bi][:].bitcast(BF16)
                    for cc in range(4):
                        c = c4 * 4 + cc
                        B.op("pe", lambda e, c=c, cc=cc, bkb=bkb: e.transpose(
                            out=bkb[:, cc * 128:(cc + 1) * 128], in_=hb[:, c * 128:(c + 1) * 128], identity=cb("ident")),
                            r=["hb", "CB"], w=["bank%d" % bi])
                    for cc in range(4):
                        c = c4 * 4 + cc
                        B.op("dve", lambda e, c=c, cc=cc, bkb=bkb, tb=tb: e.tensor_scalar(
                            out=hT[:, c, tb * 128:(tb + 1) * 128], in0=bkb[:, cc * 128:(cc + 1) * 128],
                            scalar1=pr("nw", c), scalar2=None, op0=ALU.mult),
                            r=["bank%d" % bi, "PR"], w=["hT"])

            if T > 0:
                B.op("pool", lambda e: e.tensor_copy(out=kT2[:, :, 0:128], in_=kT2[:, :, TT:TT + 128]),
                     r=["kT2"], w=["kT2"])
                B.op("pool", lambda e: e.tensor_copy(out=vtok[:, 0, :], in_=vtok[:, 4, :]), r=["vtok"], w=["vtok"])
            sv = [w_next(), w_next()]
            for tb in range(4):
                bi = 2 + (tb % 2)
                for half in range(2):
                    o0 = half * 128
                    B.op("pe", lambda e, bi=bi, o0=o0: e.matmul(
                        banks[bi][:, o0:o0 + 128], lhsT=CB[0:1, CB_OFF["onesrow"][0]:CB_OFF["onesrow"][0] + 128],
                        rhs=CB[0:1, CB_OFF["bv"][0] + o0:CB_OFF["bv"][0] + o0 + 128], start=True, stop=False),
                        r=["CB"], w=["bank%d" % bi])
                    for c in range(16):
                        B.op("pe", lambda e, bi=bi, o0=o0, c=c, tb=tb, sl=sv[half]: e.matmul(
                            banks[bi][:, o0:o0 + 128], lhsT=hT[:, c, tb * 128:(tb + 1) * 128], rhs=ring[:, sl, c, :],
                            start=False, stop=(c == 15)), r=["hT", "ring%d" % sv[half]], w=["bank%d" % bi])
                B.op("act", lambda e, bi=bi, tb=tb: e.copy(out=vtok[:, 1 + tb, :], in_=banks[bi][:, 0:256]),
                     r=["bank%d" % bi], w=["vtok"])
            w_prefetch()

            for g in range(4):
                sl = w_next()
                bi = proj_fm(sl, lambda c: hT[:, c, :])
                B.op("dve", lambda e, bi=bi, g=g: e.tensor_scalar(
                    out=kT2[:, g, 128:128 + TT], in0=banks[bi][:], scalar1=pr("bk", g), scalar2=None, op0=ALU.add),
                    r=["bank%d" % bi, "PR"], w=["kT2"])
                w_prefetch()
                for j in range(4):
                    sl = w_next()
                    bi = proj_fm(sl, lambda c: hT[:, c, :])
                    B.op("dve", lambda e, bi=bi, g=g, j=j: e.tensor_scalar(
                        out=qT[:, j, :], in0=banks[bi][:], scalar1=pr("bq", 4 * g + j), scalar2=0.125,
                        op0=ALU.add, op1=ALU.mult), r=["bank%d" % bi, "PR"], w=["qT"])
                    w_prefetch()
                for j in range(4):
                    sl = w_next()
                    bi = proj_fm(sl, lambda c: hT[:, c, :])
                    B.op("act", lambda e, bi=bi, j=j: e.activation(out=zT[:, j, :], in_=banks[bi][:], func=AF.Silu),
                         r=["bank%d" % bi], w=["zT"])
                    w_prefetch()
                for qb in range(4):
                    nglob = 4 * T + qb
                    kinds = ["own"] + (["prev"] if nglob > 0 else [])
                    for half in range(2):
                        pp = slice(half * 64, half * 64 + 64)
                        for ki, kind in enumerate(kinds):
                            bi = 2 + half * 2 + ki
                            kcol = 128 + qb * 128 if kind == "own" else qb * 128
                            B.op("pe", lambda e, bi=bi, pp=pp, kcol=kcol, g=g, qb=qb: e.matmul(
                                banks[bi][:], lhsT=kT2[pp, g, kcol:kcol + 128],
                                rhs=qT[pp, :, qb * 128:(qb + 1) * 128], start=True, stop=True),
                                r=["kT2", "qT"], w=["bank%d" % bi])
                            pi = half * 2 + ki
                            B.op("act", lambda e, bi=bi, pi=pi: e.activation(
                                out=pT[:, pi, :], in_=banks[bi][:], func=AF.Exp), r=["bank%d" % bi], w=["pT%d" % pi])
                            mk = cb("mown") if kind == "own" else cb("mprev")
                            mk3 = AP(mk.tensor, mk.offset, [list(mk.ap[0]), [0, 4], [1, 128]])
                            B.op("pool", lambda e, pi=pi, mk3=mk3: e.tensor_tensor(
                                out=pT[:, pi, :].rearrange("p (h q) -> p h q", h=4),
                                in0=pT[:, pi, :].rearrange("p (h q) -> p h q", h=4), in1=mk3, op=ALU.mult),
                                r=["pT%d" % pi, "CB"], w=["pT%d" % pi])
                    for half in range(2):
                        po = slice(half * 64, half * 64 + 64)
                        for ki, kind in enumerate(kinds):
                            pi = half * 2 + ki
                            vb_ = 1 + qb if kind == "own" else qb
                            B.op("pe", lambda e, po=po, pi=pi, vb_=vb_, g=g, ki=ki, nk=len(kinds): e.matmul(
                                banks[6][po, :], lhsT=vtok[:, vb_, g * 64:(g + 1) * 64], rhs=pT[:, pi, :],
                                start=(ki == 0), stop=(ki == nk - 1)), r=["vtok", "pT%d" % pi], w=["bank6"])
                        for ki, kind in enumerate(kinds):
                            pi = half * 2 + ki
                            B.op("pe", lambda e, po=po, pi=pi, ki=ki, nk=len(kinds): e.matmul(
                                banks[7][po, :], lhsT=cb("ones64"), rhs=pT[:, pi, :],
                                start=(ki == 0), stop=(ki == nk - 1)), r=["CB", "pT%d" % pi], w=["bank7"])
                    es3 = AP(ES, 4 * g, [[16, 128], [1, 4], [0, 128]])
                    B.op("dve", lambda e, es3=es3: e.tensor_tensor(
                        out=dnb[:].rearrange("p (h q) -> p h q", h=4),
                        in0=banks[7][:].rearrange("p (h q) -> p h q", h=4), in1=es3, op=ALU.add),
                        r=["bank7", "ES"], w=["dnb"])
                    B.op("dve", lambda e: e.reciprocal(out=dnb[:], in_=dnb[:]), r=["dnb"], w=["dnb"])
                    B.op("dve", lambda e: e.tensor_tensor(out=ogf[:], in0=banks[6][:], in1=dnb[:], op=ALU.mult),
                         r=["bank6", "dnb"], w=["ogf"])
                    B.op("dve", lambda e, g=g, qb=qb: e.tensor_tensor(
                        out=XOb_view3(4 * g, 4, qb * 128, 128), in0=ogf[:].rearrange("p (h q) -> p h q", h=4),
                        in1=zT[:, :, qb * 128:(qb + 1) * 128], op=ALU.mult), r=["ogf", "zT"], w=["XO"])

            for d in range(16):
                sg = w_next()
                big = proj_fm(sg, lambda c: hT[:, c, :], bankset=(0, 1))
                gi = 0
                B.op("act", lambda e, big=big, gi=gi: e.activation(out=gsb[:, gi, :], in_=banks[big][:],
                                                                   func=AF.Sigmoid),
                     r=["bank%d" % big], w=["gsb%d" % gi])
                w_prefetch()
                sw = w_next()
                bia = 2 + (d % 2)
                for c in range(16):
                    B.op("pe", lambda e, c=c, bia=bia, sw=sw: e.matmul(
                        banks[bia][:], lhsT=ring[:, sw, c, :], rhs=XOb_view(c, 0, TT), start=(c == 0), stop=(c == 15)),
                        r=["ring%d" % sw, "XO"], w=["bank%d" % bia])
                B.op("dve", lambda e, bia=bia, gi=gi, d=d: e.tensor_tensor(
                    out=m1T[:, d, :], in0=banks[bia][:], in1=gsb[:, gi, :], op=ALU.mult),
                    r=["bank%d" % bia, "gsb%d" % gi], w=["m1T"])
                w_prefetch()

            if ENABLE_DN:
                DK_SCALE = 128.0 ** -0.5

                def pview(base, dims):
                    return AP(base.tensor, base.offset, [list(base.ap[0])] + [list(d_) for d_ in dims])

                sba = w_next()
                for kc in range(16):
                    B.op("pe", lambda e, kc=kc, sba=sba: e.matmul(
                        banks[2][0:64, :], lhsT=ring[:, sba, kc, 0:64], rhs=hT[:, kc, :],
                        start=(kc == 0), stop=(kc == 15)), r=["hT", "ring%d" % sba], w=["bank2"])
                w_prefetch()
                B.op("act", lambda e: e.copy(out=ACC[0][0:64, :], in_=banks[2][0:64, :]), r=["bank2"], w=["ACC0"])
                B.mute = DN_SUB < 2
                for c in range(8):
                    for h in range(2):
                        B.op("pe", lambda e, c=c, h=h: e.matmul(
                            banks[4][h * 64:(h + 1) * 64, c * 64:(c + 1) * 64], lhsT=ACC[0][0:64, c * 64:(c + 1) * 64],
                            rhs=CF[0:64, CF_OFF["ident"][0]:CF_OFF["ident"][0] + 64], start=True, stop=True),
                            r=["ACC0", "CF"], w=["bank4"])
                B.mute = DN_SUB < 3
                for h in range(2):
                    hp = slice(h * 64, (h + 1) * 64)
                    B.op("act", lambda e, h=h, hp=hp: e.activation(
                        out=BET[hp, :, :], in_=pview(banks[4][hp, h:h + 1], [[64, 8], [2, 16]]), func=AF.Sigmoid),
                        r=["bank4"], w=["BET"])
                    B.op("dve", lambda e, h=h, hp=hp: e.tensor_tensor(
                        out=GST[hp, :, :], in0=pview(banks[4][hp, 32 + h:33 + h], [[64, 8], [2, 16]]),
                        in1=pview(PR[hp, PR_OFF["dtb"][0]:PR_OFF["dtb"][0] + 1], [[0, 8], [1, 16]]), op=ALU.add),
                        r=["bank4", "PR"], w=["GST"])
                B.mute = DN_SUB < 4
                B.op("act", lambda e: e.activation(out=GST[:], in_=GST[:], func=AF.Exp), r=["GST"], w=["GST"])
                B.op("dve", lambda e: e.tensor_scalar(out=GST[:], in0=GST[:], scalar1=1.0, scalar2=None, op0=ALU.add),
                     r=["GST"], w=["GST"])
                B.op("act", lambda e: e.activation(out=GST[:], in_=GST[:], func=AF.Ln), r=["GST"], w=["GST"])
                B.op("dve", lambda e: e.tensor_tensor(
                    out=GST[:], in0=GST[:], in1=pview(NEGA[:, 0:1], [[0, 8], [1, 16]]), op=ALU.mult),
                    r=["GST", "NEGA"], w=["GST"])
                B.mute = DN_SUB < 5
                gflat = GST[:].rearrange("p c k -> p (c k)")
                for qi, nm in enumerate(["bdu", "bdsu", "hsel0", "hsel1"]):
                    B.op("pe", lambda e, qi=qi, nm=nm: e.matmul(
                        banks[3][:, qi * 128:(qi + 1) * 128], lhsT=cf(nm), rhs=gflat, start=True, stop=True),
                        r=["CF", "GST"], w=["bank3"])
                B.op("act", lambda e: e.activation(out=EX[:].rearrange("p a c k -> p (a c k)"), in_=banks[3][:],
                                                   func=AF.Exp), r=["bank3"], w=["EX"])

                B.mute = DN_SUB < 6
                def proj_fm_gen(slot):
                    bi = (0, 1)[pbank["i"] % 2]
                    pbank["i"] += 1
                    bk = banks[bi]
                    for c in range(16):
                        B.op("pe", (lambda e, c=c, bk=bk: e.matmul(bk[:], lhsT=ring[:, slot, c, :], rhs=hT[:, c, :],
                                                                   start=(c == 0), stop=(c == 15))),
                             r=["ring%d" % slot, "hT"], w=["bank%d" % bi])
                        if c % 4 == 3 and c != 15:
                            yield
                    return bi

                def step_bg(bg):
                    if bg is None:
                        return False
                    try:
                        next(bg)
                        return True
                    except StopIteration:
                        return False

                def gen_proj(kh, XQK, ZS, KQ, KK_, KZ, RS, SC, KRS, KSC, lag=True):
                    deferred = []
                    stepno = [0]

                    def tick():
                        stepno[0] += 1
                        for d_ in [d_ for d_ in deferred if d_[0] <= stepno[0]]:
                            d_[1]()
                            deferred.remove(d_)

                    def later(n, fn):
                        if lag:
                            deferred.append((stepno[0] + n, fn))
                        else:
                            fn()

                    order = [("q", kh, XQK[:, 1, :], KQ), ("k", 16 + kh, XQK[:, 0, :], KK_),
                             ("v0", 32 + 2 * kh, VT[0][:], "VT0"), ("v1", 33 + 2 * kh, VT[1][:], "VT1"),
                             ("z0", None, ZS[0][:], KZ[0]), ("z1", None, ZS[1][:], KZ[1])]
                    for oi, (nm, cblk, dst, dkey) in enumerate(order):
                        sl = w_next()
                        bi = (0, 1)[pbank["i"] % 2]
                        pbank["i"] += 1
                        ci = oi % 2

                        if cblk is not None:
                            B.op("pool", lambda e, ci=ci, cblk=cblk: e.tensor_tensor(
                                out=DIAG[ci][:], in0=pview(cb("ident")[:, 0:1], [[0, 4], [1, 128]]),
                                in1=pview(pr("cw", cblk * 4), [[1, 4], [0, 128]]), op=ALU.mult),
                                r=["CB", "PR"], w=["DIAG%d" % ci])

                        def evac(bi=bi, ci=ci, cblk=cblk):
                            B.op("act", lambda e: e.copy(out=CSb[ci][:, 3:515], in_=banks[bi][:]),
                                 r=["bank%d" % bi], w=["CSm%d" % ci])
                            B.op("dve", lambda e: e.tensor_copy(out=CSb[ci][:, 0:3], in_=HALO[:, cblk, 0:3]),
                                 r=["HALO%d" % cblk], w=["CSh%d" % ci])
                            B.op("dve", lambda e: e.tensor_copy(out=HALO[:, cblk, 0:3], in_=CSb[ci][:, 512:515]),
                                 r=["CSm%d" % ci], w=["HALO%d" % cblk])

                        def conv(ci=ci, cblk=cblk):
                            for ti in range(4):
                                B.op("pe", lambda e, ti=ti: e.matmul(
                                    banks[2][:], lhsT=DIAG[ci][:, ti, :], rhs=CSb[ci][:, ti:ti + 512],
                                    start=(ti == 0), stop=(ti == 3)),
                                    r=["DIAG%d" % ci, "CSm%d" % ci, "CSh%d" % ci], w=["bank2"])

                        def silu_c(ci=ci, dst=dst, dkey=dkey):
                            B.op("act", lambda e: e.activation(out=dst, in_=banks[2][:], func=AF.Silu),
                                 r=["bank2"], w=[dkey])

                        def silu_z(bi=bi, dst=dst, dkey=dkey):
                            B.op("act", lambda e: e.activation(out=dst, in_=banks[bi][:], func=AF.Silu),
                                 r=["bank%d" % bi], w=[dkey])

                        for c in range(16):
                            B.op("pe", (lambda e, c=c, bi=bi, sl=sl: e.matmul(
                                banks[bi][:], lhsT=ring[:, sl, c, :], rhs=hT[:, c, :], start=(c == 0), stop=(c == 15))),
                                r=["ring%d" % sl, "hT"], w=["bank%d" % bi])
                            if c % 4 == 3:
                                if c == 15:
                                    w_prefetch()
                                    if cblk is not None:
                                        later(0, evac)
                                        later(1, conv)
                                        later(2, silu_c)
                                    else:
                                        later(2, silu_z)
                                tick()
                                yield
                    while deferred:
                        tick()
                        yield
                    B.op("pool", lambda e: e.tensor_tensor(out=SQ3.ap(), in0=XQK[:], in1=XQK[:], op=ALU.mult),
                         r=[KQ, KK_], w=["ogf"])
                    for c in range(8):
                        for qi in range(2):
                            for h in range(2):
                                B.op("pe", lambda e, c=c, qi=qi, h=h: e.matmul(
                                    banks[7][h * 64:(h + 1) * 64, 256 + c * 2 + qi:256 + c * 2 + qi + 1],
                                    lhsT=SQ3.ap(c0=1 - qi, c1=2 - qi, j0=c * 64, j1=(c + 1) * 64),
                                    rhs=CB[:, CB_OFF["onescol"][0]:CB_OFF["onescol"][0] + 1], start=True, stop=True),
                                    r=["ogf", "CB"], w=["bank7"])
                    B.op("dve", lambda e: e.tensor_scalar(
                        out=RS[:].rearrange("p c q -> p (c q)"), in0=banks[7][:, 256:272], scalar1=EPS, scalar2=None,
                        op0=ALU.add), r=["bank7"], w=[KRS])
                    B.op("act", lambda e: e.activation(out=RS[:], in_=RS[:], func=AF.Ln), r=[KRS], w=[KRS])
                    B.op("act", lambda e: e.activation(out=RS[:], in_=RS[:], func=AF.Exp, scale=-0.5), r=[KRS], w=[KRS])
                    rq = RS[:, :, 0]
                    rk = RS[:, :, 1]
                    betk = BET[:, :, kh]
                    egck = EX[:, 0, :, kh]
                    C1, CA, C2, CO, COE = (SC[:, i, :] for i in range(5))
                    B.op("dve", lambda e, betk=betk: e.tensor_tensor(out=C1, in0=betk, in1=rk, op=ALU.mult),
                         r=["BET", KRS], w=[KSC[0]])
                    B.op("dve", lambda e: e.tensor_tensor(out=CA, in0=C1, in1=rk, op=ALU.mult),
                         r=[KSC[0], KRS], w=[KSC[1]])
                    B.op("dve", lambda e, egck=egck: e.scalar_tensor_tensor(out=C2, in0=CA, scalar=-1.0, in1=egck,
                                                                 op0=ALU.mult, op1=ALU.mult),
                         r=[KSC[1], "EX"], w=[KSC[2]])
                    B.op("dve", lambda e: e.tensor_scalar(out=CO, in0=rq, scalar1=DK_SCALE, scalar2=None, op0=ALU.mult),
                         r=[KRS], w=[KSC[3]])
                    B.op("dve", lambda e, egck=egck: e.tensor_tensor(out=COE, in0=CO, in1=egck, op=ALU.mult),
                         r=[KSC[3], "EX"], w=[KSC[4]])


                    yield

                def kh_body(kh, XQK, ZS, KQ, KK_, KZ, RS, SC, KRS, KSC, bg):
                    if DN_LEVEL >= 5:
                        KEY_GBD = ["qT"]
                        KEY_DSS = ["zT"]
                        KEY_Y32 = ["pT0", "pT1", "pT2", "pT3"]
                        gb = pview(GST[:, 0, kh:kh + 1], [[16, 8]])
                        B.op("dve", lambda e, gb=gb: e.tensor_copy(out=GHb[:], in_=gb), r=["GST"], w=["GHb"])
                        B.op("dve", lambda e, gb=gb: e.tensor_tensor(out=GHL[:, 1, :], in0=gb, in1=GHb[:], op=ALU.subtract),
                             r=["GST", "GHb"], w=["GHL"])
                        for (dst, dkey, cname, width, src, skey) in [
                                (GP8h[:], "GP8h", "u2", 64, GHb, "GHb"), (GP8l[:], "GP8l", "u2", 64, None, "GHL"),
                                (GBD8h.ap(), "qT", "bdu", 128, GHb, "GHb"), (GBD8l.ap(), "qT", "bdu", 128, None, "GHL")]:
                            if src is not None:
                                in1 = pview(GHb[:, 0:1], [[1, 8], [0, width]])
                            else:
                                in1 = pview(GHL[:, 1, 0:1], [[1, 8], [0, width]])
                            B.op("dve", lambda e, dst=dst, cname=cname, width=width, in1=in1: e.tensor_tensor(
                                out=dst, in0=pview(cf(cname)[:, 0:1], [[0, 8], [1, width]]), in1=in1, op=ALU.mult),
                                r=["CF", skey], w=[dkey])
                        for c in range(8):
                            bi = 2 + c // 4
                            co_ = (c % 4) * 128
                            for h in range(2):
                                B.op("pe", lambda e, c=c, h=h, bi=bi, co_=co_: e.matmul(
                                    banks[bi][h * 64:(h + 1) * 64, co_:co_ + 128], lhsT=XQK[:, 0, c * 64:(c + 1) * 64],
                                    rhs=XQK[:, :, c * 64:(c + 1) * 64], start=True, stop=True),
                                    r=[KK_, KQ], w=["bank%d" % bi])
                        for (bi, cconst, cmask, cones) in [(4, "bdneg", "masks", "ones64"), (5, "bdones", "maskt", "neg64")]:
                            bk_ = "bank%d" % bi
                            B.op("pe", lambda e, bi=bi, cconst=cconst: e.matmul(
                                banks[bi][:], lhsT=cb(cconst), rhs=GP8h[:], start=True, stop=False, skip_group_check=True),
                                r=["CB", "GP8h"], w=[bk_])
                            B.op("pe", lambda e, bi=bi, cconst=cconst: e.matmul(
                                banks[bi][:], lhsT=cb(cconst), rhs=GP8l[:], start=False, stop=False, skip_group_check=True),
                                r=["CB", "GP8l"], w=[bk_])
                            B.op("pe", lambda e, bi=bi, cmask=cmask: e.matmul(
                                banks[bi][:], lhsT=cb("ident"), rhs=pview(cb(cmask)[:, 0:1], [[0, 8], [1, 64]]),
                                start=False, stop=False, skip_group_check=True), r=["CB"], w=[bk_])
                            for c in range(8):
                                B.op("pe", lambda e, bi=bi, c=c, cones=cones: e.matmul(
                                    banks[bi][:, c * 64:(c + 1) * 64], lhsT=GBD8h.ap(c0=c, c1=c + 1), rhs=cb(cones),
                                    start=False, stop=False, skip_group_check=True), r=["CB", "qT"], w=[bk_])
                                B.op("pe", lambda e, bi=bi, c=c, cones=cones: e.matmul(
                                    banks[bi][:, c * 64:(c + 1) * 64], lhsT=GBD8l.ap(c0=c, c1=c + 1), rhs=cb(cones),
                                    start=False, stop=(c == 7), skip_group_check=True), r=["CB", "qT"], w=[bk_])
                        B.op("act", lambda e: e.activation(out=DSSa.ap(), in_=banks[4][:].rearrange("p (c j) -> p c j", c=8),
                                                           func=AF.Exp), r=["bank4"], w=KEY_DSS)
                        B.op("act", lambda e: e.activation(out=DSSb.ap(), in_=banks[5][:].rearrange("p (c j) -> p c j", c=8),
                                                           func=AF.Exp), r=["bank5"], w=KEY_DSS)
                        B.op("dve", lambda e: e.tensor_tensor(
                            out=DSSa.ap(), in0=DSSa.ap(), in1=pview(SC[:, 1, 0:1], [[1, 8], [0, 64]]), op=ALU.mult),
                            r=KEY_DSS + [KSC[1]], w=KEY_DSS)
                        for b2 in range(2):
                            for h in range(2):
                                hp = slice(h * 64, (h + 1) * 64)
                                B.op("dve", lambda e, b2=b2, h=h, hp=hp: e.tensor_tensor(
                                    out=P0B8[hp, 4 * b2:4 * b2 + 4, h * 64:(h + 1) * 64],
                                    in0=pview(banks[2 + b2][hp, 0:1], [[128, 4], [1, 64]]),
                                    in1=DSSa.ap(p0=h * 64, p1=(h + 1) * 64, c0=4 * b2, c1=4 * b2 + 4),
                                    op=ALU.mult), r=["bank%d" % (2 + b2)] + KEY_DSS, w=["P0B8"])
                                B.op("dve", lambda e, b2=b2, h=h, hp=hp: e.tensor_tensor(
                                    out=MQ[hp, 4 * b2:4 * b2 + 4, h * 64:(h + 1) * 64],
                                    in0=pview(banks[2 + b2][hp, 64:65], [[128, 4], [1, 64]]),
                                    in1=DSSb.ap(p0=h * 64, p1=(h + 1) * 64, c0=4 * b2, c1=4 * b2 + 4),
                                    op=ALU.mult), r=["bank%d" % (2 + b2)] + KEY_DSS, w=["MQ"])
                        for c in range(8):
                            bi = 6 + c // 4
                            co_ = (c % 4) * 128
                            B.op("pe", lambda e, c=c, bi=bi, co_=co_: e.matmul(
                                banks[bi][:, co_:co_ + 128], lhsT=P0B8[:, c, :], rhs=cb("ident"), start=True, stop=True),
                                r=["P0B8", "CB"], w=["bank%d" % bi])
                        for b2 in range(2):
                            B.op("act", lambda e, b2=b2: e.copy(
                                out=Q0B8[:, 4 * b2:4 * b2 + 4, :], in_=banks[6 + b2][:].rearrange("p (c j) -> p c j", c=4)),
                                r=["bank%d" % (6 + b2)], w=["Q0B8"])
                            B.op("dve", lambda e, b2=b2: e.tensor_tensor(
                                out=Y32_8.ap(c0=4 * b2, c1=4 * b2 + 4), in0=pview(cf("ident")[:, 0:1], [[0, 4], [1, 128]]),
                                in1=banks[6 + b2][:].rearrange("p (c j) -> p c j", c=4), op=ALU.subtract),
                                r=["bank%d" % (6 + b2), "CF"], w=KEY_Y32)
                        B.op("act", lambda e: e.copy(out=Yb8[:], in_=Y32_8.ap()), r=KEY_Y32, w=["Yb8"])
                        def PQk(k):
                            if k == 0:
                                return (lambda c: P0B8[:, c, :]), (lambda c: Q0B8[:, c, :]), ["P0B8", "Q0B8"]
                            t_ = PQ8[k % 2]
                            return (lambda c: t_[:, 0, c, :]), (lambda c: t_[:, 1, c, :]), ["PQ8_%d" % (k % 2)]

                        def squaring(k):
                            Pk, Qk, pk_keys = PQk(k)
                            nxt = PQ8[(k + 1) % 2]
                            nk_ = "PQ8_%d" % ((k + 1) % 2)
                            for c in range(8):
                                bi = 2 + c // 4
                                co_ = (c % 4) * 128
                                B.op("pe", lambda e, c=c, bi=bi, co_=co_: e.matmul(
                                    banks[bi][:, co_:co_ + 128], lhsT=Qk(c), rhs=Pk(c), start=True, stop=True),
                                    r=pk_keys, w=["bank%d" % bi])
                            if k < 4:
                                for c in range(8):
                                    bi = 4 + c // 4
                                    co_ = (c % 4) * 128
                                    B.op("pe", lambda e, c=c, bi=bi, co_=co_: e.matmul(
                                        banks[bi][:, co_:co_ + 128], lhsT=Pk(c), rhs=Qk(c), start=True, stop=True),
                                        r=pk_keys, w=["bank%d" % bi])
                            for b2 in range(2):
                                B.op("act", lambda e, b2=b2: e.copy(
                                    out=nxt[:, 0, 4 * b2:4 * b2 + 4, :],
                                    in_=banks[2 + b2][:].rearrange("p (c j) -> p c j", c=4)),
                                    r=["bank%d" % (2 + b2)], w=[nk_])
                            if k < 4:
                                for b2 in range(2):
                                    B.op("dve", lambda e, b2=b2: e.tensor_copy(
                                        out=nxt[:, 1, 4 * b2:4 * b2 + 4, :],
                                        in_=banks[4 + b2][:].rearrange("p (c j) -> p c j", c=4)),
                                        r=["bank%d" % (4 + b2)], w=[nk_])

                        def apply(k):
                            nxt = PQ8[(k + 1) % 2]
                            nk_ = "PQ8_%d" % ((k + 1) % 2)
                            for c in range(8):
                                bi = 6 + c // 4
                                co_ = (c % 4) * 128
                                B.op("pe", lambda e, c=c, bi=bi, co_=co_: e.matmul(
                                    banks[bi][:, co_:co_ + 128], lhsT=nxt[:, 0, c, :], rhs=Yb8[:, c, :],
                                    start=True, stop=True), r=[nk_, "Yb8"], w=["bank%d" % bi])
                            for b2 in range(2):
                                B.op("dve", lambda e, b2=b2: e.tensor_tensor(
                                    out=Y32_8.ap(c0=4 * b2, c1=4 * b2 + 4),
                                    in0=banks[6 + b2][:].rearrange("p (c j) -> p c j", c=4),
                                    in1=Y32_8.ap(c0=4 * b2, c1=4 * b2 + 4), op=ALU.add),
                                    r=["bank%d" % (6 + b2)] + KEY_Y32, w=KEY_Y32)
                            if k < 4:
                                B.op("act", lambda e: e.copy(out=Yb8[:], in_=Y32_8.ap()), r=KEY_Y32, w=["Yb8"])
                            else:
                                B.op("act", lambda e: e.copy(out=YT[:], in_=Y32_8.ap()), r=KEY_Y32, w=["YT"])

                        squaring(0)
                        for (srcs, dst, dkey, in1f, skeys) in [
                                (lambda c, h: XQK[:, 0, c * 64:(c + 1) * 64], XKD, "XKD",
                                 lambda b2: pview(EX[:, 1, 4 * b2, kh:kh + 1], [[16, 4], [0, 128]]), [KK_, "EX"]),
                                (lambda c, h: VT[h][:, c * 64:(c + 1) * 64], VB, "VB",
                                 lambda b2: pview(SC[:, 0, 4 * b2:4 * b2 + 1], [[1, 4], [0, 128]]), ["VT0", "VT1", KSC[0]])]:
                            for b2 in range(2):
                                bi = b2
                                for cc in range(4):
                                    c = 4 * b2 + cc
                                    for h in range(2):
                                        B.op("pe", lambda e, c=c, cc=cc, h=h, bi=bi, srcs=srcs: e.matmul(
                                            banks[bi][h * 64:(h + 1) * 64, cc * 128:(cc + 1) * 128], lhsT=srcs(c, h),
                                            rhs=cb("ident"), start=True, stop=True), r=skeys[:-1] + ["CB"], w=["bank%d" % bi])
                                B.op("dve", lambda e, b2=b2, bi=bi, dst=dst, in1f=in1f: e.tensor_tensor(
                                    out=dst[:, 4 * b2:4 * b2 + 4, :], in0=banks[bi][:].rearrange("p (c j) -> p c j", c=4),
                                    in1=in1f(b2), op=ALU.mult), r=["bank%d" % bi, skeys[-1]], w=[dkey])
                        for k in range(1, 5):
                            squaring(k)
                            apply(k - 1)
                        apply(4)
                    if DN_LEVEL >= 6:
                        for c in range(8):
                            for h in range(2):
                                hp = slice(h * 64, (h + 1) * 64)
                                B.op("pe", lambda e, c=c, h=h, hp=hp, kh=kh: e.matmul(
                                    banks[6][hp, 0:128], lhsT=XQK[:, 0, c * 64:(c + 1) * 64],
                                    rhs=S2b[:, kh, h * 128:(h + 1) * 128], start=True, stop=True),
                                    r=[KK_, "S2b"], w=["bank6"])
                                B.op("pe", lambda e, c=c, h=h, hp=hp, kh=kh: e.matmul(
                                    banks[6][hp, 128:256], lhsT=XQK[:, 1, c * 64:(c + 1) * 64],
                                    rhs=S2b[:, kh, h * 128:(h + 1) * 128], start=True, stop=True),
                                    r=[KQ, "S2b"], w=["bank6"])
                            step_bg(bg)
                            B.op("dve", lambda e, c=c: e.scalar_tensor_tensor(
                                out=RB[:], in0=banks[6][:, 0:128], scalar=SC[:, 2, c:c + 1], in1=VB[:, c, :],
                                op0=ALU.mult, op1=ALU.add), r=["bank6", KSC[2], "VB"], w=["RB"])
                            B.op("pe", lambda e, c=c: e.matmul(banks[5][:, 0:128], lhsT=YT[:, c, :], rhs=RB[:],
                                                               start=True, stop=True), r=["YT", "RB"], w=["bank5"])
                            step_bg(bg)
                            B.op("act", lambda e: e.copy(out=VPB[:], in_=banks[5][:, 0:128]), r=["bank5"], w=["VPB"])
                            B.op("pe", lambda e, c=c: e.matmul(banks[7][:, 0:128], lhsT=MQ[:, c, :], rhs=VPB[:],
                                                               start=True, stop=True), r=["MQ", "VPB"], w=["bank7"])
                            B.op("act", lambda e, c=c: e.activation(out=T1[:], in_=banks[6][:, 128:256], func=AF.Identity,
                                                                    scale=SC[:, 4, c:c + 1]),
                                 r=["bank6", KSC[4]], w=["T1"])
                            B.op("dve", lambda e, c=c: e.scalar_tensor_tensor(
                                out=OB[:, c, :], in0=banks[7][:, 0:128], scalar=SC[:, 3, c:c + 1], in1=T1[:],
                                op0=ALU.mult, op1=ALU.add), r=["bank7", KSC[3], "T1"], w=["OB"])
                            sbk = [banks[3][:, 0:128], banks[4][:, 0:128]]
                            sbk_key = ["bank3", "bank4"]
                            for h in range(2):
                                hp = slice(h * 64, (h + 1) * 64)
                                B.op("pe", lambda e, c=c, h=h, hp=hp, sbk=sbk: e.matmul(
                                    sbk[h], lhsT=XKD[hp, c, :], rhs=VPB[hp, :],
                                    start=True, stop=True), r=["XKD", "VPB"], w=[sbk_key[h]])
                            step_bg(bg)
                            for h in range(2):
                                B.op("dve", lambda e, c=c, h=h, kh=kh, sbk=sbk: e.scalar_tensor_tensor(
                                    out=S2b[:, kh, h * 128:(h + 1) * 128], in0=S2[:, kh, h * 128:(h + 1) * 128],
                                    scalar=EX[:, 2 + h, c, kh:kh + 1], in1=sbk[h],
                                    op0=ALU.mult, op1=ALU.add), r=["S2", "EX", sbk_key[h]], w=["S2b"])
                            for h in range(2):
                                B.op("dve", lambda e, c=c, h=h, kh=kh, sbk=sbk: e.scalar_tensor_tensor(
                                    out=S2[:, kh, h * 128:(h + 1) * 128], in0=S2[:, kh, h * 128:(h + 1) * 128],
                                    scalar=EX[:, 2 + h, c, kh:kh + 1], in1=sbk[h],
                                    op0=ALU.mult, op1=ALU.add), r=["S2", "EX", sbk_key[h]], w=["S2"])
                            B.op("act", lambda e, c=c: e.activation(out=JK[:], in_=OB[:, c, :], func=AF.Square,
                                                                    accum_out=SSO[:, c, 0:1]), r=["OB"], w=["JK", "SSO"])

                    while step_bg(bg):
                        pass
                    if DN_LEVEL >= 7:
                        B.op("dve", lambda e: e.tensor_scalar(out=SSO[:, :, 1], in0=SSO[:, :, 0], scalar1=1.0 / 128.0,
                                                              scalar2=EPS, op0=ALU.mult, op1=ALU.add),
                             r=["SSO"], w=["SSO1"])
                        B.op("act", lambda e: e.activation(out=SSO[:, :, 2], in_=SSO[:, :, 1], func=AF.Ln),
                             r=["SSO1"], w=["SSO2"])
                        B.op("act", lambda e: e.activation(out=SSO[:, :, 3], in_=SSO[:, :, 2], func=AF.Exp, scale=-0.5),
                             r=["SSO2"], w=["SSO3"])
                        B.op("dve", lambda e: e.tensor_tensor(
                            out=ON3.ap(), in0=OB[:], in1=pview(SSO[:, 0, 3:4], [[4, 8], [0, 128]]), op=ALU.mult),
                            r=["OB", "SSO3"], w=["dnb"])
                        for c4 in range(2):
                            bi = 4 + c4
                            bkb = banks[bi][:].bitcast(BF16)
                            for cc in range(4):
                                c = c4 * 4 + cc
                                B.op("pe", lambda e, c=c, cc=cc, bkb=bkb: e.transpose(
                                    out=bkb[:, cc * 128:(cc + 1) * 128], in_=ON3.ap(c0=c, c1=c + 1), identity=cb("ident")),
                                    r=["dnb", "CB"], w=["bank%d" % bi])
                            for h in range(2):
                                B.op("dve", lambda e, bkb=bkb, h=h, c4=c4, kh=kh: e.scalar_tensor_tensor(
                                    out=XOb_view(2 * kh + h, c4 * 256, 256).rearrange("p (c i) -> p c i", c=4),
                                    in0=pview(bkb[:, h * 64:h * 64 + 1], [[128, 4], [1, 64]]), scalar=pr("dnw", 0),
                                    in1=ZS[h][:, c4 * 256:(c4 + 1) * 256].rearrange("p (c i) -> p c i", c=4),
                                    op0=ALU.mult, op1=ALU.mult), r=["bank%d" % bi, "PR", KZ[h]], w=["XO"])


                def pars(kh):
                    p_ = kh % 2
                    return (XQKs[p_], ZSs[p_], "XQKq%d" % p_, "XQKk%d" % p_, ["ZS%d_%d" % (p_, h) for h in range(2)],
                            RSs[p_], SCs[p_], "RS%d" % p_, ["SC%d_%d" % (p_, i) for i in range(5)])

                g0 = gen_proj(0, *pars(0), lag=False)
                while step_bg(g0):
                    pass
                for kh in range(16):
                    bg = gen_proj(kh + 1, *pars(kh + 1)) if kh < 15 else None
                    kh_body(kh, *pars(kh), bg)

                B.mute = DN_SUB < 7
                for d in range(16):
                    sg = w_next()
                    big = proj_fm(sg, lambda c: hT[:, c, :], bankset=(0, 1))
                    gi = 0
                    B.op("act", lambda e, big=big, gi=gi: e.activation(out=gsb[:, gi, :], in_=banks[big][:],
                                                                       func=AF.Sigmoid),
                         r=["bank%d" % big], w=["gsb%d" % gi])
                    w_prefetch()
                    s1 = w_next()
                    s2 = w_next()
                    bia = 2 + (d % 2)
                    for c in range(32):
                        sl_ = s1 if c < 16 else s2
                        B.op("pe", lambda e, c=c, bia=bia, sl_=sl_: e.matmul(
                            banks[bia][:], lhsT=ring[:, sl_, c % 16, :], rhs=XOb_view(c, 0, TT),
                            start=(c == 0), stop=(c == 31)), r=["ring%d" % sl_, "XO"], w=["bank%d" % bia])
                    B.op("dve", lambda e, bia=bia, gi=gi: e.tensor_tensor(
                        out=gsb[:, gi, :], in0=banks[bia][:], in1=gsb[:, gi, :], op=ALU.mult),
                        r=["bank%d" % bia, "gsb%d" % gi], w=["gsb%d" % gi])
                    B.op("pool", lambda e, gi=gi, d=d: e.tensor_tensor(
                        out=m1T[:, d, :], in0=gsb[:, gi, :], in1=m1T[:, d, :], op=ALU.add),
                        r=["gsb%d" % gi, "m1T"], w=["m1T"])
                    w_prefetch()
            else:
                for _ in range(N_DN_BLK):
                    w_next()
                    w_prefetch()
            B.mute = False
            while wstate["next_use"] % 4 != 0:
                w_next()
                w_prefetch()

            if SKIP_P3:
                continue
            B.op("sp", lambda e, t0=t0: e.dma_start(
                out=XO[:], in_=x_d[t0:t0 + TT, :].rearrange("(t p) d -> p t d", p=128)), r=[], w=["XO"], dma="x")
            for cg in range(4):
                sls = [w_next() for _ in range(4)]
                assert sls[0] % 4 == 0 and sls == list(range(sls[0], sls[0] + 4))
                for tb in range(4):
                    bi = 2 + ((cg * 4 + tb) % 4)
                    for c in range(16):
                        rhs = AP(ring, ring[:, sls[0], c, :].offset, [list(ring[:].ap[0]), [16 * 128, 4], [1, 128]])
                        B.op("pe", lambda e, bi=bi, c=c, tb=tb, rhs=rhs: e.matmul(
                            banks[bi][:], lhsT=m1T[:, c, tb * 128:(tb + 1) * 128], rhs=rhs,
                            start=(c == 0), stop=(c == 15)),
                            r=["m1T"] + ["ring%d" % s_ for s_ in sls], w=["bank%d" % bi])
                    B.op("dve", lambda e, bi=bi, tb=tb, cg=cg: e.tensor_tensor(
                        out=XO[:, tb, cg * 512:(cg + 1) * 512], in0=banks[bi][:], in1=XO[:, tb, cg * 512:(cg + 1) * 512],
                        op=ALU.add), r=["bank%d" % bi, "XO"], w=["XO"])
                w_prefetch()
            for tb in range(4):
                B.op("act", lambda e, tb=tb: e.activation(out=hb[:], in_=XO[:, tb, :], func=AF.Square,
                                                          accum_out=st1[:, 0:1]), r=["XO"], w=["hb", "st1"])
                B.op("dve", lambda e: e.tensor_scalar(out=st1[:, 1:2], in0=st1[:, 0:1], scalar1=1.0 / D, scalar2=EPS,
                                                      op0=ALU.mult, op1=ALU.add), r=["st1"], w=["st1b"])
                B.op("act", lambda e: e.activation(out=st1[:, 2:3], in_=st1[:, 1:2], func=AF.Sqrt),
                     r=["st1b"], w=["st1c"])
                B.op("dve", lambda e: e.reciprocal(out=st1[:, 3:4], in_=st1[:, 2:3]), r=["st1c"], w=["st1d"])
                B.op("dve", lambda e, tb=tb: e.scalar_tensor_tensor(
                    out=XO[:, tb, :], in0=XO[:, tb, :], scalar=st1[:, 3:4], in1=FW[:], op0=ALU.mult, op1=ALU.mult),
                    r=["XO", "st1d", "FW"], w=["XO"])
                tok = B.op("sp", lambda e, tb=tb, t0=t0: e.dma_start(
                    out=y_d[t0 + tb * 128:t0 + (tb + 1) * 128, :], in_=XO[:, tb, :]), r=["XO"], dma="o%d" % tb)
                out_toks.append(tok)

        B.wait_all("sp", out_toks)
        block = stack.enter_context(nc.Block())
        B.emit(nc, block, stack)
    return nc


_CACHE = {}


def kernel(x, norm_w, w_in, b_qkv, sinks, conv_w, a_log, dt_bias, dn_norm_w,
           w_att_branch, w_dn_branch, w_out, final_norm_w):
    x = np.asarray(x, np.float32)
    wst = build_wstream(np.asarray(w_in[0], np.float32), np.asarray(w_att_branch[0], np.float32),
                        np.asarray(w_dn_branch[0], np.float32), np.asarray(w_out[0], np.float32))
    cfa = consts_f32()
    cba = consts_b16(np.asarray(b_qkv[0], np.float32))
    pra = params_f32(np.asarray(norm_w[0]), np.asarray(b_qkv[0]), np.asarray(sinks[0]), np.asarray(conv_w[0]),
                     np.asarray(a_log[0]), np.asarray(dt_bias[0]), np.asarray(dn_norm_w[0]))
    fwa = np.ascontiguousarray(np.broadcast_to(np.asarray(final_norm_w, np.float32)[None, :], (128, D)))
    n_wblk = wst.shape[0]
    if "nc" not in _CACHE:
        _CACHE["nc"] = build_program(n_wblk)
    nc = _CACHE["nc"]
    in_maps = [{"x": np.ascontiguousarray(x[b]), "wst": wst, "cf": cfa, "cb": cba, "pr": pra, "fw": fwa}
               for b in range(8)]
    res = run_bass_kernel_spmd(nc, in_maps, core_ids=list(range(8)))
    return np.stack([res.results[b]["y"] for b in range(8)], axis=0).astype(np.float32)
```

```python
import contextlib
import numpy as np
import concourse.bass as bass
import concourse.mybir as mybir
from concourse.bass_utils import run_bass_kernel_spmd

F32 = mybir.dt.float32
BF16 = mybir.dt.bfloat16
AF = mybir.ActivationFunctionType
ALU = mybir.AluOpType
AX = mybir.AxisListType

S = 2048
D = 2048
TT = 512
NT = S // TT
P = 128
NSLOT = 8
EPS = 1e-6
OFF_AQ = 0
OFF_AK = 2048
OFF_AV = 2304
OFF_AZ = 2560
OFF_DQKV = 4608
OFF_DZ = 12800
OFF_DB = 16896
OFF_DA = 16928
OFF_G = 16960
NEG = -30000.0
DGE_SCRATCH = 4096

ENABLE_DN = True
NT_RUN = NT
SKIP_P3 = False
DN_LEVEL = 9
DN_SUB = 99
import os as _os
BISV = int(_os.environ.get('BISV', '0'))


class Builder:
    ENG = ["pe", "act", "dve", "pool", "sp"]

    def __init__(self):
        self.ops = {e: [] for e in self.ENG}
        self.lastw = {}
        self.readers = {}
        self.dma_cnt = {}
        self.mute = False

    def op(self, eng, fn, r=(), w=(), dma=None):
        if self.mute and dma is None:
            return None
        deps = set()
        for k in r:
            t = self.lastw.get(k)
            if t is not None:
                deps.add(t)
            if k.startswith("bank"):
                for t in self.readers.get(k, ()):
                    if not (t[0] == "e" and t[1] == eng):
                        deps.add(t)
        for k in w:
            t = self.lastw.get(k)
            if t is not None:
                deps.add(t)
            for t in self.readers.get(k, ()):
                deps.add(t)
        idx = len(self.ops[eng])
        if dma is not None:
            self.dma_cnt[dma] = self.dma_cnt.get(dma, 0) + 1
            tok = ("d", dma, 16 * self.dma_cnt[dma])
        else:
            tok = ("e", eng, idx)
        self.ops[eng].append({"fn": fn, "deps": deps, "dma": dma, "sig": False})
        for k in w:
            self.lastw[k] = tok
            self.readers[k] = []
        for k in r:
            if k in w:
                continue
            lst = self.readers.setdefault(k, [])
            if tok[0] == "e":
                lst[:] = [t for t in lst if not (t[0] == "e" and t[1] == eng)]
            lst.append(tok)
        return tok

    def wait_all(self, eng, toks):
        self.ops[eng].append({"fn": None, "deps": set(toks), "dma": None, "sig": False})

    def emit(self, nc, block, stack):
        for e in self.ENG:
            for rec in self.ops[e]:
                for t in rec["deps"]:
                    if t[0] == "e" and not (t[1] == "pe" and e == "pe"):
                        self.ops[t[1]][t[2]]["sig"] = True
        sem_e = {e: stack.enter_context(nc.semaphore("se_" + e)) for e in self.ENG}
        sem_d = {k: stack.enter_context(nc.semaphore("sd_" + k)) for k in self.dma_cnt}
        for e in self.ENG:
            c = 0
            for rec in self.ops[e]:
                if rec["sig"]:
                    c += 1
                rec["sv"] = c
        ops = self.ops

        def gen(e):
            def run(eh):
                waited = {}
                for rec in ops[e]:
                    need = {}
                    for t in rec["deps"]:
                        if t[0] == "e":
                            if t[1] == "pe" and e == "pe":
                                continue
                            key = ("e", t[1])
                            val = ops[t[1]][t[2]]["sv"]
                        else:
                            key = ("d", t[1])
                            val = t[2]
                        if val > need.get(key, 0):
                            need[key] = val
                    for key, val in need.items():
                        if waited.get(key, 0) >= val:
                            continue
                        waited[key] = val
                        sem = sem_e[key[1]] if key[0] == "e" else sem_d[key[1]]
                        eh.wait_ge(sem, val)
                    if rec["fn"] is None:
                        continue
                    ins = rec["fn"](eh)
                    if rec["dma"] is not None:
                        ins.then_inc(sem_d[rec["dma"]], 16)
                    elif rec["sig"]:
                        ins.then_inc(sem_e[e], 1)
            return run

        block.tensor(gen("pe"))
        block.scalar(gen("act"))
        block.vector(gen("dve"))
        block.gpsimd(gen("pool"))
        block.sync(gen("sp"))


def AP(t, off, dims):
    return bass.AP(t, off, [list(d) for d in dims])


def dup2(ap):
    a = [list(d) for d in ap.ap]
    assert len(a) == 2
    return bass.AP(ap.tensor, ap.offset, [a[0], [0, 2], a[1]])


def wblock(w, r0, cols):
    sub = w[r0:r0 + 2048][:, cols]
    return np.ascontiguousarray(sub.reshape(16, 128, 128).transpose(1, 0, 2))


def build_wstream(w_in, w_att, w_dn, w_out):
    blocks = []
    ar = np.arange

    def cin(c0, n=128):
        return ar(c0, c0 + n)

    blocks.append(wblock(w_in, 0, cin(OFF_AV)))
    blocks.append(wblock(w_in, 0, cin(OFF_AV + 128)))
    for g in range(4):
        kc = cin(OFF_AK + 64 * g, 64)
        blocks.append(wblock(w_in, 0, np.concatenate([kc, kc])))
        for j in range(4):
            blocks.append(wblock(w_in, 0, cin(OFF_AQ + 128 * (4 * g + j))))
        for j in range(4):
            blocks.append(wblock(w_in, 0, cin(OFF_AZ + 128 * (4 * g + j))))
    for d in range(16):
        blocks.append(wblock(w_in, 0, cin(OFF_G + 128 * d)))
        blocks.append(wblock(w_att, 0, cin(128 * d)))
    ba = cin(OFF_DB, 64)
    blocks.append(wblock(w_in, 0, np.concatenate([ba, ba])))
    for kh in range(16):
        blocks.append(wblock(w_in, 0, cin(OFF_DQKV + 128 * kh)))
        blocks.append(wblock(w_in, 0, cin(OFF_DQKV + 2048 + 128 * kh)))
        for h in range(2):
            blocks.append(wblock(w_in, 0, cin(OFF_DQKV + 4096 + 128 * (2 * kh + h))))
        for h in range(2):
            blocks.append(wblock(w_in, 0, cin(OFF_DZ + 128 * (2 * kh + h))))
    for d in range(16):
        blocks.append(wblock(w_in, 0, cin(OFF_G + 2048 + 128 * d)))
        blocks.append(wblock(w_dn, 0, cin(128 * d)))
        blocks.append(wblock(w_dn, 2048, cin(128 * d)))
    while len(blocks) % 4 != 0:
        blocks.append(blocks[-1])
    for cg in range(4):
        for j in range(4):
            blocks.append(wblock(w_out, 0, cin(512 * cg + 128 * j)))
    return np.ascontiguousarray(np.stack(blocks).reshape(len(blocks), 128, 2048))


N_ATT_BLK = 2 + 4 * 9 + 32
N_DN_BLK = 1 + 96 + 48


def consts_f32():
    p = np.arange(128)
    h = p // 64
    t = p % 64
    same = (h[:, None] == h[None, :])
    bdu = (same & (t[:, None] <= t[None, :])).astype(np.float32)
    bdsu = (same & (t[:, None] > t[None, :])).astype(np.float32)
    hsel0 = np.repeat((h == 0)[:, None], 128, 1).astype(np.float32)
    hsel1 = np.repeat((h == 1)[:, None], 128, 1).astype(np.float32)
    bdones = same.astype(np.float32)
    ident = np.eye(128, dtype=np.float32)
    i64 = np.arange(64)
    u2 = (t[:, None] <= i64[None, :]).astype(np.float32)
    ones64 = np.ones((128, 64), np.float32)
    masks = np.where(i64[None, :] < t[:, None], 0.0, NEG).astype(np.float32)
    maskt = np.where(i64[None, :] >= t[:, None], 0.0, NEG).astype(np.float32)
    parts = [ident, bdu, bdsu, hsel0, hsel1, bdones, -bdones, u2, ones64, -ones64, masks, maskt]
    return np.ascontiguousarray(np.concatenate(parts, axis=1))


CF_OFF = {}
_o = 0
for _n, _w in [("ident", 128), ("bdu", 128), ("bdsu", 128), ("hsel0", 128), ("hsel1", 128), ("bdones", 128),
               ("bdneg", 128), ("u2", 64), ("ones64", 64), ("neg64", 64), ("masks", 64), ("maskt", 64)]:
    CF_OFF[_n] = (_o, _w)
    _o += _w
NCF = _o

CB_OFF = {}
_o = 0
for _n, _w in [("ident", 128), ("mown", 128), ("mprev", 128), ("ones64", 64), ("onescol", 2), ("onesrow", 128),
               ("bv", 256), ("bdones", 128), ("bdneg", 128), ("neg64", 64), ("masks", 64), ("maskt", 64)]:
    CB_OFF[_n] = (_o, _w)
    _o += _w
NCB = _o


def consts_b16(b_qkv):
    j = np.arange(128)[:, None]
    i = np.arange(128)[None, :]
    ident = np.eye(128, dtype=np.float32)
    mown = (j <= i).astype(np.float32)
    mprev = (j > i).astype(np.float32)
    ones64 = np.ones((128, 64), np.float32)
    onescol = np.ones((128, 2), np.float32)
    onesrow = np.zeros((128, 128), np.float32)
    onesrow[0, :] = 1.0
    bv = np.zeros((128, 256), np.float32)
    bv[0, :] = b_qkv[2304:2560]
    cfull = consts_f32()
    extra = [cfull[:, CF_OFF[n][0]:CF_OFF[n][0] + CF_OFF[n][1]] for n in ("bdones", "bdneg", "neg64", "masks", "maskt")]
    return np.ascontiguousarray(np.concatenate([ident, mown, mprev, ones64, onescol, onesrow, bv] + extra, axis=1))


PR_OFF = {}
_o = 0
for _n, _w in [("nw", 16), ("bq", 16), ("bk", 4), ("sk", 16), ("cw", 256), ("dtb", 16), ("alg", 16), ("dnw", 1)]:
    PR_OFF[_n] = (_o, _w)
    _o += _w
NPR = _o


def params_f32(norm_w, b_qkv, sinks, conv_w, a_log, dt_bias, dn_norm_w):
    p = np.arange(128)
    hi = (p >= 64).astype(np.int64)
    nw = norm_w.reshape(16, 128).T
    bq = b_qkv[0:2048].reshape(16, 128).T
    bk = b_qkv[2048:2304].reshape(4, 64)[:, p % 64].T
    sk = sinks[(2 * np.arange(16))[None, :] + hi[:, None]]
    cw = conv_w.reshape(4, 64, 128).transpose(2, 1, 0).reshape(128, 256)
    dtb = dt_bias[(2 * np.arange(16))[None, :] + hi[:, None]]
    alg = a_log[(2 * np.arange(16))[None, :] + hi[:, None]]
    dnw = dn_norm_w.reshape(128, 1)
    return np.ascontiguousarray(np.concatenate([nw, bq, bk, sk, cw, dtb, alg, dnw], axis=1).astype(np.float32))


def build_program(n_wblk):
    nc = bass.Bass("TRN2", target_bir_lowering=False, dynamic_dma_scratch_size=DGE_SCRATCH)
    x_d = nc.dram_tensor("x", [S, D], F32, kind="ExternalInput").ap()
    w_d = nc.dram_tensor("wst", [n_wblk, 128, 2048], F32, kind="ExternalInput").ap()
    cf_d = nc.dram_tensor("cf", [128, NCF], F32, kind="ExternalInput").ap()
    cb_d = nc.dram_tensor("cb", [128, NCB], F32, kind="ExternalInput").ap()
    pr_d = nc.dram_tensor("pr", [128, NPR], F32, kind="ExternalInput").ap()
    fw_d = nc.dram_tensor("fw", [128, D], F32, kind="ExternalInput").ap()
    y_d = nc.dram_tensor("y", [S, D], F32, kind="ExternalOutput").ap()

    B = Builder()
    stack = contextlib.ExitStack()
    with stack:
        def sb(name, shape, dt):
            return stack.enter_context(nc.sbuf_tensor(name, shape, dt))

        def ps(name, shape=(128, 512), dt=F32):
            return stack.enter_context(nc.psum_tensor(name, list(shape), dt))

        ring = sb("ring", [128, NSLOT, 16, 128], BF16)
        hT = sb("hT", [128, 16, TT], BF16)
        m1T = sb("m1T", [128, 16, TT], BF16)
        XO = sb("XO", [128, 4, D], F32)
        XOb = XO[:].bitcast(BF16)
        FW = sb("FW", [128, D], F32)
        CF = sb("CF", [128, NCF], F32)
        CB = sb("CB", [128, NCB], BF16)
        PR = sb("PR", [128, NPR], F32)
        hb = sb("hb", [128, D], BF16)
        st1 = sb("st1", [128, 8], F32)
        qT = sb("qT", [128, 4, TT], BF16)
        zT = sb("zT", [128, 4, TT], BF16)
        kT2 = sb("kT2", [128, 4, 128 + TT], BF16)
        vtok = sb("vtok", [128, 5, 256], BF16)
        pT = sb("pT", [128, 4, 512], BF16)
        dnb = sb("dnb", [128, 512], F32)
        ogf = sb("ogf", [128, 512], F32)
        gsb = sb("gsb", [128, 1, 512], F32)
        ES = sb("ES", [128, 16], F32)
        NEGA = sb("NEGA", [128, 16], F32)

        S2 = sb("S2", [128, 16, 256], F32)
        S2b = sb("S2b", [128, 16, 256], BF16)
        HALO = sb("HALO", [128, 64, 4], BF16)
        BET = sb("BET", [128, 8, 16], F32)
        GST = sb("GST", [128, 8, 16], F32)
        EX = sb("EX", [128, 4, 8, 16], F32)
        CSb = [sb("CSb%d" % i, [128, 516], BF16) for i in range(2)]
        DIAG = [sb("DIAG%d" % i, [128, 4, 128], BF16) for i in range(2)]
        ACC = [sb("ACC0", [128, 512], F32)]
        XQKs = [sb("XQK%d" % i, [128, 2, TT], BF16) for i in range(2)]
        VT = [sb("VT%d" % i, [128, TT], BF16) for i in range(2)]
        ZSs = [[sb("ZS%d_%d" % (p_, i), [128, TT], BF16) for i in range(2)] for p_ in range(2)]
        RSs = [sb("RS%d" % i, [128, 8, 2], F32) for i in range(2)]
        SCs = [sb("SC%d" % i, [128, 5, 8], F32) for i in range(2)]
        XKD = sb("XKD", [128, 8, 128], BF16)
        VB = sb("VB", [128, 8, 128], F32)
        GP8h = sb("GP8h", [128, 8, 64], BF16)
        GP8l = sb("GP8l", [128, 8, 64], BF16)
        GHL = sb("GHL", [128, 2, 8], F32)
        GHb = sb("GHb", [128, 8], BF16)
        P0B8 = sb("P0B8", [128, 8, 128], BF16)
        Q0B8 = sb("Q0B8", [128, 8, 128], BF16)
        PQ8 = [sb("PQ8_%d" % i, [128, 2, 8, 128], BF16) for i in range(2)]
        Yb8 = sb("Yb8", [128, 8, 128], BF16)
        YT = sb("YT", [128, 8, 128], BF16)
        MQ = sb("MQ", [128, 8, 128], BF16)
        RB = sb("RB", [128, 128], BF16)
        VPB = sb("VPB", [128, 128], BF16)
        T1 = sb("T1", [128, 128], F32)
        OB = sb("OB", [128, 8, 128], F32)
        JK = sb("JK", [128, 128], BF16)
        SSO = sb("SSO", [128, 8, 4], F32)

        class V3:
            def __init__(self, base_ap, C, J):
                self.t = base_ap.tensor
                self.off = base_ap.offset
                self.ps = base_ap.ap[0][0]
                self.C, self.J = C, J

            def ap(self, p0=0, p1=128, c0=0, c1=None, j0=0, j1=None):
                c1 = self.C if c1 is None else c1
                j1 = self.J if j1 is None else j1
                dims = [[self.ps, p1 - p0]]
                if c1 - c0 > 1:
                    dims.append([self.J, c1 - c0])
                dims.append([1, j1 - j0])
                return AP(self.t, self.off + p0 * self.ps + c0 * self.J + j0, dims)

        _qf = qT[:].rearrange("p a b -> p (a b)")
        GBD8h = V3(_qf[:, 0:1024], 8, 128)
        GBD8l = V3(_qf[:, 1024:2048], 8, 128)
        _zf = zT[:].bitcast(F32).rearrange("p a b -> p (a b)")
        DSSa = V3(_zf[:, 0:512], 8, 64)
        DSSb = V3(_zf[:, 512:1024], 8, 64)
        Y32_8 = V3(pT[:].bitcast(F32), 8, 128)
        ON3 = V3(dnb[:].bitcast(BF16), 8, 128)
        SQ3 = V3(ogf[:].bitcast(BF16), 2, TT)

        banks = [ps("bank%d" % i) for i in range(8)]

        def cf(name):
            o, w = CF_OFF[name]
            return CF[:, o:o + w]

        def cb(name):
            o, w = CB_OFF[name]
            return CB[:, o:o + w]

        def pr(name, j=None, n=1):
            o, w = PR_OFF[name]
            if j is None:
                return PR[:, o:o + w]
            return PR[:, o + j:o + j + n]

        xob_t = XOb.tensor
        xob_ps = XOb.ap[0][0]

        def XOb_view(b, t0, n):
            return AP(xob_t, XOb.offset + b * TT + t0, [[xob_ps, 128], [1, n]])

        def XOb_view3(b0, nb, t0, n):
            return AP(xob_t, XOb.offset + b0 * TT + t0, [[xob_ps, 128], [TT, nb], [1, n]])

        B.op("sp", lambda e: e.dma_start(out=CF[:], in_=cf_d), w=["CF"], dma="c0")
        B.op("sp", lambda e: e.dma_start(out=PR[:], in_=pr_d), w=["PR"], dma="c1")
        B.op("sp", lambda e: e.dma_start(out=FW[:], in_=fw_d), w=["FW"], dma="c2")
        B.op("pool", lambda e: e.dma_start(out=CB[:], in_=cb_d), w=["CB"], dma="c3")
        B.op("act", lambda e: e.activation(out=ES[:], in_=pr("sk"), func=AF.Exp), r=["PR"], w=["ES"])
        B.op("act", lambda e: e.activation(out=NEGA[:], in_=pr("alg"), func=AF.Exp), r=["PR"], w=["NEGA"])
        B.op("dve", lambda e: e.tensor_scalar(out=NEGA[:], in0=NEGA[:], scalar1=-1.0, scalar2=None, op0=ALU.mult),
             r=["NEGA"], w=["NEGA"])
        B.op("pool", lambda e: e.memset(kT2[:], 0.0), w=["kT2"])
        B.op("pool", lambda e: e.memset(vtok[:], 0.0), w=["vtok"])
        B.op("pool", lambda e: e.memset(S2[:], 0.0), w=["S2"])
        B.op("pool", lambda e: e.memset(S2b[:], 0.0), w=["S2b"])
        B.op("pool", lambda e: e.memset(HALO[:], 0.0), w=["HALO"])
        B.op("pool", lambda e: e.memset(MQ[:], 0.0), w=["MQ"])
        B.op("pool", lambda e: e.memset(P0B8[:], 0.0), w=["P0B8"])

        wstate = {"next_load": 0, "next_use": 0}
        n_total = n_wblk * 0 + 0

        def w_load_upto(k):
            while wstate["next_load"] < k:
                i = wstate["next_load"]
                sl = i % NSLOT
                src = w_d[i % n_wblk]
                B.op("pool", (lambda e, sl=sl, src=src: e.dma_start(
                    out=ring[:, sl, :, :].rearrange("p c n -> p (c n)"), in_=src)),
                    w=["ring%d" % sl], dma="w%d" % sl)
                wstate["next_load"] += 1

        def w_next():
            i = wstate["next_use"]
            wstate["next_use"] += 1
            w_load_upto(i + 1)
            return i % NSLOT

        def w_prefetch():
            lim = min(wstate["next_use"] + NSLOT, n_wblk * NT_RUN)
            w_load_upto(min(lim, wstate["next_use"] + NSLOT))

        pbank = {"i": 0}

        def proj_fm(slot, rhs_fn, nk=16, bankset=(0, 1)):
            bi = bankset[pbank["i"] % len(bankset)]
            pbank["i"] += 1
            bk = banks[bi]
            for c in range(nk):
                B.op("pe", (lambda e, c=c, bk=bk: e.matmul(bk[:], lhsT=ring[:, slot, c, :], rhs=rhs_fn(c),
                                                           start=(c == 0), stop=(c == nk - 1))),
                     r=["ring%d" % slot, "hT"], w=["bank%d" % bi])
            return bi

        out_toks = []

        for T in range(NT_RUN):
            t0 = T * TT
            B.op("sp", lambda e, t0=t0: e.dma_start(
                out=XO[:], in_=x_d[t0:t0 + TT, :].rearrange("(t p) d -> p t d", p=128)), w=["XO"], dma="x")
            for tb in range(4):
                B.op("act", lambda e, tb=tb: e.activation(out=hb[:], in_=XO[:, tb, :], func=AF.Square,
                                                          accum_out=st1[:, 0:1]), r=["XO"], w=["hb", "st1"])
                B.op("dve", lambda e: e.tensor_scalar(out=st1[:, 1:2], in0=st1[:, 0:1], scalar1=1.0 / D, scalar2=EPS,
                                                      op0=ALU.mult, op1=ALU.add), r=["st1"], w=["st1b"])
                B.op("act", lambda e: e.activation(out=st1[:, 2:3], in_=st1[:, 1:2], func=AF.Sqrt),
                     r=["st1b"], w=["st1c"])
                B.op("dve", lambda e: e.reciprocal(out=st1[:, 3:4], in_=st1[:, 2:3]), r=["st1c"], w=["st1d"])
                B.op("dve", lambda e, tb=tb: e.tensor_scalar(out=hb[:], in0=XO[:, tb, :], scalar1=st1[:, 3:4],
                                                             scalar2=None, op0=ALU.mult),
                     r=["XO", "st1d"], w=["hb"])
                for c4 in range(4):
                    bi = 2 + (c4 % 2)
                    bkb = banks[bi][:].bitcast(BF16)
                    for cc in range(4):
                        c = c4 * 4 + cc
                        B.op("pe", lambda e, c=c, cc=cc, bkb=bkb: e.transpose(
                            out=bkb[:, cc * 128:(cc + 1) * 128], in_=hb[:, c * 128:(c + 1) * 128], identity=cb("ident")),
                            r=["hb", "CB"], w=["bank%d" % bi])
                    for cc in range(4):
                        c = c4 * 4 + cc
                        B.op("dve", lambda e, c=c, cc=cc, bkb=bkb, tb=tb: e.tensor_scalar(
                            out=hT[:, c, tb * 128:(tb + 1) * 128], in0=bkb[:, cc * 128:(cc + 1) * 128],
                            scalar1=pr("nw", c), scalar2=None, op0=ALU.mult),
                            r=["bank%d" % bi, "PR"], w=["hT"])

            if T > 0:
                B.op("pool", lambda e: e.tensor_copy(out=kT2[:, :, 0:128], in_=kT2[:, :, TT:TT + 128]),
                     r=["kT2"], w=["kT2"])
                B.op("pool", lambda e: e.tensor_copy(out=vtok[:, 0, :], in_=vtok[:, 4, :]), r=["vtok"], w=["vtok"])
            sv = [w_next(), w_next()]
            for tb in range(4):
                bi = 2 + (tb % 2)
                for half in range(2):
                    o0 = half * 128
                    B.op("pe", lambda e, bi=bi, o0=o0: e.matmul(
                        banks[bi][:, o0:o0 + 128], lhsT=CB[0:1, CB_OFF["onesrow"][0]:CB_OFF["onesrow"][0] + 128],
                        rhs=CB[0:1, CB_OFF["bv"][0] + o0:CB_OFF["bv"][0] + o0 + 128], start=True, stop=False),
                        r=["CB"], w=["bank%d" % bi])
                    for c in range(16):
                        B.op("pe", lambda e, bi=bi, o0=o0, c=c, tb=tb, sl=sv[half]: e.matmul(
                            banks[bi][:, o0:o0 + 128], lhsT=hT[:, c, tb * 128:(tb + 1) * 128], rhs=ring[:, sl, c, :],
                            start=False, stop=(c == 15)), r=["hT", "ring%d" % sv[half]], w=["bank%d" % bi])
                B.op("act", lambda e, bi=bi, tb=tb: e.copy(out=vtok[:, 1 + tb, :], in_=banks[bi][:, 0:256]),
                     r=["bank%d" % bi], w=["vtok"])
            w_prefetch()

            for g in range(4):
                sl = w_next()
                bi = proj_fm(sl, lambda c: hT[:, c, :])
                B.op("dve", lambda e, bi=bi, g=g: e.tensor_scalar(
                    out=kT2[:, g, 128:128 + TT], in0=banks[bi][:], scalar1=pr("bk", g), scalar2=None, op0=ALU.add),
                    r=["bank%d" % bi, "PR"], w=["kT2"])
                w_prefetch()
                for j in range(4):
                    sl = w_next()
                    bi = proj_fm(sl, lambda c: hT[:, c, :])
                    B.op("dve", lambda e, bi=bi, g=g, j=j: e.tensor_scalar(
                        out=qT[:, j, :], in0=banks[bi][:], scalar1=pr("bq", 4 * g + j), scalar2=0.125,
                        op0=ALU.add, op1=ALU.mult), r=["bank%d" % bi, "PR"], w=["qT"])
                    w_prefetch()
                for j in range(4):
                    sl = w_next()
                    bi = proj_fm(sl, lambda c: hT[:, c, :])
                    B.op("act", lambda e, bi=bi, j=j: e.activation(out=zT[:, j, :], in_=banks[bi][:], func=AF.Silu),
                         r=["bank%d" % bi], w=["zT"])
                    w_prefetch()
                for qb in range(4):
                    nglob = 4 * T + qb
                    kinds = ["own"] + (["prev"] if nglob > 0 else [])
                    for half in range(2):
                        pp = slice(half * 64, half * 64 + 64)
                        for ki, kind in enumerate(kinds):
                            bi = 2 + half * 2 + ki
                            kcol = 128 + qb * 128 if kind == "own" else qb * 128
                            B.op("pe", lambda e, bi=bi, pp=pp, kcol=kcol, g=g, qb=qb: e.matmul(
                                banks[bi][:], lhsT=kT2[pp, g, kcol:kcol + 128],
                                rhs=qT[pp, :, qb * 128:(qb + 1) * 128], start=True, stop=True),
                                r=["kT2", "qT"], w=["bank%d" % bi])
                            pi = half * 2 + ki
                            B.op("act", lambda e, bi=bi, pi=pi: e.activation(
                                out=pT[:, pi, :], in_=banks[bi][:], func=AF.Exp), r=["bank%d" % bi], w=["pT%d" % pi])
                            mk = cb("mown") if kind == "own" else cb("mprev")
                            mk3 = AP(mk.tensor, mk.offset, [list(mk.ap[0]), [0, 4], [1, 128]])
                            B.op("dve", lambda e, pi=pi, mk3=mk3: e.tensor_tensor(
                                out=pT[:, pi, :].rearrange("p (h q) -> p h q", h=4),
                                in0=pT[:, pi, :].rearrange("p (h q) -> p h q", h=4), in1=mk3, op=ALU.mult),
                                r=["pT%d" % pi, "CB"], w=["pT%d" % pi])
                    for half in range(2):
                        po = slice(half * 64, half * 64 + 64)
                        for ki, kind in enumerate(kinds):
                            pi = half * 2 + ki
                            vb_ = 1 + qb if kind == "own" else qb
                            B.op("pe", lambda e, po=po, pi=pi, vb_=vb_, g=g, ki=ki, nk=len(kinds): e.matmul(
                                banks[6][po, :], lhsT=vtok[:, vb_, g * 64:(g + 1) * 64], rhs=pT[:, pi, :],
                                start=(ki == 0), stop=(ki == nk - 1)), r=["vtok", "pT%d" % pi], w=["bank6"])
                        for ki, kind in enumerate(kinds):
                            pi = half * 2 + ki
                            B.op("pe", lambda e, po=po, pi=pi, ki=ki, nk=len(kinds): e.matmul(
                                banks[7][po, :], lhsT=cb("ones64"), rhs=pT[:, pi, :],
                                start=(ki == 0), stop=(ki == nk - 1)), r=["CB", "pT%d" % pi], w=["bank7"])
                    es3 = AP(ES, 4 * g, [[16, 128], [1, 4], [0, 128]])
                    B.op("dve", lambda e, es3=es3: e.tensor_tensor(
                        out=dnb[:].rearrange("p (h q) -> p h q", h=4),
                        in0=banks[7][:].rearrange("p (h q) -> p h q", h=4), in1=es3, op=ALU.add),
                        r=["bank7", "ES"], w=["dnb"])
                    B.op("act", lambda e: e.activation(out=dnb[:], in_=dnb[:], func=AF.Ln), r=["dnb"], w=["dnb"])
                    B.op("act", lambda e: e.activation(out=dnb[:], in_=dnb[:], func=AF.Exp, scale=-1.0),
                         r=["dnb"], w=["dnb"])
                    B.op("dve", lambda e: e.tensor_tensor(out=ogf[:], in0=banks[6][:], in1=dnb[:], op=ALU.mult),
                         r=["bank6", "dnb"], w=["ogf"])
                    B.op("dve", lambda e, g=g, qb=qb: e.tensor_tensor(
                        out=XOb_view3(4 * g, 4, qb * 128, 128), in0=ogf[:].rearrange("p (h q) -> p h q", h=4),
                        in1=zT[:, :, qb * 128:(qb + 1) * 128], op=ALU.mult), r=["ogf", "zT"], w=["XO"])

            for d in range(16):
                sg = w_next()
                big = proj_fm(sg, lambda c: hT[:, c, :], bankset=(0, 1))
                gi = 0
                B.op("act", lambda e, big=big, gi=gi: e.activation(out=gsb[:, gi, :], in_=banks[big][:],
                                                                   func=AF.Sigmoid),
                     r=["bank%d" % big], w=["gsb%d" % gi])
                w_prefetch()
                sw = w_next()
                bia = 2 + (d % 2)
                for c in range(16):
                    B.op("pe", lambda e, c=c, bia=bia, sw=sw: e.matmul(
                        banks[bia][:], lhsT=ring[:, sw, c, :], rhs=XOb_view(c, 0, TT), start=(c == 0), stop=(c == 15)),
                        r=["ring%d" % sw, "XO"], w=["bank%d" % bia])
                B.op("dve", lambda e, bia=bia, gi=gi, d=d: e.tensor_tensor(
                    out=m1T[:, d, :], in0=banks[bia][:], in1=gsb[:, gi, :], op=ALU.mult),
                    r=["bank%d" % bia, "gsb%d" % gi], w=["m1T"])
                w_prefetch()

            if ENABLE_DN:
                DK_SCALE = 128.0 ** -0.5

                def pview(base, dims):
                    return AP(base.tensor, base.offset, [list(base.ap[0])] + [list(d_) for d_ in dims])

                sba = w_next()
                for kc in range(16):
                    B.op("pe", lambda e, kc=kc, sba=sba: e.matmul(
                        banks[2][0:64, :], lhsT=ring[:, sba, kc, 0:64], rhs=hT[:, kc, :],
                        start=(kc == 0), stop=(kc == 15)), r=["hT", "ring%d" % sba], w=["bank2"])
                w_prefetch()
                B.op("act", lambda e: e.copy(out=ACC[0][0:64, :], in_=banks[2][0:64, :]), r=["bank2"], w=["ACC0"])
                B.mute = DN_SUB < 2
                for c in range(8):
                    for h in range(2):
                        B.op("pe", lambda e, c=c, h=h: e.matmul(
                            banks[4][h * 64:(h + 1) * 64, c * 64:(c + 1) * 64], lhsT=ACC[0][0:64, c * 64:(c + 1) * 64],
                            rhs=CF[0:64, CF_OFF["ident"][0]:CF_OFF["ident"][0] + 64], start=True, stop=True),
                            r=["ACC0", "CF"], w=["bank4"])
                B.mute = DN_SUB < 3
                for h in range(2):
                    hp = slice(h * 64, (h + 1) * 64)
                    B.op("act", lambda e, h=h, hp=hp: e.activation(
                        out=BET[hp, :, :], in_=pview(banks[4][hp, h:h + 1], [[64, 8], [2, 16]]), func=AF.Sigmoid),
                        r=["bank4"], w=["BET"])
                    B.op("dve", lambda e, h=h, hp=hp: e.tensor_tensor(
                        out=GST[hp, :, :], in0=pview(banks[4][hp, 32 + h:33 + h], [[64, 8], [2, 16]]),
                        in1=pview(PR[hp, PR_OFF["dtb"][0]:PR_OFF["dtb"][0] + 1], [[0, 8], [1, 16]]), op=ALU.add),
                        r=["bank4", "PR"], w=["GST"])
                B.mute = DN_SUB < 4
                B.op("act", lambda e: e.activation(out=GST[:], in_=GST[:], func=AF.Exp), r=["GST"], w=["GST"])
                B.op("dve", lambda e: e.tensor_scalar(out=GST[:], in0=GST[:], scalar1=1.0, scalar2=None, op0=ALU.add),
                     r=["GST"], w=["GST"])
                B.op("act", lambda e: e.activation(out=GST[:], in_=GST[:], func=AF.Ln), r=["GST"], w=["GST"])
                B.op("dve", lambda e: e.tensor_tensor(
                    out=GST[:], in0=GST[:], in1=pview(NEGA[:, 0:1], [[0, 8], [1, 16]]), op=ALU.mult),
                    r=["GST", "NEGA"], w=["GST"])
                B.mute = DN_SUB < 5
                gflat = GST[:].rearrange("p c k -> p (c k)")
                for qi, nm in enumerate(["bdu", "bdsu", "hsel0", "hsel1"]):
                    B.op("pe", lambda e, qi=qi, nm=nm: e.matmul(
                        banks[3][:, qi * 128:(qi + 1) * 128], lhsT=cf(nm), rhs=gflat, start=True, stop=True),
                        r=["CF", "GST"], w=["bank3"])
                B.op("act", lambda e: e.activation(out=EX[:].rearrange("p a c k -> p (a c k)"), in_=banks[3][:],
                                                   func=AF.Exp), r=["bank3"], w=["EX"])

                B.mute = DN_SUB < 6
                def proj_fm_gen(slot):
                    bi = (0, 1)[pbank["i"] % 2]
                    pbank["i"] += 1
                    bk = banks[bi]
                    for c in range(16):
                        B.op("pe", (lambda e, c=c, bk=bk: e.matmul(bk[:], lhsT=ring[:, slot, c, :], rhs=hT[:, c, :],
                                                                   start=(c == 0), stop=(c == 15))),
                             r=["ring%d" % slot, "hT"], w=["bank%d" % bi])
                        if c % 4 == 3 and c != 15:
                            yield
                    return bi

                def step_bg(bg):
                    if bg is None:
                        return False
                    try:
                        next(bg)
                        return True
                    except StopIteration:
                        return False

                def gen_proj(kh, XQK, ZS, KQ, KK_, KZ, RS, SC, KRS, KSC, lag=True):
                    deferred = []
                    stepno = [0]

                    def tick():
                        stepno[0] += 1
                        for d_ in [d_ for d_ in deferred if d_[0] <= stepno[0]]:
                            d_[1]()
                            deferred.remove(d_)

                    def later(n, fn):
                        if lag:
                            deferred.append((stepno[0] + n, fn))
                        else:
                            fn()

                    order = [("q", kh, XQK[:, 1, :], KQ), ("k", 16 + kh, XQK[:, 0, :], KK_),
                             ("v0", 32 + 2 * kh, VT[0][:], "VT0"), ("v1", 33 + 2 * kh, VT[1][:], "VT1"),
                             ("z0", None, ZS[0][:], KZ[0]), ("z1", None, ZS[1][:], KZ[1])]
                    for oi, (nm, cblk, dst, dkey) in enumerate(order):
                        sl = w_next()
                        bi = (0, 1)[pbank["i"] % 2]
                        pbank["i"] += 1
                        ci = oi % 2

                        if cblk is not None:
                            B.op("pool", lambda e, ci=ci, cblk=cblk: e.tensor_tensor(
                                out=DIAG[ci][:], in0=pview(cb("ident")[:, 0:1], [[0, 4], [1, 128]]),
                                in1=pview(pr("cw", cblk * 4), [[1, 4], [0, 128]]), op=ALU.mult),
                                r=["CB", "PR"], w=["DIAG%d" % ci])

                        def evac(bi=bi, ci=ci, cblk=cblk):
                            B.op("act", lambda e: e.copy(out=CSb[ci][:, 3:515], in_=banks[bi][:]),
                                 r=["bank%d" % bi], w=["CSm%d" % ci])
                            B.op("dve", lambda e: e.tensor_copy(out=CSb[ci][:, 0:3], in_=HALO[:, cblk, 0:3]),
                                 r=["HALO%d" % cblk], w=["CSh%d" % ci])
                            B.op("dve", lambda e: e.tensor_copy(out=HALO[:, cblk, 0:3], in_=CSb[ci][:, 512:515]),
                                 r=["CSm%d" % ci], w=["HALO%d" % cblk])

                        def conv(ci=ci, cblk=cblk):
                            for ti in range(4):
                                B.op("pe", lambda e, ti=ti: e.matmul(
                                    banks[2][:], lhsT=DIAG[ci][:, ti, :], rhs=CSb[ci][:, ti:ti + 512],
                                    start=(ti == 0), stop=(ti == 3)),
                                    r=["DIAG%d" % ci, "CSm%d" % ci, "CSh%d" % ci], w=["bank2"])

                        def silu_c(ci=ci, dst=dst, dkey=dkey):
                            B.op("act", lambda e: e.activation(out=dst, in_=banks[2][:], func=AF.Silu),
                                 r=["bank2"], w=[dkey])

                        def silu_z(bi=bi, dst=dst, dkey=dkey):
                            B.op("act", lambda e: e.activation(out=dst, in_=banks[bi][:], func=AF.Silu),
                                 r=["bank%d" % bi], w=[dkey])

                        for c in range(16):
                            B.op("pe", (lambda e, c=c, bi=bi, sl=sl: e.matmul(
                                banks[bi][:], lhsT=ring[:, sl, c, :], rhs=hT[:, c, :], start=(c == 0), stop=(c == 15))),
                                r=["ring%d" % sl, "hT"], w=["bank%d" % bi])
                            if c % 4 == 3:
                                if c == 15:
                                    w_prefetch()
                                    if cblk is not None:
                                        later(0, evac)
                                        later(1, conv)
                                        later(2, silu_c)
                                    else:
                                        later(2, silu_z)
                                tick()
                                yield
                    while deferred:
                        tick()
                        yield
                    B.op("pool", lambda e: e.tensor_tensor(out=SQ3.ap(), in0=XQK[:], in1=XQK[:], op=ALU.mult),
                         r=[KQ, KK_], w=["ogf"])
                    for c in range(8):
                        for qi in range(2):
                            for h in range(2):
                                B.op("pe", lambda e, c=c, qi=qi, h=h: e.matmul(
                                    banks[7][h * 64:(h + 1) * 64, 256 + c * 2 + qi:256 + c * 2 + qi + 1],
                                    lhsT=SQ3.ap(c0=1 - qi, c1=2 - qi, j0=c * 64, j1=(c + 1) * 64),
                                    rhs=CB[:, CB_OFF["onescol"][0]:CB_OFF["onescol"][0] + 1], start=True, stop=True),
                                    r=["ogf", "CB"], w=["bank7"])
                    B.op("dve", lambda e: e.tensor_scalar(
                        out=RS[:].rearrange("p c q -> p (c q)"), in0=banks[7][:, 256:272], scalar1=EPS, scalar2=None,
                        op0=ALU.add), r=["bank7"], w=[KRS])
                    B.op("act", lambda e: e.activation(out=RS[:], in_=RS[:], func=AF.Ln), r=[KRS], w=[KRS])
                    B.op("act", lambda e: e.activation(out=RS[:], in_=RS[:], func=AF.Exp, scale=-0.5), r=[KRS], w=[KRS])
                    rq = RS[:, :, 0]
                    rk = RS[:, :, 1]
                    betk = BET[:, :, kh]
                    egck = EX[:, 0, :, kh]
                    C1, CA, C2, CO, COE = (SC[:, i, :] for i in range(5))
                    B.op("dve", lambda e, betk=betk: e.tensor_tensor(out=C1, in0=betk, in1=rk, op=ALU.mult),
                         r=["BET", KRS], w=[KSC[0]])
                    B.op("dve", lambda e: e.tensor_tensor(out=CA, in0=C1, in1=rk, op=ALU.mult),
                         r=[KSC[0], KRS], w=[KSC[1]])
                    B.op("dve", lambda e, egck=egck: e.scalar_tensor_tensor(out=C2, in0=CA, scalar=-1.0, in1=egck,
                                                                 op0=ALU.mult, op1=ALU.mult),
                         r=[KSC[1], "EX"], w=[KSC[2]])
                    B.op("dve", lambda e: e.tensor_scalar(out=CO, in0=rq, scalar1=DK_SCALE, scalar2=None, op0=ALU.mult),
                         r=[KRS], w=[KSC[3]])
                    B.op("dve", lambda e, egck=egck: e.tensor_tensor(out=COE, in0=CO, in1=egck, op=ALU.mult),
                         r=[KSC[3], "EX"], w=[KSC[4]])


                    yield

                def kh_body(kh, XQK, ZS, KQ, KK_, KZ, RS, SC, KRS, KSC, bg, prev_out):
                    if DN_LEVEL >= 5:
                        KEY_GBD = ["qT"]
                        KEY_DSS = ["zT"]
                        KEY_Y32 = ["pT0", "pT1", "pT2", "pT3"]
                        gb = pview(GST[:, 0, kh:kh + 1], [[16, 8]])
                        B.op("dve", lambda e, gb=gb: e.tensor_copy(out=GHb[:], in_=gb), r=["GST"], w=["GHb"])
                        B.op("dve", lambda e, gb=gb: e.tensor_tensor(out=GHL[:, 1, :], in0=gb, in1=GHb[:], op=ALU.subtract),
                             r=["GST", "GHb"], w=["GHL"])
                        for (dst, dkey, cname, width, src, skey) in [
                                (GP8h[:], "GP8h", "u2", 64, GHb, "GHb"), (GP8l[:], "GP8l", "u2", 64, None, "GHL"),
                                (GBD8h.ap(), "qT", "bdu", 128, GHb, "GHb"), (GBD8l.ap(), "qT", "bdu", 128, None, "GHL")]:
                            if src is not None:
                                in1 = pview(GHb[:, 0:1], [[1, 8], [0, width]])
                            else:
                                in1 = pview(GHL[:, 1, 0:1], [[1, 8], [0, width]])
                            B.op("dve", lambda e, dst=dst, cname=cname, width=width, in1=in1: e.tensor_tensor(
                                out=dst, in0=pview(cf(cname)[:, 0:1], [[0, 8], [1, width]]), in1=in1, op=ALU.mult),
                                r=["CF", skey], w=[dkey])
                        for c in range(8):
                            bi = 2 + c // 4
                            co_ = (c % 4) * 128
                            for h in range(2):
                                B.op("pe", lambda e, c=c, h=h, bi=bi, co_=co_: e.matmul(
                                    banks[bi][h * 64:(h + 1) * 64, co_:co_ + 128], lhsT=XQK[:, 0, c * 64:(c + 1) * 64],
                                    rhs=XQK[:, :, c * 64:(c + 1) * 64], start=True, stop=True),
                                    r=[KK_, KQ], w=["bank%d" % bi])
                        for (bi, cconst, cmask, cones) in [(4, "bdneg", "masks", "ones64"), (5, "bdones", "maskt", "neg64")]:
                            bk_ = "bank%d" % bi
                            B.op("pe", lambda e, bi=bi, cconst=cconst: e.matmul(
                                banks[bi][:], lhsT=cb(cconst), rhs=GP8h[:], start=True, stop=False, skip_group_check=True),
                                r=["CB", "GP8h"], w=[bk_])
                            B.op("pe", lambda e, bi=bi, cconst=cconst: e.matmul(
                                banks[bi][:], lhsT=cb(cconst), rhs=GP8l[:], start=False, stop=False, skip_group_check=True),
                                r=["CB", "GP8l"], w=[bk_])
                            B.op("pe", lambda e, bi=bi, cmask=cmask: e.matmul(
                                banks[bi][:], lhsT=cb("ident"), rhs=pview(cb(cmask)[:, 0:1], [[0, 8], [1, 64]]),
                                start=False, stop=False, skip_group_check=True), r=["CB"], w=[bk_])
                            for c in range(8):
                                B.op("pe", lambda e, bi=bi, c=c, cones=cones: e.matmul(
                                    banks[bi][:, c * 64:(c + 1) * 64], lhsT=GBD8h.ap(c0=c, c1=c + 1), rhs=cb(cones),
                                    start=False, stop=False, skip_group_check=True), r=["CB", "qT"], w=[bk_])
                                B.op("pe", lambda e, bi=bi, c=c, cones=cones: e.matmul(
                                    banks[bi][:, c * 64:(c + 1) * 64], lhsT=GBD8l.ap(c0=c, c1=c + 1), rhs=cb(cones),
                                    start=False, stop=(c == 7), skip_group_check=True), r=["CB", "qT"], w=[bk_])
                        B.op("act", lambda e: e.activation(out=DSSa.ap(), in_=banks[4][:].rearrange("p (c j) -> p c j", c=8),
                                                           func=AF.Exp), r=["bank4"], w=KEY_DSS)
                        B.op("act", lambda e: e.activation(out=DSSb.ap(), in_=banks[5][:].rearrange("p (c j) -> p c j", c=8),
                                                           func=AF.Exp), r=["bank5"], w=KEY_DSS)
                        B.op("dve", lambda e: e.tensor_tensor(
                            out=DSSa.ap(), in0=DSSa.ap(), in1=pview(SC[:, 1, 0:1], [[1, 8], [0, 64]]), op=ALU.mult),
                            r=KEY_DSS + [KSC[1]], w=KEY_DSS)
                        for b2 in range(2):
                            for h in range(2):
                                hp = slice(h * 64, (h + 1) * 64)
                                B.op("dve", lambda e, b2=b2, h=h, hp=hp: e.tensor_tensor(
                                    out=P0B8[hp, 4 * b2:4 * b2 + 4, h * 64:(h + 1) * 64],
                                    in0=pview(banks[2 + b2][hp, 0:1], [[128, 4], [1, 64]]),
                                    in1=DSSa.ap(p0=h * 64, p1=(h + 1) * 64, c0=4 * b2, c1=4 * b2 + 4),
                                    op=ALU.mult), r=["bank%d" % (2 + b2)] + KEY_DSS, w=["P0B8"])
                                B.op("dve", lambda e, b2=b2, h=h, hp=hp: e.tensor_tensor(
                                    out=MQ[hp, 4 * b2:4 * b2 + 4, h * 64:(h + 1) * 64],
                                    in0=pview(banks[2 + b2][hp, 64:65], [[128, 4], [1, 64]]),
                                    in1=DSSb.ap(p0=h * 64, p1=(h + 1) * 64, c0=4 * b2, c1=4 * b2 + 4),
                                    op=ALU.mult), r=["bank%d" % (2 + b2)] + KEY_DSS, w=["MQ"])
                        for c in range(8):
                            bi = 6 + c // 4
                            co_ = (c % 4) * 128
                            B.op("pe", lambda e, c=c, bi=bi, co_=co_: e.matmul(
                                banks[bi][:, co_:co_ + 128], lhsT=P0B8[:, c, :], rhs=cb("ident"), start=True, stop=True),
                                r=["P0B8", "CB"], w=["bank%d" % bi])
                        for b2 in range(2):
                            B.op("act", lambda e, b2=b2: e.copy(
                                out=Q0B8[:, 4 * b2:4 * b2 + 4, :], in_=banks[6 + b2][:].rearrange("p (c j) -> p c j", c=4)),
                                r=["bank%d" % (6 + b2)], w=["Q0B8"])
                            B.op("dve", lambda e, b2=b2: e.tensor_tensor(
                                out=Y32_8.ap(c0=4 * b2, c1=4 * b2 + 4), in0=pview(cf("ident")[:, 0:1], [[0, 4], [1, 128]]),
                                in1=banks[6 + b2][:].rearrange("p (c j) -> p c j", c=4), op=ALU.subtract),
                                r=["bank%d" % (6 + b2), "CF"], w=KEY_Y32)
                        B.op("act", lambda e: e.copy(out=Yb8[:], in_=Y32_8.ap()), r=KEY_Y32, w=["Yb8"])
                        def PQk(k):
                            if k == 0:
                                return (lambda c: P0B8[:, c, :]), (lambda c: Q0B8[:, c, :]), ["P0B8", "Q0B8"]
                            t_ = PQ8[k % 2]
                            return (lambda c: t_[:, 0, c, :]), (lambda c: t_[:, 1, c, :]), ["PQ8_%d" % (k % 2)]

                        def squaring(k):
                            Pk, Qk, pk_keys = PQk(k)
                            nxt = PQ8[(k + 1) % 2]
                            nk_ = "PQ8_%d" % ((k + 1) % 2)
                            for c in range(8):
                                bi = 2 + c // 4
                                co_ = (c % 4) * 128
                                B.op("pe", lambda e, c=c, bi=bi, co_=co_: e.matmul(
                                    banks[bi][:, co_:co_ + 128], lhsT=Qk(c), rhs=Pk(c), start=True, stop=True),
                                    r=pk_keys, w=["bank%d" % bi])
                            if k < 4:
                                for c in range(8):
                                    bi = 4 + c // 4
                                    co_ = (c % 4) * 128
                                    B.op("pe", lambda e, c=c, bi=bi, co_=co_: e.matmul(
                                        banks[bi][:, co_:co_ + 128], lhsT=Pk(c), rhs=Qk(c), start=True, stop=True),
                                        r=pk_keys, w=["bank%d" % bi])
                            for b2 in range(2):
                                B.op("act", lambda e, b2=b2: e.copy(
                                    out=nxt[:, 0, 4 * b2:4 * b2 + 4, :],
                                    in_=banks[2 + b2][:].rearrange("p (c j) -> p c j", c=4)),
                                    r=["bank%d" % (2 + b2)], w=[nk_])
                            if k < 4:
                                for b2 in range(2):
                                    B.op("dve", lambda e, b2=b2: e.tensor_copy(
                                        out=nxt[:, 1, 4 * b2:4 * b2 + 4, :],
                                        in_=banks[4 + b2][:].rearrange("p (c j) -> p c j", c=4)),
                                        r=["bank%d" % (4 + b2)], w=[nk_])

                        def apply(k):
                            nxt = PQ8[(k + 1) % 2]
                            nk_ = "PQ8_%d" % ((k + 1) % 2)
                            for c in range(8):
                                bi = 6 + c // 4
                                co_ = (c % 4) * 128
                                B.op("pe", lambda e, c=c, bi=bi, co_=co_: e.matmul(
                                    banks[bi][:, co_:co_ + 128], lhsT=nxt[:, 0, c, :], rhs=Yb8[:, c, :],
                                    start=True, stop=True), r=[nk_, "Yb8"], w=["bank%d" % bi])
                            for b2 in range(2):
                                B.op("dve", lambda e, b2=b2: e.tensor_tensor(
                                    out=Y32_8.ap(c0=4 * b2, c1=4 * b2 + 4),
                                    in0=banks[6 + b2][:].rearrange("p (c j) -> p c j", c=4),
                                    in1=Y32_8.ap(c0=4 * b2, c1=4 * b2 + 4), op=ALU.add),
                                    r=["bank%d" % (6 + b2)] + KEY_Y32, w=KEY_Y32)
                            if k < 4:
                                B.op("act", lambda e: e.copy(out=Yb8[:], in_=Y32_8.ap()), r=KEY_Y32, w=["Yb8"])
                            else:
                                B.op("act", lambda e: e.copy(out=YT[:], in_=Y32_8.ap()), r=KEY_Y32, w=["YT"])

                        squaring(0)
                        for (srcs, dst, dkey, in1f, skeys) in [
                                (lambda c, h: XQK[:, 0, c * 64:(c + 1) * 64], XKD, "XKD",
                                 lambda b2: pview(EX[:, 1, 4 * b2, kh:kh + 1], [[16, 4], [0, 128]]), [KK_, "EX"]),
                                (lambda c, h: VT[h][:, c * 64:(c + 1) * 64], VB, "VB",
                                 lambda b2: pview(SC[:, 0, 4 * b2:4 * b2 + 1], [[1, 4], [0, 128]]), ["VT0", "VT1", KSC[0]])]:
                            for b2 in range(2):
                                bi = b2
                                for cc in range(4):
                                    c = 4 * b2 + cc
                                    for h in range(2):
                                        B.op("pe", lambda e, c=c, cc=cc, h=h, bi=bi, srcs=srcs: e.matmul(
                                            banks[bi][h * 64:(h + 1) * 64, cc * 128:(cc + 1) * 128], lhsT=srcs(c, h),
                                            rhs=cb("ident"), start=True, stop=True), r=skeys[:-1] + ["CB"], w=["bank%d" % bi])
                                B.op("dve", lambda e, b2=b2, bi=bi, dst=dst, in1f=in1f: e.tensor_tensor(
                                    out=dst[:, 4 * b2:4 * b2 + 4, :], in0=banks[bi][:].rearrange("p (c j) -> p c j", c=4),
                                    in1=in1f(b2), op=ALU.mult), r=["bank%d" % bi, skeys[-1]], w=[dkey])
                        if prev_out is not None:
                            prev_out()
                        for k in range(1, 5):
                            squaring(k)
                            apply(k - 1)
                        apply(4)
                    if DN_LEVEL >= 6:
                        for c in range(8):
                            for h in range(2):
                                hp = slice(h * 64, (h + 1) * 64)
                                B.op("pe", lambda e, c=c, h=h, hp=hp, kh=kh: e.matmul(
                                    banks[6][hp, 0:128], lhsT=XQK[:, 0, c * 64:(c + 1) * 64],
                                    rhs=S2b[:, kh, h * 128:(h + 1) * 128], start=True, stop=True),
                                    r=[KK_, "S2b"], w=["bank6"])
                                B.op("pe", lambda e, c=c, h=h, hp=hp, kh=kh: e.matmul(
                                    banks[6][hp, 128:256], lhsT=XQK[:, 1, c * 64:(c + 1) * 64],
                                    rhs=S2b[:, kh, h * 128:(h + 1) * 128], start=True, stop=True),
                                    r=[KQ, "S2b"], w=["bank6"])
                            step_bg(bg)
                            B.op("dve", lambda e, c=c: e.scalar_tensor_tensor(
                                out=RB[:], in0=banks[6][:, 0:128], scalar=SC[:, 2, c:c + 1], in1=VB[:, c, :],
                                op0=ALU.mult, op1=ALU.add), r=["bank6", KSC[2], "VB"], w=["RB"])
                            B.op("pe", lambda e, c=c: e.matmul(banks[5][:, 0:128], lhsT=YT[:, c, :], rhs=RB[:],
                                                               start=True, stop=True), r=["YT", "RB"], w=["bank5"])
                            step_bg(bg)
                            B.op("act", lambda e: e.copy(out=VPB[:], in_=banks[5][:, 0:128]), r=["bank5"], w=["VPB"])
                            B.op("pe", lambda e, c=c: e.matmul(banks[7][:, 0:128], lhsT=MQ[:, c, :], rhs=VPB[:],
                                                               start=True, stop=True), r=["MQ", "VPB"], w=["bank7"])
                            B.op("act", lambda e, c=c: e.activation(out=T1[:], in_=banks[6][:, 128:256], func=AF.Identity,
                                                                    scale=SC[:, 4, c:c + 1]),
                                 r=["bank6", KSC[4]], w=["T1"])
                            B.op("dve", lambda e, c=c: e.scalar_tensor_tensor(
                                out=OB[:, c, :], in0=banks[7][:, 0:128], scalar=SC[:, 3, c:c + 1], in1=T1[:],
                                op0=ALU.mult, op1=ALU.add), r=["bank7", KSC[3], "T1"], w=["OB"])
                            sbk = [banks[3][:, 0:128], banks[4][:, 0:128]]
                            sbk_key = ["bank3", "bank4"]
                            for h in range(2):
                                hp = slice(h * 64, (h + 1) * 64)
                                B.op("pe", lambda e, c=c, h=h, hp=hp, sbk=sbk: e.matmul(
                                    sbk[h], lhsT=XKD[hp, c, :], rhs=VPB[hp, :],
                                    start=True, stop=True), r=["XKD", "VPB"], w=[sbk_key[h]])
                            step_bg(bg)
                            for h in range(2):
                                B.op("dve", lambda e, c=c, h=h, kh=kh, sbk=sbk: e.scalar_tensor_tensor(
                                    out=S2b[:, kh, h * 128:(h + 1) * 128], in0=S2[:, kh, h * 128:(h + 1) * 128],
                                    scalar=EX[:, 2 + h, c, kh:kh + 1], in1=sbk[h],
                                    op0=ALU.mult, op1=ALU.add), r=["S2", "EX", sbk_key[h]], w=["S2b"])
                            for h in range(2):
                                B.op("dve", lambda e, c=c, h=h, kh=kh, sbk=sbk: e.scalar_tensor_tensor(
                                    out=S2[:, kh, h * 128:(h + 1) * 128], in0=S2[:, kh, h * 128:(h + 1) * 128],
                                    scalar=EX[:, 2 + h, c, kh:kh + 1], in1=sbk[h],
                                    op0=ALU.mult, op1=ALU.add), r=["S2", "EX", sbk_key[h]], w=["S2"])
                            B.op("act", lambda e, c=c: e.activation(out=JK[:], in_=OB[:, c, :], func=AF.Square,
                                                                    accum_out=SSO[:, c, 0:1]), r=["OB"], w=["JK", "SSO"])

                    while step_bg(bg):
                        pass
                    def outnorm():
                        B.op("dve", lambda e: e.tensor_scalar(out=SSO[:, :, 1], in0=SSO[:, :, 0], scalar1=1.0 / 128.0,
                                                              scalar2=EPS, op0=ALU.mult, op1=ALU.add),
                             r=["SSO"], w=["SSO1"])
                        B.op("act", lambda e: e.activation(out=SSO[:, :, 2], in_=SSO[:, :, 1], func=AF.Ln),
                             r=["SSO1"], w=["SSO2"])
                        B.op("act", lambda e: e.activation(out=SSO[:, :, 3], in_=SSO[:, :, 2], func=AF.Exp, scale=-0.5),
                             r=["SSO2"], w=["SSO3"])
                        B.op("dve", lambda e: e.tensor_tensor(
                            out=ON3.ap(), in0=OB[:], in1=pview(SSO[:, 0, 3:4], [[4, 8], [0, 128]]), op=ALU.mult),
                            r=["OB", "SSO3"], w=["dnb"])
                        for c4 in range(2):
                            bi = c4
                            bkb = banks[bi][:].bitcast(BF16)
                            for cc in range(4):
                                c = c4 * 4 + cc
                                B.op("pe", lambda e, c=c, cc=cc, bkb=bkb: e.transpose(
                                    out=bkb[:, cc * 128:(cc + 1) * 128], in_=ON3.ap(c0=c, c1=c + 1), identity=cb("ident")),
                                    r=["dnb", "CB"], w=["bank%d" % bi])
                            for h in range(2):
                                B.op("dve", lambda e, bkb=bkb, h=h, c4=c4, kh=kh: e.scalar_tensor_tensor(
                                    out=XOb_view(2 * kh + h, c4 * 256, 256).rearrange("p (c i) -> p c i", c=4),
                                    in0=pview(bkb[:, h * 64:h * 64 + 1], [[128, 4], [1, 64]]), scalar=pr("dnw", 0),
                                    in1=ZS[h][:, c4 * 256:(c4 + 1) * 256].rearrange("p (c i) -> p c i", c=4),
                                    op0=ALU.mult, op1=ALU.mult), r=["bank%d" % bi, "PR", KZ[h]], w=["XO"])
                    return outnorm

                def pars(kh):
                    p_ = kh % 2
                    return (XQKs[p_], ZSs[p_], "XQKq%d" % p_, "XQKk%d" % p_, ["ZS%d_%d" % (p_, h) for h in range(2)],
                            RSs[p_], SCs[p_], "RS%d" % p_, ["SC%d_%d" % (p_, i) for i in range(5)])

                g0 = gen_proj(0, *pars(0), lag=False)
                while step_bg(g0):
                    pass
                prev_out = None
                for kh in range(16):
                    bg = gen_proj(kh + 1, *pars(kh + 1)) if kh < 15 else None
                    prev_out = kh_body(kh, *pars(kh), bg, prev_out)
                prev_out()

                B.mute = DN_SUB < 7
                for d in range(16):
                    sg = w_next()
                    big = proj_fm(sg, lambda c: hT[:, c, :], bankset=(0, 1))
                    gi = 0
                    B.op("act", lambda e, big=big, gi=gi: e.activation(out=gsb[:, gi, :], in_=banks[big][:],
                                                                       func=AF.Sigmoid),
                         r=["bank%d" % big], w=["gsb%d" % gi])
                    w_prefetch()
                    s1 = w_next()
                    s2 = w_next()
                    bia = 2 + (d % 2)
                    for c in range(32):
                        sl_ = s1 if c < 16 else s2
                        B.op("pe", lambda e, c=c, bia=bia, sl_=sl_: e.matmul(
                            banks[bia][:], lhsT=ring[:, sl_, c % 16, :], rhs=XOb_view(c, 0, TT),
                            start=(c == 0), stop=(c == 31)), r=["ring%d" % sl_, "XO"], w=["bank%d" % bia])
                    B.op("dve", lambda e, bia=bia, gi=gi: e.tensor_tensor(
                        out=ogf[:], in0=banks[bia][:], in1=gsb[:, gi, :], op=ALU.mult),
                        r=["bank%d" % bia, "gsb%d" % gi], w=["ogf"])
                    B.op("pool", lambda e, gi=gi, d=d: e.tensor_tensor(
                        out=m1T[:, d, :], in0=ogf[:], in1=m1T[:, d, :], op=ALU.add),
                        r=["ogf", "m1T"], w=["m1T"])
                    w_prefetch()
            else:
                for _ in range(N_DN_BLK):
                    w_next()
                    w_prefetch()
            B.mute = False
            while wstate["next_use"] % 4 != 0:
                w_next()
                w_prefetch()

            if SKIP_P3:
                continue
            B.op("sp", lambda e, t0=t0: e.dma_start(
                out=XO[:], in_=x_d[t0:t0 + TT, :].rearrange("(t p) d -> p t d", p=128)), r=[], w=["XO"], dma="x")
            for cg in range(4):
                sls = [w_next() for _ in range(4)]
                assert sls[0] % 4 == 0 and sls == list(range(sls[0], sls[0] + 4))
                for tb in range(4):
                    bi = 2 + ((cg * 4 + tb) % 4)
                    for c in range(16):
                        rhs = AP(ring, ring[:, sls[0], c, :].offset, [list(ring[:].ap[0]), [16 * 128, 4], [1, 128]])
                        B.op("pe", lambda e, bi=bi, c=c, tb=tb, rhs=rhs: e.matmul(
                            banks[bi][:], lhsT=m1T[:, c, tb * 128:(tb + 1) * 128], rhs=rhs,
                            start=(c == 0), stop=(c == 15)),
                            r=["m1T"] + ["ring%d" % s_ for s_ in sls], w=["bank%d" % bi])
                    B.op("dve", lambda e, bi=bi, tb=tb, cg=cg: e.tensor_tensor(
                        out=XO[:, tb, cg * 512:(cg + 1) * 512], in0=banks[bi][:], in1=XO[:, tb, cg * 512:(cg + 1) * 512],
                        op=ALU.add), r=["bank%d" % bi, "XO"], w=["XO"])
                w_prefetch()
            for tb in range(4):
                B.op("act", lambda e, tb=tb: e.activation(out=hb[:], in_=XO[:, tb, :], func=AF.Square,
                                                          accum_out=st1[:, 0:1]), r=["XO"], w=["hb", "st1"])
                B.op("dve", lambda e: e.tensor_scalar(out=st1[:, 1:2], in0=st1[:, 0:1], scalar1=1.0 / D, scalar2=EPS,
                                                      op0=ALU.mult, op1=ALU.add), r=["st1"], w=["st1b"])
                B.op("act", lambda e: e.activation(out=st1[:, 2:3], in_=st1[:, 1:2], func=AF.Sqrt),
                     r=["st1b"], w=["st1c"])
                B.op("dve", lambda e: e.reciprocal(out=st1[:, 3:4], in_=st1[:, 2:3]), r=["st1c"], w=["st1d"])
                B.op("dve", lambda e, tb=tb: e.scalar_tensor_tensor(
                    out=XO[:, tb, :], in0=XO[:, tb, :], scalar=st1[:, 3:4], in1=FW[:], op0=ALU.mult, op1=ALU.mult),
                    r=["XO", "st1d", "FW"], w=["XO"])
                tok = B.op("sp", lambda e, tb=tb, t0=t0: e.dma_start(
                    out=y_d[t0 + tb * 128:t0 + (tb + 1) * 128, :], in_=XO[:, tb, :]), r=["XO"], dma="o%d" % tb)
                out_toks.append(tok)

        B.wait_all("sp", out_toks)
        block = stack.enter_context(nc.Block())
        B.emit(nc, block, stack)
    return nc


_CACHE = {}


def kernel(x, norm_w, w_in, b_qkv, sinks, conv_w, a_log, dt_bias, dn_norm_w,
           w_att_branch, w_dn_branch, w_out, final_norm_w):
    x = np.asarray(x, np.float32)
    wst = build_wstream(np.asarray(w_in[0], np.float32), np.asarray(w_att_branch[0], np.float32),
                        np.asarray(w_dn_branch[0], np.float32), np.asarray(w_out[0], np.float32))
    cfa = consts_f32()
    cba = consts_b16(np.asarray(b_qkv[0], np.float32))
    pra = params_f32(np.asarray(norm_w[0]), np.asarray(b_qkv[0]), np.asarray(sinks[0]), np.asarray(conv_w[0]),
                     np.asarray(a_log[0]), np.asarray(dt_bias[0]), np.asarray(dn_norm_w[0]))
    fwa = np.ascontiguousarray(np.broadcast_to(np.asarray(final_norm_w, np.float32)[None, :], (128, D)))
    n_wblk = wst.shape[0]
    if "nc" not in _CACHE:
        _CACHE["nc"] = build_program(n_wblk)
    nc = _CACHE["nc"]
    in_maps = [{"x": np.ascontiguousarray(x[b]), "wst": wst, "cf": cfa, "cb": cba, "pr": pra, "fw": fwa}
               for b in range(8)]
    res = run_bass_kernel_spmd(nc, in_maps, core_ids=list(range(8)))
    return np.stack([res.results[b]["y"] for b in range(8)], axis=0).astype(np.float32)
```

```python
import contextlib
import numpy as np
import concourse.bass as bass
import concourse.mybir as mybir
from concourse.bass_utils import run_bass_kernel_spmd

F32 = mybir.dt.float32
BF16 = mybir.dt.bfloat16
AF = mybir.ActivationFunctionType
ALU = mybir.AluOpType
AX = mybir.AxisListType

S = 2048
D = 2048
TT = 512
NT = S // TT
P = 128
NSLOT = 8
EPS = 1e-6
OFF_AQ = 0
OFF_AK = 2048
OFF_AV = 2304
OFF_AZ = 2560
OFF_DQKV = 4608
OFF_DZ = 12800
OFF_DB = 16896
OFF_DA = 16928
OFF_G = 16960
NEG = -30000.0
DGE_SCRATCH = 4096

ENABLE_DN = True
NT_RUN = NT
SKIP_P3 = False
DN_LEVEL = 9
DN_SUB = 99
LAGS = (0, 1, 4)
import os as _os
BISV = int(_os.environ.get('BISV', '0'))


class Builder:
    ENG = ["pe", "act", "dve", "pool", "sp"]

    def __init__(self):
        self.ops = {e: [] for e in self.ENG}
        self.lastw = {}
        self.readers = {}
        self.dma_cnt = {}
        self.mute = False

    def op(self, eng, fn, r=(), w=(), dma=None):
        if self.mute and dma is None:
            return None
        deps = set()
        for k in r:
            t = self.lastw.get(k)
            if t is not None:
                deps.add(t)
            if k.startswith("bank"):
                for t in self.readers.get(k, ()):
                    if not (t[0] == "e" and t[1] == eng):
                        deps.add(t)
        for k in w:
            t = self.lastw.get(k)
            if t is not None:
                deps.add(t)
            for t in self.readers.get(k, ()):
                deps.add(t)
        idx = len(self.ops[eng])
        if dma is not None:
            self.dma_cnt[dma] = self.dma_cnt.get(dma, 0) + 1
            tok = ("d", dma, 16 * self.dma_cnt[dma])
        else:
            tok = ("e", eng, idx)
        self.ops[eng].append({"fn": fn, "deps": deps, "dma": dma, "sig": False})
        for k in w:
            self.lastw[k] = tok
            self.readers[k] = []
        for k in r:
            if k in w:
                continue
            lst = self.readers.setdefault(k, [])
            if tok[0] == "e":
                lst[:] = [t for t in lst if not (t[0] == "e" and t[1] == eng)]
            lst.append(tok)
        return tok

    def wait_all(self, eng, toks):
        self.ops[eng].append({"fn": None, "deps": set(toks), "dma": None, "sig": False})

    def emit(self, nc, block, stack):
        for e in self.ENG:
            for rec in self.ops[e]:
                for t in rec["deps"]:
                    if t[0] == "e" and not (t[1] == "pe" and e == "pe"):
                        self.ops[t[1]][t[2]]["sig"] = True
        sem_e = {e: stack.enter_context(nc.semaphore("se_" + e)) for e in self.ENG}
        sem_d = {k: stack.enter_context(nc.semaphore("sd_" + k)) for k in self.dma_cnt}
        for e in self.ENG:
            c = 0
            for rec in self.ops[e]:
                if rec["sig"]:
                    c += 1
                rec["sv"] = c
        ops = self.ops

        def gen(e):
            def run(eh):
                waited = {}
                for rec in ops[e]:
                    need = {}
                    for t in rec["deps"]:
                        if t[0] == "e":
                            if t[1] == "pe" and e == "pe":
                                continue
                            key = ("e", t[1])
                            val = ops[t[1]][t[2]]["sv"]
                        else:
                            key = ("d", t[1])
                            val = t[2]
                        if val > need.get(key, 0):
                            need[key] = val
                    for key, val in need.items():
                        if waited.get(key, 0) >= val:
                            continue
                        waited[key] = val
                        sem = sem_e[key[1]] if key[0] == "e" else sem_d[key[1]]
                        eh.wait_ge(sem, val)
                    if rec["fn"] is None:
                        continue
                    ins = rec["fn"](eh)
                    if rec["dma"] is not None:
                        ins.then_inc(sem_d[rec["dma"]], 16)
                    elif rec["sig"]:
                        ins.then_inc(sem_e[e], 1)
            return run

        block.tensor(gen("pe"))
        block.scalar(gen("act"))
        block.vector(gen("dve"))
        block.gpsimd(gen("pool"))
        block.sync(gen("sp"))


def AP(t, off, dims):
    return bass.AP(t, off, [list(d) for d in dims])


def dup2(ap):
    a = [list(d) for d in ap.ap]
    assert len(a) == 2
    return bass.AP(ap.tensor, ap.offset, [a[0], [0, 2], a[1]])


def wblock(w, r0, cols):
    sub = w[r0:r0 + 2048][:, cols]
    return np.ascontiguousarray(sub.reshape(16, 128, 128).transpose(1, 0, 2))


def build_wstream(w_in, w_att, w_dn, w_out):
    blocks = []
    ar = np.arange

    def cin(c0, n=128):
        return ar(c0, c0 + n)

    blocks.append(wblock(w_in, 0, cin(OFF_AV)))
    blocks.append(wblock(w_in, 0, cin(OFF_AV + 128)))
    for g in range(4):
        kc = cin(OFF_AK + 64 * g, 64)
        blocks.append(wblock(w_in, 0, np.concatenate([kc, kc])))
        for j in range(4):
            blocks.append(wblock(w_in, 0, cin(OFF_AQ + 128 * (4 * g + j))))
        for j in range(4):
            blocks.append(wblock(w_in, 0, cin(OFF_AZ + 128 * (4 * g + j))))
    for d in range(16):
        blocks.append(wblock(w_in, 0, cin(OFF_G + 128 * d)))
        blocks.append(wblock(w_att, 0, cin(128 * d)))
    ba = cin(OFF_DB, 64)
    blocks.append(wblock(w_in, 0, np.concatenate([ba, ba])))
    for kh in range(16):
        blocks.append(wblock(w_in, 0, cin(OFF_DQKV + 128 * kh)))
        blocks.append(wblock(w_in, 0, cin(OFF_DQKV + 2048 + 128 * kh)))
        for h in range(2):
            blocks.append(wblock(w_in, 0, cin(OFF_DQKV + 4096 + 128 * (2 * kh + h))))
        for h in range(2):
            blocks.append(wblock(w_in, 0, cin(OFF_DZ + 128 * (2 * kh + h))))
    for d in range(16):
        blocks.append(wblock(w_in, 0, cin(OFF_G + 2048 + 128 * d)))
        blocks.append(wblock(w_dn, 0, cin(128 * d)))
        blocks.append(wblock(w_dn, 2048, cin(128 * d)))
    while len(blocks) % 4 != 0:
        blocks.append(blocks[-1])
    for cg in range(4):
        for j in range(4):
            blocks.append(wblock(w_out, 0, cin(512 * cg + 128 * j)))
    return np.ascontiguousarray(np.stack(blocks).reshape(len(blocks), 128, 2048))


N_ATT_BLK = 2 + 4 * 9 + 32
N_DN_BLK = 1 + 96 + 48


def consts_f32():
    p = np.arange(128)
    h = p // 64
    t = p % 64
    same = (h[:, None] == h[None, :])
    bdu = (same & (t[:, None] <= t[None, :])).astype(np.float32)
    bdsu = (same & (t[:, None] > t[None, :])).astype(np.float32)
    hsel0 = np.repeat((h == 0)[:, None], 128, 1).astype(np.float32)
    hsel1 = np.repeat((h == 1)[:, None], 128, 1).astype(np.float32)
    bdones = same.astype(np.float32)
    ident = np.eye(128, dtype=np.float32)
    i64 = np.arange(64)
    u2 = (t[:, None] <= i64[None, :]).astype(np.float32)
    ones64 = np.ones((128, 64), np.float32)
    masks = np.where(i64[None, :] < t[:, None], 0.0, NEG).astype(np.float32)
    maskt = np.where(i64[None, :] >= t[:, None], 0.0, NEG).astype(np.float32)
    parts = [ident, bdu, bdsu, hsel0, hsel1, bdones, -bdones, u2, ones64, -ones64, masks, maskt]
    return np.ascontiguousarray(np.concatenate(parts, axis=1))


CF_OFF = {}
_o = 0
for _n, _w in [("ident", 128), ("bdu", 128), ("bdsu", 128), ("hsel0", 128), ("hsel1", 128), ("bdones", 128),
               ("bdneg", 128), ("u2", 64), ("ones64", 64), ("neg64", 64), ("masks", 64), ("maskt", 64)]:
    CF_OFF[_n] = (_o, _w)
    _o += _w
NCF = _o

CB_OFF = {}
_o = 0
for _n, _w in [("ident", 128), ("mown", 128), ("mprev", 128), ("ones64", 64), ("onescol", 2), ("onesrow", 128),
               ("bv", 256), ("bdones", 128), ("bdneg", 128), ("neg64", 64), ("masks", 64), ("maskt", 64)]:
    CB_OFF[_n] = (_o, _w)
    _o += _w
NCB = _o


def consts_b16(b_qkv):
    j = np.arange(128)[:, None]
    i = np.arange(128)[None, :]
    ident = np.eye(128, dtype=np.float32)
    mown = (j <= i).astype(np.float32)
    mprev = (j > i).astype(np.float32)
    ones64 = np.ones((128, 64), np.float32)
    onescol = np.ones((128, 2), np.float32)
    onesrow = np.zeros((128, 128), np.float32)
    onesrow[0, :] = 1.0
    bv = np.zeros((128, 256), np.float32)
    bv[0, :] = b_qkv[2304:2560]
    cfull = consts_f32()
    extra = [cfull[:, CF_OFF[n][0]:CF_OFF[n][0] + CF_OFF[n][1]] for n in ("bdones", "bdneg", "neg64", "masks", "maskt")]
    return np.ascontiguousarray(np.concatenate([ident, mown, mprev, ones64, onescol, onesrow, bv] + extra, axis=1))


PR_OFF = {}
_o = 0
for _n, _w in [("nw", 16), ("bq", 16), ("bk", 4), ("sk", 16), ("cw", 256), ("dtb", 16), ("alg", 16), ("dnw", 1)]:
    PR_OFF[_n] = (_o, _w)
    _o += _w
NPR = _o


def params_f32(norm_w, b_qkv, sinks, conv_w, a_log, dt_bias, dn_norm_w):
    p = np.arange(128)
    hi = (p >= 64).astype(np.int64)
    nw = norm_w.reshape(16, 128).T
    bq = b_qkv[0:2048].reshape(16, 128).T
    bk = b_qkv[2048:2304].reshape(4, 64)[:, p % 64].T
    sk = sinks[(2 * np.arange(16))[None, :] + hi[:, None]]
    cw = conv_w.reshape(4, 64, 128).transpose(2, 1, 0).reshape(128, 256)
    dtb = dt_bias[(2 * np.arange(16))[None, :] + hi[:, None]]
    alg = a_log[(2 * np.arange(16))[None, :] + hi[:, None]]
    dnw = dn_norm_w.reshape(128, 1)
    return np.ascontiguousarray(np.concatenate([nw, bq, bk, sk, cw, dtb, alg, dnw], axis=1).astype(np.float32))


def build_program(n_wblk):
    nc = bass.Bass("TRN2", target_bir_lowering=False, dynamic_dma_scratch_size=DGE_SCRATCH)
    x_d = nc.dram_tensor("x", [S, D], F32, kind="ExternalInput").ap()
    w_d = nc.dram_tensor("wst", [n_wblk, 128, 2048], F32, kind="ExternalInput").ap()
    cf_d = nc.dram_tensor("cf", [128, NCF], F32, kind="ExternalInput").ap()
    cb_d = nc.dram_tensor("cb", [128, NCB], F32, kind="ExternalInput").ap()
    pr_d = nc.dram_tensor("pr", [128, NPR], F32, kind="ExternalInput").ap()
    fw_d = nc.dram_tensor("fw", [128, D], F32, kind="ExternalInput").ap()
    y_d = nc.dram_tensor("y", [S, D], F32, kind="ExternalOutput").ap()

    B = Builder()
    stack = contextlib.ExitStack()
    with stack:
        def sb(name, shape, dt):
            return stack.enter_context(nc.sbuf_tensor(name, shape, dt))

        def ps(name, shape=(128, 512), dt=F32):
            return stack.enter_context(nc.psum_tensor(name, list(shape), dt))

        ring = sb("ring", [128, NSLOT, 16, 128], BF16)
        hT = sb("hT", [128, 16, TT], BF16)
        m1T = sb("m1T", [128, 16, TT], BF16)
        XO = sb("XO", [128, 4, D], F32)
        XOb = XO[:].bitcast(BF16)
        FW = sb("FW", [128, D], F32)
        CF = sb("CF", [128, NCF], F32)
        CB = sb("CB", [128, NCB], BF16)
        PR = sb("PR", [128, NPR], F32)
        hb = sb("hb", [128, D], BF16)
        st1 = sb("st1", [128, 8], F32)
        qT = sb("qT", [128, 4, TT], BF16)
        zT = sb("zT", [128, 4, TT], BF16)
        kT2 = sb("kT2", [128, 4, 128 + TT], BF16)
        vtok = sb("vtok", [128, 5, 256], BF16)
        pT = sb("pT", [128, 4, 512], BF16)
        dnb = sb("dnb", [128, 512], F32)
        ogf = sb("ogf", [128, 512], F32)
        gsb = sb("gsb", [128, 1, 512], F32)
        ES = sb("ES", [128, 16], F32)
        NEGA = sb("NEGA", [128, 16], F32)

        S2 = sb("S2", [128, 16, 256], F32)
        S2b = sb("S2b", [128, 16, 256], BF16)
        HALO = sb("HALO", [128, 64, 4], BF16)
        BET = sb("BET", [128, 8, 16], F32)
        GST = sb("GST", [128, 8, 16], F32)
        EX = sb("EX", [128, 4, 8, 16], F32)
        CSb = [sb("CSb%d" % i, [128, 516], BF16) for i in range(2)]
        DIAG = [sb("DIAG%d" % i, [128, 4, 128], BF16) for i in range(2)]
        ACC = [sb("ACC0", [128, 512], F32)]
        XQKs = [sb("XQK%d" % i, [128, 2, TT], BF16) for i in range(2)]
        VT = [sb("VT%d" % i, [128, TT], BF16) for i in range(2)]
        ZSs = [[sb("ZS%d_%d" % (p_, i), [128, TT], BF16) for i in range(2)] for p_ in range(2)]
        RSs = [sb("RS%d" % i, [128, 8, 2], F32) for i in range(2)]
        SCs = [sb("SC%d" % i, [128, 5, 8], F32) for i in range(2)]
        XKD = sb("XKD", [128, 8, 128], BF16)
        VB = sb("VB", [128, 8, 128], F32)
        GP8h = sb("GP8h", [128, 8, 64], BF16)
        GP8l = sb("GP8l", [128, 8, 64], BF16)
        GHL = sb("GHL", [128, 2, 8], F32)
        GHb = sb("GHb", [128, 8], BF16)
        P0B8 = sb("P0B8", [128, 8, 128], BF16)
        Q0B8 = sb("Q0B8", [128, 8, 128], BF16)
        PQ8 = [sb("PQ8_%d" % i, [128, 2, 8, 128], BF16) for i in range(2)]
        Yb8 = sb("Yb8", [128, 8, 128], BF16)
        YT = sb("YT", [128, 8, 128], BF16)
        MQ = sb("MQ", [128, 8, 128], BF16)
        RB = sb("RB", [128, 128], BF16)
        VPB = sb("VPB", [128, 128], BF16)
        T1 = sb("T1", [128, 128], F32)
        OB = sb("OB", [128, 8, 128], F32)
        JK = sb("JK", [128, 128], BF16)
        SSO = sb("SSO", [128, 8, 4], F32)

        class V3:
            def __init__(self, base_ap, C, J):
                self.t = base_ap.tensor
                self.off = base_ap.offset
                self.ps = base_ap.ap[0][0]
                self.C, self.J = C, J

            def ap(self, p0=0, p1=128, c0=0, c1=None, j0=0, j1=None):
                c1 = self.C if c1 is None else c1
                j1 = self.J if j1 is None else j1
                dims = [[self.ps, p1 - p0]]
                if c1 - c0 > 1:
                    dims.append([self.J, c1 - c0])
                dims.append([1, j1 - j0])
                return AP(self.t, self.off + p0 * self.ps + c0 * self.J + j0, dims)

        _qf = qT[:].rearrange("p a b -> p (a b)")
        GBD8h = V3(_qf[:, 0:1024], 8, 128)
        GBD8l = V3(_qf[:, 1024:2048], 8, 128)
        _zf = zT[:].bitcast(F32).rearrange("p a b -> p (a b)")
        DSSa = V3(_zf[:, 0:512], 8, 64)
        DSSb = V3(_zf[:, 512:1024], 8, 64)
        Y32_8 = V3(pT[:].bitcast(F32), 8, 128)
        ON3 = V3(dnb[:].bitcast(BF16), 8, 128)
        SQ3 = V3(ogf[:].bitcast(BF16), 2, TT)

        banks = [ps("bank%d" % i) for i in range(8)]

        def cf(name):
            o, w = CF_OFF[name]
            return CF[:, o:o + w]

        def cb(name):
            o, w = CB_OFF[name]
            return CB[:, o:o + w]

        def pr(name, j=None, n=1):
            o, w = PR_OFF[name]
            if j is None:
                return PR[:, o:o + w]
            return PR[:, o + j:o + j + n]

        xob_t = XOb.tensor
        xob_ps = XOb.ap[0][0]

        def XOb_view(b, t0, n):
            return AP(xob_t, XOb.offset + b * TT + t0, [[xob_ps, 128], [1, n]])

        def XOb_view3(b0, nb, t0, n):
            return AP(xob_t, XOb.offset + b0 * TT + t0, [[xob_ps, 128], [TT, nb], [1, n]])

        B.op("sp", lambda e: e.dma_start(out=CF[:], in_=cf_d), w=["CF"], dma="c0")
        B.op("sp", lambda e: e.dma_start(out=PR[:], in_=pr_d), w=["PR"], dma="c1")
        B.op("sp", lambda e: e.dma_start(out=FW[:], in_=fw_d), w=["FW"], dma="c2")
        B.op("pool", lambda e: e.dma_start(out=CB[:], in_=cb_d), w=["CB"], dma="c3")
        B.op("act", lambda e: e.activation(out=ES[:], in_=pr("sk"), func=AF.Exp), r=["PR"], w=["ES"])
        B.op("act", lambda e: e.activation(out=NEGA[:], in_=pr("alg"), func=AF.Exp), r=["PR"], w=["NEGA"])
        B.op("dve", lambda e: e.tensor_scalar(out=NEGA[:], in0=NEGA[:], scalar1=-1.0, scalar2=None, op0=ALU.mult),
             r=["NEGA"], w=["NEGA"])
        B.op("pool", lambda e: e.memset(kT2[:], 0.0), w=["kT2"])
        B.op("pool", lambda e: e.memset(vtok[:], 0.0), w=["vtok"])
        B.op("pool", lambda e: e.memset(S2[:], 0.0), w=["S2"])
        B.op("pool", lambda e: e.memset(S2b[:], 0.0), w=["S2b"])
        B.op("pool", lambda e: e.memset(HALO[:], 0.0), w=["HALO"])
        B.op("pool", lambda e: e.memset(MQ[:], 0.0), w=["MQ"])
        B.op("pool", lambda e: e.memset(P0B8[:], 0.0), w=["P0B8"])

        wstate = {"next_load": 0, "next_use": 0}
        n_total = n_wblk * 0 + 0

        def w_load_upto(k):
            while wstate["next_load"] < k:
                i = wstate["next_load"]
                sl = i % NSLOT
                src = w_d[i % n_wblk]
                B.op("pool", (lambda e, sl=sl, src=src: e.dma_start(
                    out=ring[:, sl, :, :].rearrange("p c n -> p (c n)"), in_=src)),
                    w=["ring%d" % sl], dma="w%d" % sl)
                wstate["next_load"] += 1

        def w_next():
            i = wstate["next_use"]
            wstate["next_use"] += 1
            w_load_upto(i + 1)
            return i % NSLOT

        def w_prefetch():
            lim = min(wstate["next_use"] + NSLOT, n_wblk * NT_RUN)
            w_load_upto(min(lim, wstate["next_use"] + NSLOT))

        pbank = {"i": 0}

        def proj_fm(slot, rhs_fn, nk=16, bankset=(0, 1)):
            bi = bankset[pbank["i"] % len(bankset)]
            pbank["i"] += 1
            bk = banks[bi]
            for c in range(nk):
                B.op("pe", (lambda e, c=c, bk=bk: e.matmul(bk[:], lhsT=ring[:, slot, c, :], rhs=rhs_fn(c),
                                                           start=(c == 0), stop=(c == nk - 1))),
                     r=["ring%d" % slot, "hT"], w=["bank%d" % bi])
            return bi

        out_toks = []

        for T in range(NT_RUN):
            t0 = T * TT
            for tb in range(4):
                B.op("sp", lambda e, t0=t0, tb=tb: e.dma_start(
                    out=XO[:, tb, :], in_=x_d[t0 + tb * 128:t0 + (tb + 1) * 128, :]),
                    w=["XO", "XOx%d" % tb], dma="x%d" % tb)
            for tb in range(4):
                B.op("act", lambda e, tb=tb: e.activation(out=hb[:], in_=XO[:, tb, :], func=AF.Square,
                                                          accum_out=st1[:, 0:1]), r=["XOx%d" % tb], w=["hb", "st1"])
                B.op("dve", lambda e: e.tensor_scalar(out=st1[:, 1:2], in0=st1[:, 0:1], scalar1=1.0 / D, scalar2=EPS,
                                                      op0=ALU.mult, op1=ALU.add), r=["st1"], w=["st1b"])
                B.op("act", lambda e: e.activation(out=st1[:, 2:3], in_=st1[:, 1:2], func=AF.Sqrt),
                     r=["st1b"], w=["st1c"])
                B.op("dve", lambda e: e.reciprocal(out=st1[:, 3:4], in_=st1[:, 2:3]), r=["st1c"], w=["st1d"])
                B.op("dve", lambda e, tb=tb: e.tensor_scalar(out=hb[:], in0=XO[:, tb, :], scalar1=st1[:, 3:4],
                                                             scalar2=None, op0=ALU.mult),
                     r=["XOx%d" % tb, "st1d"], w=["hb"])
                for c4 in range(4):
                    bi = 2 + (c4 % 2)
                    bkb = banks[bi][:].bitcast(BF16)
                    for cc in range(4):
                        c = c4 * 4 + cc
                        B.op("pe", lambda e, c=c, cc=cc, bkb=bkb: e.transpose(
                            out=bkb[:, cc * 128:(cc + 1) * 128], in_=hb[:, c * 128:(c + 1) * 128], identity=cb("ident")),
                            r=["hb", "CB"], w=["bank%d" % bi])
                    for cc in range(4):
                        c = c4 * 4 + cc
                        B.op("dve", lambda e, c=c, cc=cc, bkb=bkb, tb=tb: e.tensor_scalar(
                            out=hT[:, c, tb * 128:(tb + 1) * 128], in0=bkb[:, cc * 128:(cc + 1) * 128],
                            scalar1=pr("nw", c), scalar2=None, op0=ALU.mult),
                            r=["bank%d" % bi, "PR"], w=["hT"])

            if T > 0:
                B.op("pool", lambda e: e.tensor_copy(out=kT2[:, :, 0:128], in_=kT2[:, :, TT:TT + 128]),
                     r=["kT2"], w=["kT2"])
                B.op("pool", lambda e: e.tensor_copy(out=vtok[:, 0, :], in_=vtok[:, 4, :]), r=["vtok"], w=["vtok"])
            sv = [w_next(), w_next()]
            for tb in range(4):
                bi = 2 + (tb % 2)
                for half in range(2):
                    o0 = half * 128
                    B.op("pe", lambda e, bi=bi, o0=o0: e.matmul(
                        banks[bi][:, o0:o0 + 128], lhsT=CB[0:1, CB_OFF["onesrow"][0]:CB_OFF["onesrow"][0] + 128],
                        rhs=CB[0:1, CB_OFF["bv"][0] + o0:CB_OFF["bv"][0] + o0 + 128], start=True, stop=False),
                        r=["CB"], w=["bank%d" % bi])
                    for c in range(16):
                        B.op("pe", lambda e, bi=bi, o0=o0, c=c, tb=tb, sl=sv[half]: e.matmul(
                            banks[bi][:, o0:o0 + 128], lhsT=hT[:, c, tb * 128:(tb + 1) * 128], rhs=ring[:, sl, c, :],
                            start=False, stop=(c == 15)), r=["hT", "ring%d" % sv[half]], w=["bank%d" % bi])
                B.op("act", lambda e, bi=bi, tb=tb: e.copy(out=vtok[:, 1 + tb, :], in_=banks[bi][:, 0:256]),
                     r=["bank%d" % bi], w=["vtok"])
            w_prefetch()

            for g in range(4):
                sl = w_next()
                bi = proj_fm(sl, lambda c: hT[:, c, :])
                B.op("dve", lambda e, bi=bi, g=g: e.tensor_scalar(
                    out=kT2[:, g, 128:128 + TT], in0=banks[bi][:], scalar1=pr("bk", g), scalar2=None, op0=ALU.add),
                    r=["bank%d" % bi, "PR"], w=["kT2"])
                w_prefetch()
                for j in range(4):
                    sl = w_next()
                    bi = proj_fm(sl, lambda c: hT[:, c, :])
                    B.op("dve", lambda e, bi=bi, g=g, j=j: e.tensor_scalar(
                        out=qT[:, j, :], in0=banks[bi][:], scalar1=pr("bq", 4 * g + j), scalar2=0.125,
                        op0=ALU.add, op1=ALU.mult), r=["bank%d" % bi, "PR"], w=["qT"])
                    w_prefetch()
                for j in range(4):
                    sl = w_next()
                    bi = proj_fm(sl, lambda c: hT[:, c, :])
                    B.op("act", lambda e, bi=bi, j=j: e.activation(out=zT[:, j, :], in_=banks[bi][:], func=AF.Silu),
                         r=["bank%d" % bi], w=["zT"])
                    w_prefetch()
                for qb in range(4):
                    nglob = 4 * T + qb
                    kinds = ["own"] + (["prev"] if nglob > 0 else [])
                    for half in range(2):
                        pp = slice(half * 64, half * 64 + 64)
                        for ki, kind in enumerate(kinds):
                            bi = 2 + half * 2 + ki
                            kcol = 128 + qb * 128 if kind == "own" else qb * 128
                            B.op("pe", lambda e, bi=bi, pp=pp, kcol=kcol, g=g, qb=qb: e.matmul(
                                banks[bi][:], lhsT=kT2[pp, g, kcol:kcol + 128],
                                rhs=qT[pp, :, qb * 128:(qb + 1) * 128], start=True, stop=True),
                                r=["kT2", "qT"], w=["bank%d" % bi])
                            pi = half * 2 + ki
                            B.op("act", lambda e, bi=bi, pi=pi: e.activation(
                                out=pT[:, pi, :], in_=banks[bi][:], func=AF.Exp), r=["bank%d" % bi], w=["pT%d" % pi])
                            mk = cb("mown") if kind == "own" else cb("mprev")
                            mk3 = AP(mk.tensor, mk.offset, [list(mk.ap[0]), [0, 4], [1, 128]])
                            B.op("dve", lambda e, pi=pi, mk3=mk3: e.tensor_tensor(
                                out=pT[:, pi, :].rearrange("p (h q) -> p h q", h=4),
                                in0=pT[:, pi, :].rearrange("p (h q) -> p h q", h=4), in1=mk3, op=ALU.mult),
                                r=["pT%d" % pi, "CB"], w=["pT%d" % pi])
                    for half in range(2):
                        po = slice(half * 64, half * 64 + 64)
                        for ki, kind in enumerate(kinds):
                            pi = half * 2 + ki
                            vb_ = 1 + qb if kind == "own" else qb
                            B.op("pe", lambda e, po=po, pi=pi, vb_=vb_, g=g, ki=ki, nk=len(kinds): e.matmul(
                                banks[6][po, :], lhsT=vtok[:, vb_, g * 64:(g + 1) * 64], rhs=pT[:, pi, :],
                                start=(ki == 0), stop=(ki == nk - 1)), r=["vtok", "pT%d" % pi], w=["bank6"])
                        for ki, kind in enumerate(kinds):
                            pi = half * 2 + ki
                            B.op("pe", lambda e, po=po, pi=pi, ki=ki, nk=len(kinds): e.matmul(
                                banks[7][po, :], lhsT=cb("ones64"), rhs=pT[:, pi, :],
                                start=(ki == 0), stop=(ki == nk - 1)), r=["CB", "pT%d" % pi], w=["bank7"])
                    es3 = AP(ES, 4 * g, [[16, 128], [1, 4], [0, 128]])
                    B.op("dve", lambda e, es3=es3: e.tensor_tensor(
                        out=dnb[:].rearrange("p (h q) -> p h q", h=4),
                        in0=banks[7][:].rearrange("p (h q) -> p h q", h=4), in1=es3, op=ALU.add),
                        r=["bank7", "ES"], w=["dnb"])
                    B.op("act", lambda e: e.activation(out=dnb[:], in_=dnb[:], func=AF.Ln), r=["dnb"], w=["dnb"])
                    B.op("act", lambda e: e.activation(out=dnb[:], in_=dnb[:], func=AF.Exp, scale=-1.0),
                         r=["dnb"], w=["dnb"])
                    B.op("dve", lambda e: e.tensor_tensor(out=ogf[:], in0=banks[6][:], in1=dnb[:], op=ALU.mult),
                         r=["bank6", "dnb"], w=["ogf"])
                    B.op("dve", lambda e, g=g, qb=qb: e.tensor_tensor(
                        out=XOb_view3(4 * g, 4, qb * 128, 128), in0=ogf[:].rearrange("p (h q) -> p h q", h=4),
                        in1=zT[:, :, qb * 128:(qb + 1) * 128], op=ALU.mult), r=["ogf", "zT"], w=["XO"])

            for d in range(16):
                sg = w_next()
                big = proj_fm(sg, lambda c: hT[:, c, :], bankset=(0, 1))
                gi = 0
                B.op("act", lambda e, big=big, gi=gi: e.activation(out=gsb[:, gi, :], in_=banks[big][:],
                                                                   func=AF.Sigmoid),
                     r=["bank%d" % big], w=["gsb%d" % gi])
                w_prefetch()
                sw = w_next()
                bia = 2 + (d % 2)
                for c in range(16):
                    B.op("pe", lambda e, c=c, bia=bia, sw=sw: e.matmul(
                        banks[bia][:], lhsT=ring[:, sw, c, :], rhs=XOb_view(c, 0, TT), start=(c == 0), stop=(c == 15)),
                        r=["ring%d" % sw, "XO"], w=["bank%d" % bia])
                B.op("dve", lambda e, bia=bia, gi=gi, d=d: e.tensor_tensor(
                    out=m1T[:, d, :], in0=banks[bia][:], in1=gsb[:, gi, :], op=ALU.mult),
                    r=["bank%d" % bia, "gsb%d" % gi], w=["m1T"])
                w_prefetch()

            if ENABLE_DN:
                DK_SCALE = 128.0 ** -0.5

                def pview(base, dims):
                    return AP(base.tensor, base.offset, [list(base.ap[0])] + [list(d_) for d_ in dims])

                sba = w_next()
                for kc in range(16):
                    B.op("pe", lambda e, kc=kc, sba=sba: e.matmul(
                        banks[2][0:64, :], lhsT=ring[:, sba, kc, 0:64], rhs=hT[:, kc, :],
                        start=(kc == 0), stop=(kc == 15)), r=["hT", "ring%d" % sba], w=["bank2"])
                w_prefetch()
                B.op("act", lambda e: e.copy(out=ACC[0][0:64, :], in_=banks[2][0:64, :]), r=["bank2"], w=["ACC0"])
                B.mute = DN_SUB < 2
                for c in range(8):
                    for h in range(2):
                        B.op("pe", lambda e, c=c, h=h: e.matmul(
                            banks[4][h * 64:(h + 1) * 64, c * 64:(c + 1) * 64], lhsT=ACC[0][0:64, c * 64:(c + 1) * 64],
                            rhs=CF[0:64, CF_OFF["ident"][0]:CF_OFF["ident"][0] + 64], start=True, stop=True),
                            r=["ACC0", "CF"], w=["bank4"])
                B.mute = DN_SUB < 3
                for h in range(2):
                    hp = slice(h * 64, (h + 1) * 64)
                    B.op("act", lambda e, h=h, hp=hp: e.activation(
                        out=BET[hp, :, :], in_=pview(banks[4][hp, h:h + 1], [[64, 8], [2, 16]]), func=AF.Sigmoid),
                        r=["bank4"], w=["BET"])
                    B.op("dve", lambda e, h=h, hp=hp: e.tensor_tensor(
                        out=GST[hp, :, :], in0=pview(banks[4][hp, 32 + h:33 + h], [[64, 8], [2, 16]]),
                        in1=pview(PR[hp, PR_OFF["dtb"][0]:PR_OFF["dtb"][0] + 1], [[0, 8], [1, 16]]), op=ALU.add),
                        r=["bank4", "PR"], w=["GST"])
                B.mute = DN_SUB < 4
                B.op("act", lambda e: e.activation(out=GST[:], in_=GST[:], func=AF.Exp), r=["GST"], w=["GST"])
                B.op("dve", lambda e: e.tensor_scalar(out=GST[:], in0=GST[:], scalar1=1.0, scalar2=None, op0=ALU.add),
                     r=["GST"], w=["GST"])
                B.op("act", lambda e: e.activation(out=GST[:], in_=GST[:], func=AF.Ln), r=["GST"], w=["GST"])
                B.op("dve", lambda e: e.tensor_tensor(
                    out=GST[:], in0=GST[:], in1=pview(NEGA[:, 0:1], [[0, 8], [1, 16]]), op=ALU.mult),
                    r=["GST", "NEGA"], w=["GST"])
                B.mute = DN_SUB < 5
                gflat = GST[:].rearrange("p c k -> p (c k)")
                for qi, nm in enumerate(["bdu", "bdsu", "hsel0", "hsel1"]):
                    B.op("pe", lambda e, qi=qi, nm=nm: e.matmul(
                        banks[3][:, qi * 128:(qi + 1) * 128], lhsT=cf(nm), rhs=gflat, start=True, stop=True),
                        r=["CF", "GST"], w=["bank3"])
                B.op("act", lambda e: e.activation(out=EX[:].rearrange("p a c k -> p (a c k)"), in_=banks[3][:],
                                                   func=AF.Exp), r=["bank3"], w=["EX"])

                B.mute = DN_SUB < 6
                def proj_fm_gen(slot):
                    bi = (0, 1)[pbank["i"] % 2]
                    pbank["i"] += 1
                    bk = banks[bi]
                    for c in range(16):
                        B.op("pe", (lambda e, c=c, bk=bk: e.matmul(bk[:], lhsT=ring[:, slot, c, :], rhs=hT[:, c, :],
                                                                   start=(c == 0), stop=(c == 15))),
                             r=["ring%d" % slot, "hT"], w=["bank%d" % bi])
                        if c % 4 == 3 and c != 15:
                            yield
                    return bi

                def step_bg(bg):
                    if bg is None:
                        return False
                    try:
                        next(bg)
                        return True
                    except StopIteration:
                        return False

                def gen_proj(kh, XQK, ZS, KQ, KK_, KZ, RS, SC, KRS, KSC, lag=True):
                    deferred = []
                    stepno = [0]

                    def tick():
                        stepno[0] += 1
                        for d_ in [d_ for d_ in deferred if d_[0] <= stepno[0]]:
                            d_[1]()
                            deferred.remove(d_)

                    def later(n, fn):
                        if lag:
                            deferred.append((stepno[0] + n, fn))
                        else:
                            fn()

                    order = [("q", kh, XQK[:, 1, :], KQ), ("k", 16 + kh, XQK[:, 0, :], KK_),
                             ("v0", 32 + 2 * kh, VT[0][:], "VT0"), ("v1", 33 + 2 * kh, VT[1][:], "VT1"),
                             ("z0", None, ZS[0][:], KZ[0]), ("z1", None, ZS[1][:], KZ[1])]
                    for oi, (nm, cblk, dst, dkey) in enumerate(order):
                        sl = w_next()
                        bi = (0, 1)[pbank["i"] % 2]
                        pbank["i"] += 1
                        ci = oi % 2

                        if cblk is not None:
                            B.op("pool", lambda e, ci=ci, cblk=cblk: e.tensor_tensor(
                                out=DIAG[ci][:], in0=pview(cb("ident")[:, 0:1], [[0, 4], [1, 128]]),
                                in1=pview(pr("cw", cblk * 4), [[1, 4], [0, 128]]), op=ALU.mult),
                                r=["CB", "PR"], w=["DIAG%d" % ci])

                        def evac(bi=bi, ci=ci, cblk=cblk):
                            B.op("act", lambda e: e.copy(out=CSb[ci][:, 3:515], in_=banks[bi][:]),
                                 r=["bank%d" % bi], w=["CSm%d" % ci])
                            B.op("dve", lambda e: e.tensor_copy(out=CSb[ci][:, 0:3], in_=HALO[:, cblk, 0:3]),
                                 r=["HALO%d" % cblk], w=["CSh%d" % ci])
                            B.op("dve", lambda e: e.tensor_copy(out=HALO[:, cblk, 0:3], in_=CSb[ci][:, 512:515]),
                                 r=["CSm%d" % ci], w=["HALO%d" % cblk])

                        def conv(ci=ci, cblk=cblk):
                            for ti in range(4):
                                B.op("pe", lambda e, ti=ti: e.matmul(
                                    banks[2][:], lhsT=DIAG[ci][:, ti, :], rhs=CSb[ci][:, ti:ti + 512],
                                    start=(ti == 0), stop=(ti == 3)),
                                    r=["DIAG%d" % ci, "CSm%d" % ci, "CSh%d" % ci], w=["bank2"])

                        def silu_c(ci=ci, dst=dst, dkey=dkey):
                            B.op("act", lambda e: e.activation(out=dst, in_=banks[2][:], func=AF.Silu),
                                 r=["bank2"], w=[dkey])

                        def silu_z(bi=bi, dst=dst, dkey=dkey):
                            B.op("act", lambda e: e.activation(out=dst, in_=banks[bi][:], func=AF.Silu),
                                 r=["bank%d" % bi], w=[dkey])

                        for c in range(16):
                            B.op("pe", (lambda e, c=c, bi=bi, sl=sl: e.matmul(
                                banks[bi][:], lhsT=ring[:, sl, c, :], rhs=hT[:, c, :], start=(c == 0), stop=(c == 15))),
                                r=["ring%d" % sl, "hT"], w=["bank%d" % bi])
                            if c % 4 == 3:
                                if c == 15:
                                    w_prefetch()
                                    if cblk is not None:
                                        later(LAGS[0], evac)
                                        later(LAGS[1], conv)
                                        later(LAGS[2], silu_c)
                                    else:
                                        later(2, silu_z)
                                tick()
                                yield
                    while deferred:
                        tick()
                        yield
                    B.op("pool", lambda e: e.tensor_tensor(out=SQ3.ap(), in0=XQK[:], in1=XQK[:], op=ALU.mult),
                         r=[KQ, KK_], w=["ogf"])
                    for c in range(8):
                        for qi in range(2):
                            for h in range(2):
                                B.op("pe", lambda e, c=c, qi=qi, h=h: e.matmul(
                                    banks[7][h * 64:(h + 1) * 64, 256 + c * 2 + qi:256 + c * 2 + qi + 1],
                                    lhsT=SQ3.ap(c0=1 - qi, c1=2 - qi, j0=c * 64, j1=(c + 1) * 64),
                                    rhs=CB[:, CB_OFF["onescol"][0]:CB_OFF["onescol"][0] + 1], start=True, stop=True),
                                    r=["ogf", "CB"], w=["bank7"])
                    B.op("dve", lambda e: e.tensor_scalar(
                        out=RS[:].rearrange("p c q -> p (c q)"), in0=banks[7][:, 256:272], scalar1=EPS, scalar2=None,
                        op0=ALU.add), r=["bank7"], w=[KRS])
                    B.op("act", lambda e: e.activation(out=RS[:], in_=RS[:], func=AF.Ln), r=[KRS], w=[KRS])
                    B.op("act", lambda e: e.activation(out=RS[:], in_=RS[:], func=AF.Exp, scale=-0.5), r=[KRS], w=[KRS])
                    rq = RS[:, :, 0]
                    rk = RS[:, :, 1]
                    betk = BET[:, :, kh]
                    egck = EX[:, 0, :, kh]
                    C1, CA, C2, CO, COE = (SC[:, i, :] for i in range(5))
                    B.op("dve", lambda e, betk=betk: e.tensor_tensor(out=C1, in0=betk, in1=rk, op=ALU.mult),
                         r=["BET", KRS], w=[KSC[0]])
                    B.op("dve", lambda e: e.tensor_tensor(out=CA, in0=C1, in1=rk, op=ALU.mult),
                         r=[KSC[0], KRS], w=[KSC[1]])
                    B.op("dve", lambda e, egck=egck: e.scalar_tensor_tensor(out=C2, in0=CA, scalar=-1.0, in1=egck,
                                                                 op0=ALU.mult, op1=ALU.mult),
                         r=[KSC[1], "EX"], w=[KSC[2]])
                    B.op("dve", lambda e: e.tensor_scalar(out=CO, in0=rq, scalar1=DK_SCALE, scalar2=None, op0=ALU.mult),
                         r=[KRS], w=[KSC[3]])
                    B.op("dve", lambda e, egck=egck: e.tensor_tensor(out=COE, in0=CO, in1=egck, op=ALU.mult),
                         r=[KSC[3], "EX"], w=[KSC[4]])


                    yield

                def kh_body(kh, XQK, ZS, KQ, KK_, KZ, RS, SC, KRS, KSC, bg, prev_out):
                    if DN_LEVEL >= 5:
                        KEY_GBD = ["qT"]
                        KEY_DSS = ["zT"]
                        KEY_Y32 = ["pT0", "pT1", "pT2", "pT3"]
                        gb = pview(GST[:, 0, kh:kh + 1], [[16, 8]])
                        B.op("dve", lambda e, gb=gb: e.tensor_copy(out=GHb[:], in_=gb), r=["GST"], w=["GHb"])
                        B.op("dve", lambda e, gb=gb: e.tensor_tensor(out=GHL[:, 1, :], in0=gb, in1=GHb[:], op=ALU.subtract),
                             r=["GST", "GHb"], w=["GHL"])
                        for (dst, dkey, cname, width, src, skey) in [
                                (GP8h[:], "GP8h", "u2", 64, GHb, "GHb"), (GP8l[:], "GP8l", "u2", 64, None, "GHL"),
                                (GBD8h.ap(), "qT", "bdu", 128, GHb, "GHb"), (GBD8l.ap(), "qT", "bdu", 128, None, "GHL")]:
                            if src is not None:
                                in1 = pview(GHb[:, 0:1], [[1, 8], [0, width]])
                            else:
                                in1 = pview(GHL[:, 1, 0:1], [[1, 8], [0, width]])
                            B.op("dve", lambda e, dst=dst, cname=cname, width=width, in1=in1: e.tensor_tensor(
                                out=dst, in0=pview(cf(cname)[:, 0:1], [[0, 8], [1, width]]), in1=in1, op=ALU.mult),
                                r=["CF", skey], w=[dkey])
                        for c in range(8):
                            bi = 2 + c // 4
                            co_ = (c % 4) * 128
                            for h in range(2):
                                B.op("pe", lambda e, c=c, h=h, bi=bi, co_=co_: e.matmul(
                                    banks[bi][h * 64:(h + 1) * 64, co_:co_ + 128], lhsT=XQK[:, 0, c * 64:(c + 1) * 64],
                                    rhs=XQK[:, :, c * 64:(c + 1) * 64], start=True, stop=True),
                                    r=[KK_, KQ], w=["bank%d" % bi])
                        for (bi, cconst, cmask, cones) in [(4, "bdneg", "masks", "ones64"), (5, "bdones", "maskt", "neg64")]:
                            bk_ = "bank%d" % bi
                            B.op("pe", lambda e, bi=bi, cconst=cconst: e.matmul(
                                banks[bi][:], lhsT=cb(cconst), rhs=GP8h[:], start=True, stop=False, skip_group_check=True),
                                r=["CB", "GP8h"], w=[bk_])
                            B.op("pe", lambda e, bi=bi, cconst=cconst: e.matmul(
                                banks[bi][:], lhsT=cb(cconst), rhs=GP8l[:], start=False, stop=False, skip_group_check=True),
                                r=["CB", "GP8l"], w=[bk_])
                            B.op("pe", lambda e, bi=bi, cmask=cmask: e.matmul(
                                banks[bi][:], lhsT=cb("ident"), rhs=pview(cb(cmask)[:, 0:1], [[0, 8], [1, 64]]),
                                start=False, stop=False, skip_group_check=True), r=["CB"], w=[bk_])
                            for c in range(8):
                                B.op("pe", lambda e, bi=bi, c=c, cones=cones: e.matmul(
                                    banks[bi][:, c * 64:(c + 1) * 64], lhsT=GBD8h.ap(c0=c, c1=c + 1), rhs=cb(cones),
                                    start=False, stop=False, skip_group_check=True), r=["CB", "qT"], w=[bk_])
                                B.op("pe", lambda e, bi=bi, c=c, cones=cones: e.matmul(
                                    banks[bi][:, c * 64:(c + 1) * 64], lhsT=GBD8l.ap(c0=c, c1=c + 1), rhs=cb(cones),
                                    start=False, stop=(c == 7), skip_group_check=True), r=["CB", "qT"], w=[bk_])
                        B.op("act", lambda e: e.activation(out=DSSa.ap(), in_=banks[4][:].rearrange("p (c j) -> p c j", c=8),
                                                           func=AF.Exp), r=["bank4"], w=KEY_DSS)
                        B.op("act", lambda e: e.activation(out=DSSb.ap(), in_=banks[5][:].rearrange("p (c j) -> p c j", c=8),
                                                           func=AF.Exp), r=["bank5"], w=KEY_DSS)
                        B.op("dve", lambda e: e.tensor_tensor(
                            out=DSSa.ap(), in0=DSSa.ap(), in1=pview(SC[:, 1, 0:1], [[1, 8], [0, 64]]), op=ALU.mult),
                            r=KEY_DSS + [KSC[1]], w=KEY_DSS)
                        for b2 in range(2):
                            for h in range(2):
                                hp = slice(h * 64, (h + 1) * 64)
                                B.op("dve", lambda e, b2=b2, h=h, hp=hp: e.tensor_tensor(
                                    out=P0B8[hp, 4 * b2:4 * b2 + 4, h * 64:(h + 1) * 64],
                                    in0=pview(banks[2 + b2][hp, 0:1], [[128, 4], [1, 64]]),
                                    in1=DSSa.ap(p0=h * 64, p1=(h + 1) * 64, c0=4 * b2, c1=4 * b2 + 4),
                                    op=ALU.mult), r=["bank%d" % (2 + b2)] + KEY_DSS, w=["P0B8"])
                                B.op("dve", lambda e, b2=b2, h=h, hp=hp: e.tensor_tensor(
                                    out=MQ[hp, 4 * b2:4 * b2 + 4, h * 64:(h + 1) * 64],
                                    in0=pview(banks[2 + b2][hp, 64:65], [[128, 4], [1, 64]]),
                                    in1=DSSb.ap(p0=h * 64, p1=(h + 1) * 64, c0=4 * b2, c1=4 * b2 + 4),
                                    op=ALU.mult), r=["bank%d" % (2 + b2)] + KEY_DSS, w=["MQ"])
                        for c in range(8):
                            bi = 6 + c // 4
                            co_ = (c % 4) * 128
                            B.op("pe", lambda e, c=c, bi=bi, co_=co_: e.matmul(
                                banks[bi][:, co_:co_ + 128], lhsT=P0B8[:, c, :], rhs=cb("ident"), start=True, stop=True),
                                r=["P0B8", "CB"], w=["bank%d" % bi])
                        for b2 in range(2):
                            B.op("act", lambda e, b2=b2: e.copy(
                                out=Q0B8[:, 4 * b2:4 * b2 + 4, :], in_=banks[6 + b2][:].rearrange("p (c j) -> p c j", c=4)),
                                r=["bank%d" % (6 + b2)], w=["Q0B8"])
                            B.op("dve", lambda e, b2=b2: e.tensor_tensor(
                                out=Y32_8.ap(c0=4 * b2, c1=4 * b2 + 4), in0=pview(cf("ident")[:, 0:1], [[0, 4], [1, 128]]),
                                in1=banks[6 + b2][:].rearrange("p (c j) -> p c j", c=4), op=ALU.subtract),
                                r=["bank%d" % (6 + b2), "CF"], w=KEY_Y32)
                        B.op("act", lambda e: e.copy(out=Yb8[:], in_=Y32_8.ap()), r=KEY_Y32, w=["Yb8"])
                        def PQk(k):
                            if k == 0:
                                return (lambda c: P0B8[:, c, :]), (lambda c: Q0B8[:, c, :]), ["P0B8", "Q0B8"]
                            t_ = PQ8[k % 2]
                            return (lambda c: t_[:, 0, c, :]), (lambda c: t_[:, 1, c, :]), ["PQ8_%d" % (k % 2)]

                        def squaring(k):
                            Pk, Qk, pk_keys = PQk(k)
                            nxt = PQ8[(k + 1) % 2]
                            nk_ = "PQ8_%d" % ((k + 1) % 2)
                            for c in range(8):
                                bi = 2 + c // 4
                                co_ = (c % 4) * 128
                                B.op("pe", lambda e, c=c, bi=bi, co_=co_: e.matmul(
                                    banks[bi][:, co_:co_ + 128], lhsT=Qk(c), rhs=Pk(c), start=True, stop=True),
                                    r=pk_keys, w=["bank%d" % bi])
                            if k < 4:
                                for c in range(8):
                                    bi = 4 + c // 4
                                    co_ = (c % 4) * 128
                                    B.op("pe", lambda e, c=c, bi=bi, co_=co_: e.matmul(
                                        banks[bi][:, co_:co_ + 128], lhsT=Pk(c), rhs=Qk(c), start=True, stop=True),
                                        r=pk_keys, w=["bank%d" % bi])
                            for b2 in range(2):
                                B.op("act", lambda e, b2=b2: e.copy(
                                    out=nxt[:, 0, 4 * b2:4 * b2 + 4, :],
                                    in_=banks[2 + b2][:].rearrange("p (c j) -> p c j", c=4)),
                                    r=["bank%d" % (2 + b2)], w=[nk_])
                            if k < 4:
                                for b2 in range(2):
                                    B.op("dve", lambda e, b2=b2: e.tensor_copy(
                                        out=nxt[:, 1, 4 * b2:4 * b2 + 4, :],
                                        in_=banks[4 + b2][:].rearrange("p (c j) -> p c j", c=4)),
                                        r=["bank%d" % (4 + b2)], w=[nk_])

                        def apply(k):
                            nxt = PQ8[(k + 1) % 2]
                            nk_ = "PQ8_%d" % ((k + 1) % 2)
                            for c in range(8):
                                bi = 6 + c // 4
                                co_ = (c % 4) * 128
                                B.op("pe", lambda e, c=c, bi=bi, co_=co_: e.matmul(
                                    banks[bi][:, co_:co_ + 128], lhsT=nxt[:, 0, c, :], rhs=Yb8[:, c, :],
                                    start=True, stop=True), r=[nk_, "Yb8"], w=["bank%d" % bi])
                            for b2 in range(2):
                                B.op("dve", lambda e, b2=b2: e.tensor_tensor(
                                    out=Y32_8.ap(c0=4 * b2, c1=4 * b2 + 4),
                                    in0=banks[6 + b2][:].rearrange("p (c j) -> p c j", c=4),
                                    in1=Y32_8.ap(c0=4 * b2, c1=4 * b2 + 4), op=ALU.add),
                                    r=["bank%d" % (6 + b2)] + KEY_Y32, w=KEY_Y32)
                            if k < 4:
                                B.op("act", lambda e: e.copy(out=Yb8[:], in_=Y32_8.ap()), r=KEY_Y32, w=["Yb8"])
                            else:
                                B.op("act", lambda e: e.copy(out=YT[:], in_=Y32_8.ap()), r=KEY_Y32, w=["YT"])

                        squaring(0)
                        for (srcs, dst, dkey, in1f, skeys) in [
                                (lambda c, h: XQK[:, 0, c * 64:(c + 1) * 64], XKD, "XKD",
                                 lambda b2: pview(EX[:, 1, 4 * b2, kh:kh + 1], [[16, 4], [0, 128]]), [KK_, "EX"]),
                                (lambda c, h: VT[h][:, c * 64:(c + 1) * 64], VB, "VB",
                                 lambda b2: pview(SC[:, 0, 4 * b2:4 * b2 + 1], [[1, 4], [0, 128]]), ["VT0", "VT1", KSC[0]])]:
                            for b2 in range(2):
                                bi = b2
                                for cc in range(4):
                                    c = 4 * b2 + cc
                                    for h in range(2):
                                        B.op("pe", lambda e, c=c, cc=cc, h=h, bi=bi, srcs=srcs: e.matmul(
                                            banks[bi][h * 64:(h + 1) * 64, cc * 128:(cc + 1) * 128], lhsT=srcs(c, h),
                                            rhs=cb("ident"), start=True, stop=True), r=skeys[:-1] + ["CB"], w=["bank%d" % bi])
                                B.op("dve", lambda e, b2=b2, bi=bi, dst=dst, in1f=in1f: e.tensor_tensor(
                                    out=dst[:, 4 * b2:4 * b2 + 4, :], in0=banks[bi][:].rearrange("p (c j) -> p c j", c=4),
                                    in1=in1f(b2), op=ALU.mult), r=["bank%d" % bi, skeys[-1]], w=[dkey])
                        if prev_out is not None:
                            prev_out()
                        for k in range(1, 5):
                            squaring(k)
                            apply(k - 1)
                        apply(4)
                    if DN_LEVEL >= 6:
                        for c in range(8):
                            for h in range(2):
                                hp = slice(h * 64, (h + 1) * 64)
                                B.op("pe", lambda e, c=c, h=h, hp=hp, kh=kh: e.matmul(
                                    banks[6][hp, 0:128], lhsT=XQK[:, 0, c * 64:(c + 1) * 64],
                                    rhs=S2b[:, kh, h * 128:(h + 1) * 128], start=True, stop=True),
                                    r=[KK_, "S2b"], w=["bank6"])
                                B.op("pe", lambda e, c=c, h=h, hp=hp, kh=kh: e.matmul(
                                    banks[6][hp, 128:256], lhsT=XQK[:, 1, c * 64:(c + 1) * 64],
                                    rhs=S2b[:, kh, h * 128:(h + 1) * 128], start=True, stop=True),
                                    r=[KQ, "S2b"], w=["bank6"])
                            step_bg(bg)
                            B.op("dve", lambda e, c=c: e.scalar_tensor_tensor(
                                out=RB[:], in0=banks[6][:, 0:128], scalar=SC[:, 2, c:c + 1], in1=VB[:, c, :],
                                op0=ALU.mult, op1=ALU.add), r=["bank6", KSC[2], "VB"], w=["RB"])
                            B.op("pe", lambda e, c=c: e.matmul(banks[5][:, 0:128], lhsT=YT[:, c, :], rhs=RB[:],
                                                               start=True, stop=True), r=["YT", "RB"], w=["bank5"])
                            step_bg(bg)
                            B.op("act", lambda e: e.copy(out=VPB[:], in_=banks[5][:, 0:128]), r=["bank5"], w=["VPB"])
                            B.op("pe", lambda e, c=c: e.matmul(banks[7][:, 0:128], lhsT=MQ[:, c, :], rhs=VPB[:],
                                                               start=True, stop=True), r=["MQ", "VPB"], w=["bank7"])
                            B.op("act", lambda e, c=c: e.activation(out=T1[:], in_=banks[6][:, 128:256], func=AF.Identity,
                                                                    scale=SC[:, 4, c:c + 1]),
                                 r=["bank6", KSC[4]], w=["T1"])
                            B.op("dve", lambda e, c=c: e.scalar_tensor_tensor(
                                out=OB[:, c, :], in0=banks[7][:, 0:128], scalar=SC[:, 3, c:c + 1], in1=T1[:],
                                op0=ALU.mult, op1=ALU.add), r=["bank7", KSC[3], "T1"], w=["OB"])
                            sbk = [banks[3][:, 0:128], banks[4][:, 0:128]]
                            sbk_key = ["bank3", "bank4"]
                            for h in range(2):
                                hp = slice(h * 64, (h + 1) * 64)
                                B.op("pe", lambda e, c=c, h=h, hp=hp, sbk=sbk: e.matmul(
                                    sbk[h], lhsT=XKD[hp, c, :], rhs=VPB[hp, :],
                                    start=True, stop=True), r=["XKD", "VPB"], w=[sbk_key[h]])
                            step_bg(bg)
                            for h in range(2):
                                B.op("dve", lambda e, c=c, h=h, kh=kh, sbk=sbk: e.scalar_tensor_tensor(
                                    out=S2b[:, kh, h * 128:(h + 1) * 128], in0=S2[:, kh, h * 128:(h + 1) * 128],
                                    scalar=EX[:, 2 + h, c, kh:kh + 1], in1=sbk[h],
                                    op0=ALU.mult, op1=ALU.add), r=["S2", "EX", sbk_key[h]], w=["S2b"])
                            for h in range(2):
                                B.op("dve", lambda e, c=c, h=h, kh=kh, sbk=sbk: e.scalar_tensor_tensor(
                                    out=S2[:, kh, h * 128:(h + 1) * 128], in0=S2[:, kh, h * 128:(h + 1) * 128],
                                    scalar=EX[:, 2 + h, c, kh:kh + 1], in1=sbk[h],
                                    op0=ALU.mult, op1=ALU.add), r=["S2", "EX", sbk_key[h]], w=["S2"])
                            B.op("act", lambda e, c=c: e.activation(out=JK[:], in_=OB[:, c, :], func=AF.Square,
                                                                    accum_out=SSO[:, c, 0:1]), r=["OB"], w=["JK", "SSO"])

                    while step_bg(bg):
                        pass
                    def outnorm():
                        B.op("dve", lambda e: e.tensor_scalar(out=SSO[:, :, 1], in0=SSO[:, :, 0], scalar1=1.0 / 128.0,
                                                              scalar2=EPS, op0=ALU.mult, op1=ALU.add),
                             r=["SSO"], w=["SSO1"])
                        B.op("act", lambda e: e.activation(out=SSO[:, :, 2], in_=SSO[:, :, 1], func=AF.Ln),
                             r=["SSO1"], w=["SSO2"])
                        B.op("act", lambda e: e.activation(out=SSO[:, :, 3], in_=SSO[:, :, 2], func=AF.Exp, scale=-0.5),
                             r=["SSO2"], w=["SSO3"])
                        B.op("dve", lambda e: e.tensor_tensor(
                            out=ON3.ap(), in0=OB[:], in1=pview(SSO[:, 0, 3:4], [[4, 8], [0, 128]]), op=ALU.mult),
                            r=["OB", "SSO3"], w=["dnb"])
                        for c4 in range(2):
                            bi = c4
                            bkb = banks[bi][:].bitcast(BF16)
                            for cc in range(4):
                                c = c4 * 4 + cc
                                B.op("pe", lambda e, c=c, cc=cc, bkb=bkb: e.transpose(
                                    out=bkb[:, cc * 128:(cc + 1) * 128], in_=ON3.ap(c0=c, c1=c + 1), identity=cb("ident")),
                                    r=["dnb", "CB"], w=["bank%d" % bi])
                            for h in range(2):
                                B.op("dve", lambda e, bkb=bkb, h=h, c4=c4, kh=kh: e.scalar_tensor_tensor(
                                    out=XOb_view(2 * kh + h, c4 * 256, 256).rearrange("p (c i) -> p c i", c=4),
                                    in0=pview(bkb[:, h * 64:h * 64 + 1], [[128, 4], [1, 64]]), scalar=pr("dnw", 0),
                                    in1=ZS[h][:, c4 * 256:(c4 + 1) * 256].rearrange("p (c i) -> p c i", c=4),
                                    op0=ALU.mult, op1=ALU.mult), r=["bank%d" % bi, "PR", KZ[h]], w=["XO"])
                    return outnorm

                def pars(kh):
                    p_ = kh % 2
                    return (XQKs[p_], ZSs[p_], "XQKq%d" % p_, "XQKk%d" % p_, ["ZS%d_%d" % (p_, h) for h in range(2)],
                            RSs[p_], SCs[p_], "RS%d" % p_, ["SC%d_%d" % (p_, i) for i in range(5)])

                g0 = gen_proj(0, *pars(0), lag=False)
                while step_bg(g0):
                    pass
                prev_out = None
                for kh in range(16):
                    bg = gen_proj(kh + 1, *pars(kh + 1)) if kh < 15 else None
                    prev_out = kh_body(kh, *pars(kh), bg, prev_out)
                prev_out()

                B.mute = DN_SUB < 7
                for d in range(16):
                    sg = w_next()
                    big = proj_fm(sg, lambda c: hT[:, c, :], bankset=(0, 1))
                    gi = 0
                    B.op("act", lambda e, big=big, gi=gi: e.activation(out=gsb[:, gi, :], in_=banks[big][:],
                                                                       func=AF.Sigmoid),
                         r=["bank%d" % big], w=["gsb%d" % gi])
                    w_prefetch()
                    s1 = w_next()
                    s2 = w_next()
                    bia = 2 + (d % 2)
                    for c in range(32):
                        sl_ = s1 if c < 16 else s2
                        B.op("pe", lambda e, c=c, bia=bia, sl_=sl_: e.matmul(
                            banks[bia][:], lhsT=ring[:, sl_, c % 16, :], rhs=XOb_view(c, 0, TT),
                            start=(c == 0), stop=(c == 31)), r=["ring%d" % sl_, "XO"], w=["bank%d" % bia])
                    B.op("dve", lambda e, bia=bia, gi=gi: e.tensor_tensor(
                        out=ogf[:], in0=banks[bia][:], in1=gsb[:, gi, :], op=ALU.mult),
                        r=["bank%d" % bia, "gsb%d" % gi], w=["ogf"])
                    B.op("pool", lambda e, gi=gi, d=d: e.tensor_tensor(
                        out=m1T[:, d, :], in0=ogf[:], in1=m1T[:, d, :], op=ALU.add),
                        r=["ogf", "m1T"], w=["m1T"])
                    w_prefetch()
            else:
                for _ in range(N_DN_BLK):
                    w_next()
                    w_prefetch()
            B.mute = False
            while wstate["next_use"] % 4 != 0:
                w_next()
                w_prefetch()

            if SKIP_P3:
                continue
            for tb in range(4):
                B.op("sp", lambda e, t0=t0, tb=tb: e.dma_start(
                    out=XO[:, tb, :], in_=x_d[t0 + tb * 128:t0 + (tb + 1) * 128, :]),
                    w=["XO", "XOx%d" % tb], dma="x%d" % tb)
            for cg in range(4):
                sls = [w_next() for _ in range(4)]
                assert sls[0] % 4 == 0 and sls == list(range(sls[0], sls[0] + 4))
                for tb in range(4):
                    bi = 2 + ((cg * 4 + tb) % 4)
                    for c in range(16):
                        rhs = AP(ring, ring[:, sls[0], c, :].offset, [list(ring[:].ap[0]), [16 * 128, 4], [1, 128]])
                        B.op("pe", lambda e, bi=bi, c=c, tb=tb, rhs=rhs: e.matmul(
                            banks[bi][:], lhsT=m1T[:, c, tb * 128:(tb + 1) * 128], rhs=rhs,
                            start=(c == 0), stop=(c == 15)),
                            r=["m1T"] + ["ring%d" % s_ for s_ in sls], w=["bank%d" % bi])
                    B.op("dve", lambda e, bi=bi, tb=tb, cg=cg: e.tensor_tensor(
                        out=XO[:, tb, cg * 512:(cg + 1) * 512], in0=banks[bi][:], in1=XO[:, tb, cg * 512:(cg + 1) * 512],
                        op=ALU.add), r=["bank%d" % bi, "XOx%d" % tb], w=["XOx%d" % tb])
                w_prefetch()
            for tb in range(4):
                B.op("act", lambda e, tb=tb: e.activation(out=hb[:], in_=XO[:, tb, :], func=AF.Square,
                                                          accum_out=st1[:, 0:1]), r=["XOx%d" % tb], w=["hb", "st1"])
                B.op("dve", lambda e: e.tensor_scalar(out=st1[:, 1:2], in0=st1[:, 0:1], scalar1=1.0 / D, scalar2=EPS,
                                                      op0=ALU.mult, op1=ALU.add), r=["st1"], w=["st1b"])
                B.op("act", lambda e: e.activation(out=st1[:, 2:3], in_=st1[:, 1:2], func=AF.Sqrt),
                     r=["st1b"], w=["st1c"])
                B.op("dve", lambda e: e.reciprocal(out=st1[:, 3:4], in_=st1[:, 2:3]), r=["st1c"], w=["st1d"])
                B.op("dve", lambda e, tb=tb: e.scalar_tensor_tensor(
                    out=XO[:, tb, :], in0=XO[:, tb, :], scalar=st1[:, 3:4], in1=FW[:], op0=ALU.mult, op1=ALU.mult),
                    r=["XOx%d" % tb, "st1d", "FW"], w=["XOx%d" % tb])
                tok = B.op("sp", lambda e, tb=tb, t0=t0: e.dma_start(
                    out=y_d[t0 + tb * 128:t0 + (tb + 1) * 128, :], in_=XO[:, tb, :]), r=["XOx%d" % tb], dma="o%d" % tb)
                out_toks.append(tok)

        B.wait_all("sp", out_toks)
        block = stack.enter_context(nc.Block())
        B.emit(nc, block, stack)
    return nc


_CACHE = {}


def kernel(x, norm_w, w_in, b_qkv, sinks, conv_w, a_log, dt_bias, dn_norm_w,
           w_att_branch, w_dn_branch, w_out, final_norm_w):
    x = np.asarray(x, np.float32)
    wst = build_wstream(np.asarray(w_in[0], np.float32), np.asarray(w_att_branch[0], np.float32),
                        np.asarray(w_dn_branch[0], np.float32), np.asarray(w_out[0], np.float32))
    cfa = consts_f32()
    cba = consts_b16(np.asarray(b_qkv[0], np.float32))
    pra = params_f32(np.asarray(norm_w[0]), np.asarray(b_qkv[0]), np.asarray(sinks[0]), np.asarray(conv_w[0]),
                     np.asarray(a_log[0]), np.asarray(dt_bias[0]), np.asarray(dn_norm_w[0]))
    fwa = np.ascontiguousarray(np.broadcast_to(np.asarray(final_norm_w, np.float32)[None, :], (128, D)))
    n_wblk = wst.shape[0]
    if "nc" not in _CACHE:
        _CACHE["nc"] = build_program(n_wblk)
    nc = _CACHE["nc"]
    in_maps = [{"x": np.ascontiguousarray(x[b]), "wst": wst, "cf": cfa, "cb": cba, "pr": pra, "fw": fwa}
               for b in range(8)]
    res = run_bass_kernel_spmd(nc, in_maps, core_ids=list(range(8)))
    return np.stack([res.results[b]["y"] for b in range(8)], axis=0).astype(np.float32)
```
